# Optimizing a Trainium2 kernel written in Bass

```python
import math
import jax
import jax.numpy as jnp
from jax import lax
import numpy as np

D_MODEL = 1024
BATCH = 2
SEQ = 16384
DEPTH = 2

N_A_LAYERS = DEPTH // 2
N_B_LAYERS = DEPTH - N_A_LAYERS
MIX_WIDTH = D_MODEL
HEAD_DIM = 64
MEM_HEADS = 4
MEM_WIDTH = MEM_HEADS * HEAD_DIM
SELF_WIDTH = MIX_WIDTH - MEM_WIDTH
S5_GROUP = 16
S5_GROUPS = SELF_WIDTH // S5_GROUP
S5_STATE = 64
S5_CHUNK = 128
S5_DT_MIN = 0.001
S5_DT_MAX = 0.1
MOBA_HEADS = SELF_WIDTH // HEAD_DIM
MOBA_BLOCK = 256
MOBA_TOPK = 3
MOBA_QBLOCK = 64
N_MEM = 256
D_FF = 2816
CONV_WIDTH = 3
ROPE_THETA = 10000.0
NORM_EPS = 1e-6
NEG_INF = -1e30

kernel_name = 'yoco_s5_moba_memory_convffn'


def rms_norm(x, gain):
    xf = x.astype(jnp.float32)
    y = xf * lax.rsqrt(jnp.mean(xf * xf, axis=-1, keepdims=True) + NORM_EPS)
    return (y * gain.astype(jnp.float32)).astype(x.dtype)


def rope_tables(seq):
    pos = jnp.arange(seq, dtype=jnp.float32)
    inv = ROPE_THETA ** (-jnp.arange(0, HEAD_DIM, 2, dtype=jnp.float32) / HEAD_DIM)
    ang = pos[:, None] * inv[None, :]
    return jnp.cos(ang), jnp.sin(ang)


def apply_rope(t, cos, sin):
    half = HEAD_DIM // 2
    tf = t.astype(jnp.float32)
    t1, t2 = tf[..., :half], tf[..., half:]
    c, s = cos[:, None, :], sin[:, None, :]
    return jnp.concatenate([t1 * c - t2 * s, t1 * s + t2 * c], axis=-1).astype(t.dtype)


def _cplx_combine(e1, e2):
    a1r, a1i, b1r, b1i = e1
    a2r, a2i, b2r, b2i = e2
    ar = a2r * a1r - a2i * a1i
    ai = a2r * a1i + a2i * a1r
    br = a2r * b1r - a2i * b1i + b2r
    bi = a2r * b1i + a2i * b1r + b2i
    return (ar, ai, br, bi)


def s5_mixer(u, lam_re, lam_im, log_step, b_re, b_im, c_re, c_im, d_skip, w_glu):
    bsz, seq, _ = u.shape
    f32 = jnp.float32
    uf = u.astype(f32).reshape(bsz, seq, S5_GROUPS, S5_GROUP)
    dt = jnp.exp(log_step.astype(f32))[:, None]
    lr, li = lam_re.astype(f32), lam_im.astype(f32)
    mag = jnp.exp(lr * dt)
    ab_re, ab_im = mag * jnp.cos(li * dt), mag * jnp.sin(li * dt)
    den = lr * lr + li * li
    num_re, num_im = ab_re - 1.0, ab_im
    coef_re = (num_re * lr + num_im * li) / den
    coef_im = (num_im * lr - num_re * li) / den
    br, bi = b_re.astype(f32), b_im.astype(f32)
    bb_re = coef_re[..., None] * br - coef_im[..., None] * bi
    bb_im = coef_re[..., None] * bi + coef_im[..., None] * br
    cr, ci = c_re.astype(f32), c_im.astype(f32)
    dd = d_skip.astype(f32)
    n_chunks = seq // S5_CHUNK
    u_chunks = uf.reshape(bsz, n_chunks, S5_CHUNK, S5_GROUPS, S5_GROUP).transpose(1, 0, 2, 3, 4)

    def step(carry, u_c):
        h_re, h_im = carry
        bu_re = jnp.einsum('blgp,gnp->blgn', u_c, bb_re)
        bu_im = jnp.einsum('blgp,gnp->blgn', u_c, bb_im)
        a_re = jnp.broadcast_to(ab_re, bu_re.shape)
        a_im = jnp.broadcast_to(ab_im, bu_im.shape)
        acc_re, acc_im, x_re, x_im = lax.associative_scan(
            _cplx_combine, (a_re, a_im, bu_re, bu_im), axis=1)
        s_re = x_re + acc_re * h_re[:, None] - acc_im * h_im[:, None]
        s_im = x_im + acc_re * h_im[:, None] + acc_im * h_re[:, None]
        y = (jnp.einsum('gpn,blgn->blgp', cr, s_re)
             - jnp.einsum('gpn,blgn->blgp', ci, s_im) + dd * u_c)
        return (s_re[:, -1], s_im[:, -1]), y

    init = (jnp.zeros((bsz, S5_GROUPS, S5_STATE), f32),
            jnp.zeros((bsz, S5_GROUPS, S5_STATE), f32))
    _, ys = lax.scan(step, init, u_chunks)
    y = ys.transpose(1, 0, 2, 3, 4).reshape(bsz, seq, SELF_WIDTH)
    y = jax.nn.gelu(y)
    y = y * jax.nn.sigmoid(y @ w_glu.astype(f32))
    return y.astype(u.dtype)


def make_shared_kv(x, kv_norm, w_kv, cos, sin):
    bsz, seq, _ = x.shape
    h = rms_norm(x, kv_norm)
    kv = h @ w_kv
    k = kv[..., :SELF_WIDTH].reshape(bsz, seq, MOBA_HEADS, HEAD_DIM)
    v = kv[..., SELF_WIDTH:].reshape(bsz, seq, MOBA_HEADS, HEAD_DIM)
    k = apply_rope(k, cos, sin)
    n_blocks = max(-(-seq // MOBA_BLOCK), MOBA_TOPK)
    pad = n_blocks * MOBA_BLOCK - seq
    k = jnp.pad(k, ((0, 0), (0, pad), (0, 0), (0, 0)))
    v = jnp.pad(v, ((0, 0), (0, pad), (0, 0), (0, 0)))
    k_blocks = k.reshape(bsz, n_blocks, MOBA_BLOCK, MOBA_HEADS, HEAD_DIM).transpose(0, 3, 1, 2, 4)
    v_blocks = v.reshape(bsz, n_blocks, MOBA_BLOCK, MOBA_HEADS, HEAD_DIM).transpose(0, 3, 1, 2, 4)
    k_mean = jnp.mean(k_blocks.astype(jnp.float32), axis=3)
    return k_blocks, v_blocks, k_mean


def moba_attention(q, k_blocks, v_blocks, k_mean):
    bsz, seq = q.shape[0], q.shape[1]
    n_blocks = k_blocks.shape[2]
    n_qb = seq // MOBA_QBLOCK
    scale = HEAD_DIM ** -0.5
    q_blocks = q.reshape(bsz, n_qb, MOBA_QBLOCK, MOBA_HEADS, HEAD_DIM).transpose(1, 0, 2, 3, 4)
    b_ix = jnp.arange(bsz)[:, None, None, None]
    h_ix = jnp.arange(MOBA_HEADS)[None, :, None, None]
    n_sel = MOBA_TOPK * MOBA_BLOCK

    def one_block(args):
        qi, qb = args
        q_start = qi * MOBA_QBLOCK
        blk = q_start // MOBA_BLOCK
        gate = jnp.einsum('bqhd,bhnd->bhqn', qb.astype(jnp.float32), k_mean)
        past = jnp.arange(n_blocks) < blk
        gate = jnp.where(past[None, None, None, :], gate, -jnp.inf)
        _, sel = lax.top_k(gate, MOBA_TOPK)
        sel_valid = sel < blk
        kg = k_blocks[b_ix, h_ix, sel]
        vg = v_blocks[b_ix, h_ix, sel]
        s_sel = jnp.einsum('bqhd,bhqtkd->bhqtk', qb, kg,
                           preferred_element_type=jnp.float32) * scale
        s_sel = jnp.where(sel_valid[..., None], s_sel, NEG_INF)
        s_sel = s_sel.reshape(bsz, MOBA_HEADS, MOBA_QBLOCK, n_sel)
        k_own = lax.dynamic_index_in_dim(k_blocks, blk, axis=2, keepdims=False)
        v_own = lax.dynamic_index_in_dim(v_blocks, blk, axis=2, keepdims=False)
        s_own = jnp.einsum('bqhd,bhkd->bhqk', qb, k_own,
                           preferred_element_type=jnp.float32) * scale
        q_pos = q_start + jnp.arange(MOBA_QBLOCK)
        k_pos = blk * MOBA_BLOCK + jnp.arange(MOBA_BLOCK)
        s_own = jnp.where(k_pos[None, :] <= q_pos[:, None], s_own, NEG_INF)
        p = jax.nn.softmax(jnp.concatenate([s_sel, s_own], axis=-1), axis=-1)
        p_sel = p[..., :n_sel].reshape(bsz, MOBA_HEADS, MOBA_QBLOCK, MOBA_TOPK, MOBA_BLOCK).astype(vg.dtype)
        p_own = p[..., n_sel:].astype(v_own.dtype)
        return (jnp.einsum('bhqtk,bhqtkd->bqhd', p_sel, vg)
                + jnp.einsum('bhqk,bhkd->bqhd', p_own, v_own))

    out = lax.map(one_block, (jnp.arange(n_qb), q_blocks))
    return out.transpose(1, 0, 2, 3, 4).reshape(bsz, seq, SELF_WIDTH)


def memory_attention(q_mem, mem, mem_gain, w_mem_kv):
    bsz, seq, _ = q_mem.shape
    m = mem.shape[1]
    kv = rms_norm(mem, mem_gain) @ w_mem_kv
    k = kv[..., :MEM_WIDTH].reshape(bsz, m, MEM_HEADS, HEAD_DIM)
    v = kv[..., MEM_WIDTH:].reshape(bsz, m, MEM_HEADS, HEAD_DIM)
    q = q_mem.reshape(bsz, seq, MEM_HEADS, HEAD_DIM)
    s = jnp.einsum('bshd,bmhd->bhsm', q, k, preferred_element_type=jnp.float32) * HEAD_DIM ** -0.5
    p = jax.nn.softmax(s, axis=-1).astype(v.dtype)
    return jnp.einsum('bhsm,bmhd->bshd', p, v).reshape(bsz, seq, MEM_WIDTH)


def causal_dwconv(x, w, b):
    ch = x.shape[-1]
    y = lax.conv_general_dilated(
        x, w[:, None, :].astype(x.dtype), window_strides=(1,),
        padding=[(CONV_WIDTH - 1, 0)], dimension_numbers=('NWC', 'WIO', 'NWC'),
        feature_group_count=ch)
    return y + b.astype(x.dtype)


def conv_ffn(x, gain, w_up, conv_w, conv_b, w_down):
    h = rms_norm(x, gain)
    up = causal_dwconv(h @ w_up, conv_w, conv_b)
    g, v = up[..., :D_FF], up[..., D_FF:]
    return (jax.nn.silu(g) * v) @ w_down


def setup_inputs(seed: int = 0) -> dict:
    key = jax.random.key(seed)
    ks = jax.random.split(key, 32)
    f32 = jnp.float32

    def nrm(k, shape, scale):
        return jax.random.normal(k, shape, f32) * scale

    def gain(k, shape):
        return 1.0 + 0.05 * jax.random.normal(k, shape, f32)

    G, N, P = S5_GROUPS, S5_STATE, S5_GROUP
    lam_re = -0.5 + 0.01 * jax.random.normal(ks[12], (N_A_LAYERS, G, N), f32)
    lam_im = (jnp.pi * jnp.arange(N, dtype=f32))[None, None, :] + 0.01 * jax.random.normal(ks[13], (N_A_LAYERS, G, N), f32)
    log_step = math.log(S5_DT_MIN) + jax.random.uniform(ks[14], (N_A_LAYERS, G), f32) * (math.log(S5_DT_MAX) - math.log(S5_DT_MIN))
    return {
        'x': jax.random.normal(ks[0], (BATCH, SEQ, D_MODEL), f32),
        'mem': jax.random.normal(ks[1], (BATCH, N_MEM, D_MODEL), f32),
        'ln_mix': gain(ks[2], (DEPTH, D_MODEL)),
        'w_in': nrm(ks[3], (DEPTH, D_MODEL, MIX_WIDTH), D_MODEL ** -0.5),
        'w_out': nrm(ks[4], (DEPTH, MIX_WIDTH, D_MODEL), MIX_WIDTH ** -0.5),
        'mem_norm': gain(ks[5], (DEPTH, D_MODEL)),
        'w_mem_kv': nrm(ks[6], (DEPTH, D_MODEL, 2 * MEM_WIDTH), D_MODEL ** -0.5),
        'ln_ffn': gain(ks[7], (DEPTH, D_MODEL)),
        'w_up': nrm(ks[8], (DEPTH, D_MODEL, 2 * D_FF), D_MODEL ** -0.5),
        'conv_w': nrm(ks[9], (DEPTH, CONV_WIDTH, 2 * D_FF), CONV_WIDTH ** -0.5),
        'conv_b': nrm(ks[10], (DEPTH, 2 * D_FF), 0.01),
        'w_down': nrm(ks[11], (DEPTH, D_FF, D_MODEL), D_FF ** -0.5),
        's5_lambda_re': lam_re,
        's5_lambda_im': lam_im,
        's5_log_step': log_step,
        's5_b_re': nrm(ks[15], (N_A_LAYERS, G, N, P), (2 * P) ** -0.5),
        's5_b_im': nrm(ks[16], (N_A_LAYERS, G, N, P), (2 * P) ** -0.5),
        's5_c_re': nrm(ks[17], (N_A_LAYERS, G, P, N), (2 * N) ** -0.5),
        's5_c_im': nrm(ks[18], (N_A_LAYERS, G, P, N), (2 * N) ** -0.5),
        's5_d': nrm(ks[19], (N_A_LAYERS, G, P), 1.0),
        's5_w_glu': nrm(ks[20], (N_A_LAYERS, SELF_WIDTH, SELF_WIDTH), SELF_WIDTH ** -0.5),
        'kv_norm': gain(ks[21], (D_MODEL,)),
        'w_kv': nrm(ks[22], (D_MODEL, 2 * SELF_WIDTH), D_MODEL ** -0.5),
        'final_norm': gain(ks[23], (D_MODEL,)),
    }


def reference(x, mem, ln_mix, w_in, w_out, mem_norm, w_mem_kv, ln_ffn, w_up, conv_w, conv_b,
              w_down, s5_lambda_re, s5_lambda_im, s5_log_step, s5_b_re, s5_b_im, s5_c_re,
              s5_c_im, s5_d, s5_w_glu, kv_norm, w_kv, final_norm):
    bsz, seq, _ = x.shape
    cos, sin = rope_tables(seq)
    shared = None
    for l in range(DEPTH):
        h = rms_norm(x, ln_mix[l])
        z = h @ w_in[l]
        u_self, q_mem = z[..., :SELF_WIDTH], z[..., SELF_WIDTH:]
        if l < N_A_LAYERS:
            self_out = s5_mixer(u_self, s5_lambda_re[l], s5_lambda_im[l], s5_log_step[l],
                                s5_b_re[l], s5_b_im[l], s5_c_re[l], s5_c_im[l], s5_d[l], s5_w_glu[l])
        else:
            if l == N_A_LAYERS:
                shared = make_shared_kv(x, kv_norm, w_kv, cos, sin)
            q = apply_rope(u_self.reshape(bsz, seq, MOBA_HEADS, HEAD_DIM), cos, sin)
            self_out = moba_attention(q, shared[0], shared[1], shared[2])
        mem_out = memory_attention(q_mem, mem, mem_norm[l], w_mem_kv[l])
        x = x + jnp.concatenate([self_out, mem_out], axis=-1) @ w_out[l]
        x = x + conv_ffn(x, ln_ffn[l], w_up[l], conv_w[l], conv_b[l], w_down[l])
    return rms_norm(x, final_norm)
```

```python
import contextlib
import numpy as np
import concourse.bass as bass
import concourse.mybir as mybir
from concourse.bass_utils import run_bass_kernel_spmd

F32 = mybir.dt.float32
BF16 = mybir.dt.bfloat16
AF = mybir.ActivationFunctionType
ALU = mybir.AluOpType
AX = mybir.AxisListType


class Prog:
    ENGS = ['pe', 'act', 'dve', 'pool', 'sp']
    NSLOT = 8

    def __init__(self, nc, stack, sync_same=('act', 'dve', 'pool')):
        self.nc = nc
        self.stack = stack
        self.ops = {e: [] for e in self.ENGS}
        self.lw = {}
        self.rd = {}
        self.sync_same = set(sync_same)
        self.ndma = {e: 0 for e in self.ENGS}
        self._n = 0
        self.final = []

    def sb(self, shape, dt, name=None):
        self._n += 1
        return self.stack.enter_context(self.nc.sbuf_tensor(name or f"sb{self._n}", list(shape), dt))

    def ps(self, shape, dt, name=None):
        self._n += 1
        return self.stack.enter_context(self.nc.psum_tensor(name or f"ps{self._n}", list(shape), dt))

    def op(self, eng, fn, reads=(), writes=(), dma=False):
        idx = len(self.ops[eng])
        deps = set()
        for k in reads:
            if k in self.lw:
                deps.add(self.lw[k])
        for k in writes:
            if k in self.lw:
                deps.add(self.lw[k])
            for r in self.rd.get(k, ()):
                deps.add(r)
        deps.discard((eng, idx))
        rec = dict(fn=fn, deps=deps, needed=False, dma=dma, eng=eng)
        if dma:
            rec['dma_i'] = self.ndma[eng]
            self.ndma[eng] += 1
        self.ops[eng].append(rec)
        for k in writes:
            self.lw[k] = (eng, idx)
            self.rd[k] = []
        for k in reads:
            self.rd.setdefault(k, []).append((eng, idx))
        return (eng, idx)

    def dma(self, out, in_, reads=(), writes=(), eng='sp', final=False, **kw):
        r = self.op(eng, lambda e: e.dma_start(out=out, in_=in_, **kw), reads, writes, dma=True)
        if final:
            self.final.append(r)
        return r

    def emit(self):
        nc = self.nc
        for e in self.ENGS:
            for i, rec in enumerate(self.ops[e]):
                nd = set()
                for (de, di) in rec['deps']:
                    drec = self.ops[de][di]
                    if de == e and not drec['dma']:
                        if rec['dma']:
                            pass
                        elif e not in self.sync_same:
                            continue
                    nd.add((de, di))
                rec['deps'] = nd
                for (de, di) in nd:
                    self.ops[de][di]['needed'] = True
        final = list(self.final)
        for (de, di) in final:
            self.ops[de][di]['needed'] = True
        csem = {e: self.stack.enter_context(nc.semaphore(f"c_{e}")) for e in self.ENGS}
        dsem = {}
        for e in self.ENGS:
            if self.ndma[e]:
                dsem[e] = [self.stack.enter_context(nc.semaphore(f"d_{e}{s}")) for s in range(self.NSLOT)]
        for e in self.ENGS:
            c = 0
            for rec in self.ops[e]:
                if rec['dma']:
                    s = rec['dma_i'] % self.NSLOT
                    rec['sem'] = dsem[e][s]
                    rec['val'] = 16 * (rec['dma_i'] // self.NSLOT + 1)
                else:
                    if rec['needed']:
                        c += 1
                    rec['sem'] = csem[e]
                    rec['val'] = c
        engobj = {'pe': 'tensor', 'act': 'scalar', 'dve': 'vector', 'pool': 'gpsimd', 'sp': 'sync'}
        nwaits = [0]

        def run(e, eng):
            wm = {}

            def wait(sem, val):
                k = id(sem)
                if wm.get(k, 0) >= val:
                    return
                eng.wait_ge(sem, val)
                wm[k] = val
                nwaits[0] += 1

            for rec in self.ops[e]:
                for (de, di) in sorted(rec['deps']):
                    d = self.ops[de][di]
                    wait(d['sem'], d['val'])
                if rec['dma'] and rec['val'] > 16:
                    wait(rec['sem'], rec['val'] - 16)
                ins = rec['fn'](eng)
                if rec['dma']:
                    ins.then_inc(rec['sem'], 16)
                elif rec['needed']:
                    ins.then_inc(rec['sem'], 1)
            if e == 'sp':
                for (de, di) in final:
                    d = self.ops[de][di]
                    wait(d['sem'], d['val'])

        with nc.Block() as block:
            for e in self.ENGS:
                if not self.ops[e] and e != 'sp':
                    continue
                getattr(block, engobj[e])(lambda eng, e=e: run(e, eng))
        self.nwaits = nwaits[0]


D = 1024
DFF = 2816
NCH = DFF // 128
EPS = 1e-6


class Ctx:
    pass


def setup_common(P):
    c = Ctx()
    c.ident = P.sb([128, 128], BF16, "ident")
    P.op('pool', lambda e: e.memset(c.ident[:], 0.0), writes=['ident'])
    P.op('pool', lambda e: e.affine_select(out=c.ident[:], in_=c.ident[:], pattern=[[-1, 128]],
                                           compare_op=ALU.not_equal, fill=1.0, base=0, channel_multiplier=1),
         reads=['ident'], writes=['ident'])
    c.n = 0
    return c


def load_bcast(P, dram_vec, n, name):
    t = P.sb([128, n], F32, name)
    P.dma(t[:], dram_vec.partition_broadcast(128), writes=[name])
    return t


def norm_T(P, c, xt, xkey, np_, gain_bc, gkey, hT, hkey, col0, tmp):
    i = c.n
    c.n += 1
    b = i % 2
    junk, ss, xb, pst = tmp['junk'][b], tmp['ss'][b], tmp['xb'][b], tmp['pst'][b]
    kj, ks, kx, kp = f'junk{b}', f'ss{b}', f'xb{b}', f'pst{b}'
    P.op('act', lambda e: e.activation(out=junk[:np_, :], in_=xt, func=AF.Square, accum_out=ss[:np_, :]),
         reads=[xkey], writes=[kj, ks])
    P.op('dve', lambda e: e.tensor_scalar(out=ss[:np_, :], in0=ss[:np_, :], scalar1=1.0 / D, scalar2=EPS, op0=ALU.mult, op1=ALU.add),
         reads=[ks], writes=[ks])
    P.op('act', lambda e: e.activation(out=ss[:np_, :], in_=ss[:np_, :], func=AF.Sqrt), reads=[ks], writes=[ks])
    P.op('dve', lambda e: e.reciprocal(out=ss[:np_, :], in_=ss[:np_, :]), reads=[ks], writes=[ks])
    P.op('dve', lambda e: e.scalar_tensor_tensor(out=xb[:np_, :], in0=xt, scalar=ss[:np_, 0:1], in1=gain_bc[:np_, :],
                                                 op0=ALU.mult, op1=ALU.mult), reads=[xkey, ks, gkey], writes=[kx])
    for k in range(8):
        P.op('pe', lambda e, k=k: e.transpose(out=pst[:, k * 128:k * 128 + np_], in_=xb[:np_, k * 128:(k + 1) * 128],
                                              identity=c.ident[:np_, :np_]), reads=[kx, 'ident'], writes=[(kp, k)])
    P.op('act', lambda e: e.activation(out=hT[:, :, col0:col0 + np_],
                                       in_=pst[:].rearrange("p (k t) -> p k t", k=8)[:, :, :np_], func=AF.Copy),
         reads=[(kp, k) for k in range(8)], writes=[hkey])
    return ss


def make_norm_tmp(P):
    return dict(junk=[P.sb([128, D], F32) for _ in range(2)], ss=[P.sb([128, 1], F32) for _ in range(2)],
                xb=[P.sb([128, D], BF16) for _ in range(2)], pst=[P.ps([128, 1024], BF16) for _ in range(2)])


def build_ffn(NT, final_norm):
    nc = bass.Bass("TRN2", target_bir_lowering=False)
    xm = nc.dram_tensor("xm", [NT, D], F32, kind="ExternalInput").ap()
    xh = nc.dram_tensor("xh", [2, D], F32, kind="ExternalInput").ap()
    ln = nc.dram_tensor("ln", [D], F32, kind="ExternalInput").ap()
    wup = nc.dram_tensor("wup", [D, 2 * DFF], F32, kind="ExternalInput").ap()
    cw = nc.dram_tensor("cw", [3, 2 * DFF], F32, kind="ExternalInput").ap()
    cb = nc.dram_tensor("cb", [2 * DFF], F32, kind="ExternalInput").ap()
    wdown = nc.dram_tensor("wdown", [DFF, D], F32, kind="ExternalInput").ap()
    fng = nc.dram_tensor("fng", [D], F32, kind="ExternalInput").ap()
    y = nc.dram_tensor("y", [NT, D], F32, kind="ExternalOutput").ap()
    T = 512
    ntile = NT // T
    with contextlib.ExitStack() as st:
        P = Prog(nc, st)
        c = setup_common(P)
        tmp = make_norm_tmp(P)
        gain = load_bcast(P, ln, D, "gain")
        fgain = load_bcast(P, fng, D, "fgain") if final_norm else None
        cwt = P.sb([128, 3, 2 * NCH], F32, "cwt")
        cbt = P.sb([128, 2 * NCH], F32, "cbt")
        P.dma(cwt[:], cw.rearrange("t (c p) -> p t c", p=128), writes=['cwt'], allow_slow_non_contiguous=True)
        P.dma(cbt[:], cb.rearrange("(c p) -> p c", p=128), writes=['cbt'], allow_slow_non_contiguous=True)
        wd = P.sb([128, NCH, D], BF16, "wd")
        wdst = [P.sb([128, D], F32) for _ in range(2)]
        for j in range(NCH):
            b = j % 2
            P.dma(wdst[b][:], wdown[j * 128:(j + 1) * 128, :], writes=[f'wdst{b}'])
            P.op('pool', lambda e, j=j, b=b: e.tensor_copy(out=wd[:, j, :], in_=wdst[b][:]), reads=[f'wdst{b}'], writes=[('wd', j)])
        xtt = [P.sb([128, D], F32) for _ in range(2)]
        xres = P.sb([128, 4, D], F32, "xres")
        hT = P.sb([128, 8, T], BF16, "hT")
        hTh = P.sb([128, 8, 2], BF16, "hTh")
        Hs = P.sb([128, 2 * NCH, 2], F32, "Hs")
        wst = [P.sb([128, 8, 256], F32) for _ in range(2)]
        wbf = [P.sb([128, 8, 256], BF16) for _ in range(2)]
        Ug = [P.sb([128, T + 2], F32) for _ in range(2)]
        Uv = [P.sb([128, T + 2], F32) for _ in range(2)]
        ag = [P.sb([128, T], F32) for _ in range(2)]
        av = [P.sb([128, T], F32) for _ in range(2)]
        hid = P.sb([128, NCH, T], BF16, "hid")
        psg = [P.ps([128, T], F32) for _ in range(2)]
        psv = [P.ps([128, T], F32) for _ in range(2)]
        pso = [P.ps([128, 512], F32) for _ in range(2)]
        yo = [P.sb([128, D], F32) for _ in range(2)]
        junk2 = P.sb([128, D], F32, "junk2")
        ss2 = [P.sb([128, 1], F32) for _ in range(2)]
        cnt = dict(w=0, o=0, x=0)

        def load_w(j):
            b = cnt['w'] % 2
            cnt['w'] += 1
            P.dma(wst[b][:, :, 0:128], wup[:, j * 128:(j + 1) * 128].rearrange("(k p) n -> p k n", p=128), writes=[(f'wst{b}', 0)])
            P.dma(wst[b][:, :, 128:256], wup[:, DFF + j * 128:DFF + (j + 1) * 128].rearrange("(k p) n -> p k n", p=128), writes=[(f'wst{b}', 1)])
            P.op('pool', lambda e: e.tensor_copy(out=wbf[b][:], in_=wst[b][:]), reads=[(f'wst{b}', 0), (f'wst{b}', 1)], writes=[f'wbf{b}'])
            return b

        b0 = cnt['x'] % 2
        cnt['x'] += 1
        P.dma(xtt[b0][:2, :], xh, writes=[f'xtt{b0}'])
        norm_T(P, c, xtt[b0][:2, :], f'xtt{b0}', 2, gain, 'gain', hTh, 'hTh', 0, tmp)
        for j in range(NCH):
            wb = load_w(j)
            pb = j % 2
            for k in range(8):
                P.op('pe', lambda e, k=k, wb=wb, pb=pb: e.matmul(psg[pb][:, 0:2], lhsT=wbf[wb][:, k, 0:128], rhs=hTh[:, k, :], start=(k == 0), stop=(k == 7)),
                     reads=[f'wbf{wb}', 'hTh'], writes=[f'psg{pb}'])
            for k in range(8):
                P.op('pe', lambda e, k=k, wb=wb, pb=pb: e.matmul(psv[pb][:, 0:2], lhsT=wbf[wb][:, k, 128:256], rhs=hTh[:, k, :], start=(k == 0), stop=(k == 7)),
                     reads=[f'wbf{wb}', 'hTh'], writes=[f'psv{pb}'])
            P.op('act', lambda e, j=j, pb=pb: e.activation(out=Hs[:, j, :], in_=psg[pb][:, 0:2], func=AF.Copy), reads=[f'psg{pb}'], writes=[('Hs', j)])
            P.op('act', lambda e, j=j, pb=pb: e.activation(out=Hs[:, NCH + j, :], in_=psv[pb][:, 0:2], func=AF.Copy), reads=[f'psv{pb}'], writes=[('Hs', NCH + j)])

        for ti in range(ntile):
            t0 = ti * T
            for s in range(4):
                bx = cnt['x'] % 2
                cnt['x'] += 1
                P.dma(xtt[bx][:], xm[t0 + s * 128:t0 + (s + 1) * 128, :], writes=[f'xtt{bx}'])
                P.op('pool', lambda e, s=s, bx=bx: e.tensor_copy(out=xres[:, s, :], in_=xtt[bx][:]), reads=[f'xtt{bx}'], writes=[('xres', s)])
                norm_T(P, c, xtt[bx][:], f'xtt{bx}', 128, gain, 'gain', hT, ('hT', s), s * 128, tmp)
            hkeys = [('hT', s) for s in range(4)]
            for j in range(NCH):
                wb = load_w(j)
                pb = j % 2
                for k in range(8):
                    P.op('pe', lambda e, k=k, wb=wb, pb=pb: e.matmul(psg[pb][:], lhsT=wbf[wb][:, k, 0:128], rhs=hT[:, k, :], start=(k == 0), stop=(k == 7)),
                         reads=[f'wbf{wb}'] + hkeys, writes=[f'psg{pb}'])
                for k in range(8):
                    P.op('pe', lambda e, k=k, wb=wb, pb=pb: e.matmul(psv[pb][:], lhsT=wbf[wb][:, k, 128:256], rhs=hT[:, k, :], start=(k == 0), stop=(k == 7)),
                         reads=[f'wbf{wb}'] + hkeys, writes=[f'psv{pb}'])
                for (U, ps, acc, ch, nm, eng2) in ((Ug[pb], psg[pb], ag[pb], j, 'g', 'dve'), (Uv[pb], psv[pb], av[pb], NCH + j, 'v', 'dve')):
                    uk, pk, ak = f'U{nm}{pb}', f'ps{nm}{pb}', f'a{nm}{pb}'
                    P.op('pool', lambda e, U=U, ch=ch: e.tensor_copy(out=U[:, 0:2], in_=Hs[:, ch, :]), reads=[('Hs', ch)], writes=[(uk, 0)])
                    P.op('act', lambda e, U=U, ps=ps: e.activation(out=U[:, 2:T + 2], in_=ps[:], func=AF.Copy), reads=[pk], writes=[(uk, 1)])
                    P.op('pool', lambda e, U=U, ch=ch: e.tensor_copy(out=Hs[:, ch, :], in_=U[:, T:T + 2]), reads=[(uk, 1)], writes=[('Hs', ch)])
                    P.op('dve', lambda e, U=U, acc=acc, ch=ch: e.tensor_scalar(out=acc[:], in0=U[:, 2:T + 2], scalar1=cwt[:, 2, ch:ch + 1], scalar2=cbt[:, ch:ch + 1],
                                                                             op0=ALU.mult, op1=ALU.add), reads=[(uk, 1), 'cwt', 'cbt'], writes=[ak])
                    P.op(eng2, lambda e, U=U, acc=acc, ch=ch: e.scalar_tensor_tensor(out=acc[:], in0=U[:, 1:T + 1], scalar=cwt[:, 1, ch:ch + 1], in1=acc[:],
                                                                                    op0=ALU.mult, op1=ALU.add), reads=[(uk, 0), (uk, 1), 'cwt', ak], writes=[ak])
                    P.op(eng2, lambda e, U=U, acc=acc, ch=ch: e.scalar_tensor_tensor(out=acc[:], in0=U[:, 0:T], scalar=cwt[:, 0, ch:ch + 1], in1=acc[:],
                                                                                    op0=ALU.mult, op1=ALU.add), reads=[(uk, 0), (uk, 1), 'cwt', ak], writes=[ak])
                P.op('act', lambda e, pb=pb: e.activation(out=ag[pb][:], in_=ag[pb][:], func=AF.Silu), reads=[f'ag{pb}'], writes=[f'ag{pb}'])
                P.op('dve', lambda e, pb=pb, j=j: e.tensor_tensor(out=hid[:, j, :], in0=ag[pb][:], in1=av[pb][:], op=ALU.mult),
                     reads=[f'ag{pb}', f'av{pb}'], writes=[('hid', j)])
            for s in range(4):
                ob = cnt['o'] % 2
                cnt['o'] += 1
                for half in range(2):
                    pb = half
                    for j in range(NCH):
                        P.op('pe', lambda e, j=j, s=s, half=half, pb=pb: e.matmul(pso[pb][:], lhsT=hid[:, j, s * 128:(s + 1) * 128],
                                                                                rhs=wd[:, j, half * 512:(half + 1) * 512], start=(j == 0), stop=(j == NCH - 1)),
                             reads=[('hid', j), ('wd', j)], writes=[f'pso{pb}'])
                    P.op('dve', lambda e, s=s, half=half, pb=pb, ob=ob: e.tensor_tensor(out=yo[ob][:, half * 512:(half + 1) * 512], in0=pso[pb][:],
                                                                                      in1=xres[:, s, half * 512:(half + 1) * 512], op=ALU.add),
                         reads=[f'pso{pb}', ('xres', s)], writes=[(f'yo{ob}', half)])
                yk = [(f'yo{ob}', 0), (f'yo{ob}', 1)]
                if final_norm:
                    sb_ = ss2[ob]
                    sk = f'ss2{ob}'
                    P.op('act', lambda e, ob=ob, sb_=sb_: e.activation(out=junk2[:], in_=yo[ob][:], func=AF.Square, accum_out=sb_[:]), reads=yk, writes=['junk2', sk])
                    P.op('dve', lambda e, sb_=sb_: e.tensor_scalar(out=sb_[:], in0=sb_[:], scalar1=1.0 / D, scalar2=EPS, op0=ALU.mult, op1=ALU.add), reads=[sk], writes=[sk])
                    P.op('act', lambda e, sb_=sb_: e.activation(out=sb_[:], in_=sb_[:], func=AF.Sqrt), reads=[sk], writes=[sk])
                    P.op('dve', lambda e, sb_=sb_: e.reciprocal(out=sb_[:], in_=sb_[:]), reads=[sk], writes=[sk])
                    P.op('dve', lambda e, ob=ob, sb_=sb_: e.scalar_tensor_tensor(out=yo[ob][:], in0=yo[ob][:], scalar=sb_[:, 0:1], in1=fgain[:], op0=ALU.mult, op1=ALU.mult),
                         reads=yk + [sk, 'fgain'], writes=yk)
                P.dma(y[t0 + s * 128:t0 + (s + 1) * 128, :], yo[ob][:], reads=yk, final=True)
        P.emit()
        print("ffn ops", {e: len(P.ops[e]) for e in P.ENGS}, "waits", P.nwaits)
    return nc


SW = 768


def load_w_bf16(P, wdram_view, shape, name, parts=128):
    w = P.sb(shape, BF16, "sbw_" + name)
    if not hasattr(P, "_stg"):
        P._stg = P.sb([128, 6144], F32, "wstage")
    stg = P._stg[0:shape[0], 0:shape[1] * shape[2]].rearrange("p (k n) -> p k n", k=shape[1])
    P.dma(stg, wdram_view, writes=["wstage"])
    P.op('pool', lambda e: e.tensor_copy(out=w[:], in_=stg), reads=["wstage"], writes=[name])
    return w


def build_tail(NT, glu):
    nc = bass.Bass("TRN2", target_bir_lowering=False)
    x = nc.dram_tensor("x", [NT, D], F32, kind="ExternalInput").ap()
    sT = nc.dram_tensor("sT", [SW, NT], F32, kind="ExternalInput").ap()
    mem = nc.dram_tensor("mem", [256, D], F32, kind="ExternalInput").ap()
    ln = nc.dram_tensor("ln", [D], F32, kind="ExternalInput").ap()
    wq = nc.dram_tensor("wq", [D, 256], F32, kind="ExternalInput").ap()
    mng = nc.dram_tensor("mng", [D], F32, kind="ExternalInput").ap()
    wmkv = nc.dram_tensor("wmkv", [D, 512], F32, kind="ExternalInput").ap()
    wglu = nc.dram_tensor("wglu", [SW, SW], F32, kind="ExternalInput").ap()
    wout = nc.dram_tensor("wout", [D, D], F32, kind="ExternalInput").ap()
    y = nc.dram_tensor("y", [NT, D], F32, kind="ExternalOutput").ap()
    T = 512
    ntile = NT // T
    with contextlib.ExitStack() as st:
        P = Prog(nc, st)
        c = setup_common(P)
        tmp = make_norm_tmp(P)
        gain = load_bcast(P, ln, D, "gain")
        mgain = load_bcast(P, mng, D, "mgain")
        wq_b = load_w_bf16(P, wq.rearrange("(k p) n -> p k n", p=128), [128, 8, 256], "wq")
        wm_b = load_w_bf16(P, wmkv.rearrange("(k p) n -> p k n", p=128), [128, 8, 512], "wm")
        wo_s = load_w_bf16(P, wout[0:SW, :].rearrange("(k p) n -> p k n", p=128), [128, 6, D], "wos")
        wo_m = load_w_bf16(P, wout[SW:D, :].rearrange("(h p) n -> p h n", p=64), [64, 4, D], "wom")
        wg_b = load_w_bf16(P, wglu.rearrange("(k p) n -> p k n", p=128), [128, 6, SW], "wg") if glu else None
        ones64 = P.sb([128, 64], BF16, "ones64")
        P.op('pool', lambda e: e.memset(ones64[:], 1.0), writes=['ones64'])
        psA = [P.ps([128, 512], F32) for _ in range(2)]
        psO = P.ps([128, 512], F32)
        psD = P.ps([128, 512], F32)
        pso = [P.ps([128, 512], F32) for _ in range(2)]
        xtt = [P.sb([128, D], F32) for _ in range(2)]
        xres = P.sb([128, 4, D], F32, "xres")
        hT = P.sb([128, 8, T], BF16, "hT")
        hTm = P.sb([128, 8, 256], BF16, "hTm")
        KmT = P.sb([128, 2, 256], BF16, "KmT")
        Vm = P.sb([128, 2, 256], BF16, "Vm")
        cnt = dict(x=0, a=0, o=0)
        for s in range(2):
            bx = cnt['x'] % 2
            cnt['x'] += 1
            P.dma(xtt[bx][:], mem[s * 128:(s + 1) * 128, :], writes=[f'xtt{bx}'])
            norm_T(P, c, xtt[bx][:], f'xtt{bx}', 128, mgain, 'mgain', hTm, ('hTm', s), s * 128, tmp)
        hmk = [('hTm', 0), ('hTm', 1)]
        for cch in range(2):
            pa = cnt['a'] % 2
            cnt['a'] += 1
            for k in range(8):
                P.op('pe', lambda e, k=k, pa=pa, cch=cch: e.matmul(psA[pa][:, 0:256], lhsT=wm_b[:, k, cch * 128:(cch + 1) * 128], rhs=hTm[:, k, :], start=(k == 0), stop=(k == 7)),
                     reads=['wm'] + hmk, writes=[f'psA{pa}'])
            P.op('act', lambda e, pa=pa, cch=cch: e.activation(out=KmT[:, cch, :], in_=psA[pa][:, 0:256], func=AF.Copy), reads=[f'psA{pa}'], writes=[('KmT', cch)])
        for s in range(2):
            pa = cnt['a'] % 2
            cnt['a'] += 1
            for k in range(8):
                P.op('pe', lambda e, k=k, pa=pa, s=s: e.matmul(psA[pa][:, 0:256], lhsT=hTm[:, k, s * 128:(s + 1) * 128], rhs=wm_b[:, k, 256:512], start=(k == 0), stop=(k == 7)),
                     reads=['wm'] + hmk, writes=[f'psA{pa}'])
            P.op('act', lambda e, pa=pa, s=s: e.activation(out=Vm[:, s, :], in_=psA[pa][:, 0:256], func=AF.Copy), reads=[f'psA{pa}'], writes=[('Vm', s)])
        qT = P.sb([128, 2, T], BF16, "qT")
        PT = [P.sb([128, T], BF16) for _ in range(2)]
        rec = P.sb([64, T], F32, "rec")
        memT = P.sb([64, 4, T], BF16, "memT")
        sin = P.sb([128, 6, T], F32, "sin")
        mixT = P.sb([128, 6, T], BF16, "mixT")
        if glu:
            gf = P.sb([128, 6, T], F32, "gf")
            gb = P.sb([128, 6, T], BF16, "gb")
            t1 = [P.sb([128, T], F32) for _ in range(2)]
            sg = [P.sb([128, T], F32) for _ in range(2)]
        yo = [P.sb([128, D], F32) for _ in range(2)]
        for ti in range(ntile):
            t0 = ti * T
            for s in range(4):
                bx = cnt['x'] % 2
                cnt['x'] += 1
                P.dma(xtt[bx][:], x[t0 + s * 128:t0 + (s + 1) * 128, :], writes=[f'xtt{bx}'])
                P.op('pool', lambda e, s=s, bx=bx: e.tensor_copy(out=xres[:, s, :], in_=xtt[bx][:]), reads=[f'xtt{bx}'], writes=[('xres', s)])
                norm_T(P, c, xtt[bx][:], f'xtt{bx}', 128, gain, 'gain', hT, ('hT', s), s * 128, tmp)
            hkeys = [('hT', s) for s in range(4)]
            for cch in range(2):
                pa = cnt['a'] % 2
                cnt['a'] += 1
                for k in range(8):
                    P.op('pe', lambda e, k=k, pa=pa, cch=cch: e.matmul(psA[pa][:], lhsT=wq_b[:, k, cch * 128:(cch + 1) * 128], rhs=hT[:, k, :], start=(k == 0), stop=(k == 7)),
                         reads=['wq'] + hkeys, writes=[f'psA{pa}'])
                P.op('act', lambda e, pa=pa, cch=cch: e.activation(out=qT[:, cch, :], in_=psA[pa][:], func=AF.Copy, scale=0.125), reads=[f'psA{pa}'], writes=[('qT', cch)])
            for h in range(4):
                cch, po = h // 2, (h % 2) * 64
                for ms in range(2):
                    pa = cnt['a'] % 2
                    cnt['a'] += 1
                    P.op('pe', lambda e, pa=pa, cch=cch, po=po, ms=ms: e.matmul(psA[pa][:], lhsT=KmT[po:po + 64, cch, ms * 128:(ms + 1) * 128], rhs=qT[po:po + 64, cch, :], start=True, stop=True),
                         reads=[('KmT', cch), ('qT', cch)], writes=[f'psA{pa}'])
                    P.op('act', lambda e, pa=pa, ms=ms: e.activation(out=PT[ms][:], in_=psA[pa][:], func=AF.Exp), reads=[f'psA{pa}'], writes=[f'PT{ms}'])
                for ms in range(2):
                    P.op('pe', lambda e, ms=ms, h=h: e.matmul(psO[0:64, :], lhsT=Vm[:, ms, h * 64:(h + 1) * 64], rhs=PT[ms][:], start=(ms == 0), stop=(ms == 1)),
                         reads=[('Vm', ms), f'PT{ms}'], writes=['psO'])
                for ms in range(2):
                    P.op('pe', lambda e, ms=ms: e.matmul(psD[0:64, :], lhsT=ones64[:], rhs=PT[ms][:], start=(ms == 0), stop=(ms == 1)),
                         reads=['ones64', f'PT{ms}'], writes=['psD'])
                P.op('dve', lambda e: e.reciprocal(out=rec[:], in_=psD[0:64, :]), reads=['psD'], writes=['rec'])
                P.op('dve', lambda e, h=h: e.tensor_tensor(out=memT[:, h, :], in0=psO[0:64, :], in1=rec[:], op=ALU.mult), reads=['psO', 'rec'], writes=[('memT', h)])
            for k in range(6):
                P.dma(sin[:, k, :], sT[k * 128:(k + 1) * 128, t0:t0 + T], writes=[('sin', k)])
            if glu:
                for k in range(6):
                    b = k % 2
                    P.op('act', lambda e, k=k, b=b: e.activation(out=t1[b][:], in_=sin[:, k, :], func=AF.Square), reads=[('sin', k)], writes=[f't1{b}'])
                    P.op('dve', lambda e, b=b: e.tensor_scalar(out=t1[b][:], in0=t1[b][:], scalar1=0.044715, scalar2=1.0, op0=ALU.mult, op1=ALU.add), reads=[f't1{b}'], writes=[f't1{b}'])
                    P.op('dve', lambda e, k=k, b=b: e.tensor_tensor(out=t1[b][:], in0=t1[b][:], in1=sin[:, k, :], op=ALU.mult), reads=[f't1{b}', ('sin', k)], writes=[f't1{b}'])
                    P.op('act', lambda e, b=b: e.activation(out=t1[b][:], in_=t1[b][:], func=AF.Sigmoid, scale=1.5957691216057308), reads=[f't1{b}'], writes=[f't1{b}'])
                    P.op('dve', lambda e, k=k, b=b: e.tensor_tensor(out=gf[:, k, :], in0=t1[b][:], in1=sin[:, k, :], op=ALU.mult), reads=[f't1{b}', ('sin', k)], writes=[('gf', k)])
                    P.op('pool', lambda e, k=k: e.tensor_copy(out=gb[:, k, :], in_=gf[:, k, :]), reads=[('gf', k)], writes=[('gb', k)])
                for cch in range(6):
                    pa = cnt['a'] % 2
                    cnt['a'] += 1
                    b = cch % 2
                    for k in range(6):
                        P.op('pe', lambda e, k=k, pa=pa, cch=cch: e.matmul(psA[pa][:], lhsT=wg_b[:, k, cch * 128:(cch + 1) * 128], rhs=gb[:, k, :], start=(k == 0), stop=(k == 5)),
                             reads=['wg'] + [('gb', kk) for kk in range(6)], writes=[f'psA{pa}'])
                    P.op('act', lambda e, pa=pa, b=b: e.activation(out=sg[b][:], in_=psA[pa][:], func=AF.Sigmoid), reads=[f'psA{pa}'], writes=[f'sg{b}'])
                    P.op('dve', lambda e, cch=cch, b=b: e.tensor_tensor(out=mixT[:, cch, :], in0=gf[:, cch, :], in1=sg[b][:], op=ALU.mult), reads=[('gf', cch), f'sg{b}'], writes=[('mixT', cch)])
            else:
                for k in range(6):
                    P.op('pool', lambda e, k=k: e.tensor_copy(out=mixT[:, k, :], in_=sin[:, k, :]), reads=[('sin', k)], writes=[('mixT', k)])
            for s in range(4):
                ob = cnt['o'] % 2
                cnt['o'] += 1
                for half in range(2):
                    pb = half
                    for k in range(6):
                        P.op('pe', lambda e, k=k, s=s, half=half, pb=pb: e.matmul(pso[pb][:], lhsT=mixT[:, k, s * 128:(s + 1) * 128], rhs=wo_s[:, k, half * 512:(half + 1) * 512], start=(k == 0), stop=False),
                             reads=[('mixT', k), 'wos'], writes=[f'pso{pb}'])
                    for h in range(4):
                        P.op('pe', lambda e, h=h, s=s, half=half, pb=pb: e.matmul(pso[pb][:], lhsT=memT[:, h, s * 128:(s + 1) * 128], rhs=wo_m[:, h, half * 512:(half + 1) * 512], start=False, stop=(h == 3)),
                             reads=[('memT', h), 'wom'], writes=[f'pso{pb}'])
                    P.op('dve', lambda e, s=s, half=half, pb=pb, ob=ob: e.tensor_tensor(out=yo[ob][:, half * 512:(half + 1) * 512], in0=pso[pb][:], in1=xres[:, s, half * 512:(half + 1) * 512], op=ALU.add),
                         reads=[f'pso{pb}', ('xres', s)], writes=[(f'yo{ob}', half)])
                P.dma(y[t0 + s * 128:t0 + (s + 1) * 128, :], yo[ob][:], reads=[(f'yo{ob}', 0), (f'yo{ob}', 1)], final=True)
        P.emit()
        print("tail ops", {e: len(P.ops[e]) for e in P.ENGS}, "waits", P.nwaits)
    return nc

import math

I32 = mybir.dt.int32
PI = math.pi


def build_s5(NTOK):
    nc = bass.Bass("TRN2", target_bir_lowering=False)
    xb = nc.dram_tensor("xb", [NTOK, D], F32, kind="ExternalInput").ap()
    ln = nc.dram_tensor("ln", [D], F32, kind="ExternalInput").ap()
    win = nc.dram_tensor("win", [D, 192], F32, kind="ExternalInput").ap()
    lre = nc.dram_tensor("lre", [12, 64], F32, kind="ExternalInput").ap()
    lim = nc.dram_tensor("lim", [12, 64], F32, kind="ExternalInput").ap()
    lst = nc.dram_tensor("lst", [12], F32, kind="ExternalInput").ap()
    bre = nc.dram_tensor("bre", [12, 64, 16], F32, kind="ExternalInput").ap()
    bim = nc.dram_tensor("bim", [12, 64, 16], F32, kind="ExternalInput").ap()
    cre = nc.dram_tensor("cre", [12, 16, 64], F32, kind="ExternalInput").ap()
    cim = nc.dram_tensor("cim", [12, 16, 64], F32, kind="ExternalInput").ap()
    dd = nc.dram_tensor("dd", [12, 16], F32, kind="ExternalInput").ap()
    yT = nc.dram_tensor("yT", [192, NTOK], F32, kind="ExternalOutput").ap()
    T = 512
    nfr = NTOK // T
    with contextlib.ExitStack() as st:
        P = Prog(nc, st)
        c = setup_common(P)
        tmp = make_norm_tmp(P)
        gain = load_bcast(P, ln, D, "gain")
        win_b = load_w_bf16(P, win.rearrange("(k p) n -> p k n", p=128), [128, 8, 192], "win")
        uid = [0]

        def new(shape, dt=F32):
            uid[0] += 1
            nm = f"t{uid[0]}"
            return P.sb(shape, dt, nm), nm

        def dve(fn, reads, writes):
            P.op('dve', fn, reads=reads, writes=writes)

        sc_tmp = {}

        def sincos(ang, kang, n):
            s_, ks = new([128, n])
            c_, kc = new([128, n])
            for (o, ko, sh) in ((s_, ks, 0.0), (c_, kc, 0.5 * PI)):
                if n not in sc_tmp:
                    sc_tmp[n] = (new([128, n]), new([128, n], I32), new([128, n]))
                (a_, ka), (ki, kki), (kf, kkf) = sc_tmp[n]
                dve(lambda e, a_=a_, sh=sh: e.tensor_scalar(out=a_[:], in0=ang[:], scalar1=sh, scalar2=None, op0=ALU.add), [kang], [ka])
                dve(lambda e, a_=a_, kf=kf: e.tensor_scalar(out=kf[:], in0=a_[:], scalar1=1.0 / (2 * PI), scalar2=None, op0=ALU.mult), [ka], [kkf])
                dve(lambda e, ki=ki, kf=kf: e.tensor_copy(out=ki[:], in_=kf[:]), [kkf], [kki])
                dve(lambda e, ki=ki, kf=kf: e.tensor_copy(out=kf[:], in_=ki[:]), [kki], [kkf])
                dve(lambda e, o=o, kf=kf, a_=a_: e.scalar_tensor_tensor(out=o[:], in0=kf[:], scalar=-2 * PI, in1=a_[:], op0=ALU.mult, op1=ALU.add), [kkf, ka], [ko])
                dve(lambda e, o=o, kf=kf: e.tensor_scalar(out=kf[:], in0=o[:], scalar1=PI, scalar2=2 * PI, op0=ALU.is_gt, op1=ALU.mult), [ko], [kkf])
                dve(lambda e, o=o, kf=kf: e.tensor_tensor(out=o[:], in0=o[:], in1=kf[:], op=ALU.subtract), [ko, kkf], [ko])
                dve(lambda e, o=o, kf=kf: e.tensor_scalar(out=kf[:], in0=o[:], scalar1=-PI, scalar2=2 * PI, op0=ALU.is_lt, op1=ALU.mult), [ko], [kkf])
                dve(lambda e, o=o, kf=kf: e.tensor_tensor(out=o[:], in0=o[:], in1=kf[:], op=ALU.add), [ko, kkf], [ko])
                P.op('act', lambda e, o=o: e.activation(out=o[:], in_=o[:], func=AF.Sin), reads=[ko], writes=[ko])
            return s_, ks, c_, kc

        io_i, kioi = new([128, T], I32)
        iof, kio = new([128, T])
        P.op('pool', lambda e: e.iota(io_i[:], pattern=[[1, T]], base=0, channel_multiplier=0), writes=[kioi])
        dve(lambda e: e.tensor_copy(out=iof[:], in_=io_i[:]), [kioi], [kio])
        ones_t, kones = new([128, T])
        P.op('pool', lambda e: e.memset(ones_t[:], 1.0), writes=[kones])
        dch0, kd0 = new([128, 1])
        dch1, kd1 = new([128, 1])
        ddf = dd.rearrange("g (p o) -> (g p) o", o=1)
        P.dma(dch0[:], ddf[0:128, :], writes=[kd0])
        P.dma(dch1[0:64, :], ddf[128:192, :], writes=[kd1])

        prm = []
        for pr in range(6):
            g0 = 2 * pr
            kt = pr // 4
            col0 = (32 * pr) % 128
            q = Ctx()
            lr, klr = new([128, 1])
            li, kli = new([128, 1])
            ls, kls = new([128, 1])
            P.dma(lr[:], lre[g0:g0 + 2, :].rearrange("g (n o) -> (g n) o", o=1), writes=[klr])
            P.dma(li[:], lim[g0:g0 + 2, :].rearrange("g (n o) -> (g n) o", o=1), writes=[kli])
            P.dma(ls[0:64, :], lst[g0:g0 + 1].partition_broadcast(64), writes=[(kls, 0)])
            P.dma(ls[64:128, :], lst[g0 + 1:g0 + 2].partition_broadcast(64), writes=[(kls, 1)])
            dt_, kdt = new([128, 1])
            P.op('act', lambda e, dt_=dt_, ls=ls: e.activation(out=dt_[:], in_=ls[:], func=AF.Exp), reads=[(kls, 0), (kls, 1)], writes=[kdt])
            rho, krho = new([128, 1])
            dve(lambda e, rho=rho, lr=lr, dt_=dt_: e.tensor_tensor(out=rho[:], in0=lr[:], in1=dt_[:], op=ALU.mult), [klr, kdt], [krho])
            P.op('act', lambda e, rho=rho: e.activation(out=rho[:], in_=rho[:], func=AF.Exp), reads=[krho], writes=[krho])
            th, kth = new([128, 1])
            dve(lambda e, th=th, li=li, dt_=dt_: e.tensor_tensor(out=th[:], in0=li[:], in1=dt_[:], op=ALU.mult), [kli, kdt], [kth])
            sn, ksn, cs, kcs = sincos(th, kth, 1)
            abr, kabr = new([128, 1])
            abi, kabi = new([128, 1])
            dve(lambda e, abr=abr, rho=rho, cs=cs: e.tensor_tensor(out=abr[:], in0=rho[:], in1=cs[:], op=ALU.mult), [krho, kcs], [kabr])
            dve(lambda e, abi=abi, rho=rho, sn=sn: e.tensor_tensor(out=abi[:], in0=rho[:], in1=sn[:], op=ALU.mult), [krho, ksn], [kabi])
            den, kden = new([128, 1])
            t_, kt_ = new([128, 1])
            dve(lambda e, den=den, lr=lr: e.tensor_tensor(out=den[:], in0=lr[:], in1=lr[:], op=ALU.mult), [klr], [kden])
            dve(lambda e, t_=t_, li=li: e.tensor_tensor(out=t_[:], in0=li[:], in1=li[:], op=ALU.mult), [kli], [kt_])
            dve(lambda e, den=den, t_=t_: e.tensor_tensor(out=den[:], in0=den[:], in1=t_[:], op=ALU.add), [kden, kt_], [kden])
            dve(lambda e, den=den: e.reciprocal(out=den[:], in_=den[:]), [kden], [kden])
            nr, knr = new([128, 1])
            dve(lambda e, nr=nr, abr=abr: e.tensor_scalar(out=nr[:], in0=abr[:], scalar1=-1.0, scalar2=None, op0=ALU.add), [kabr], [knr])
            cfr, kcfr = new([128, 1])
            cfi, kcfi = new([128, 1])
            ncfi, kncfi = new([128, 1])
            dve(lambda e, t_=t_, abi=abi, li=li: e.tensor_tensor(out=t_[:], in0=abi[:], in1=li[:], op=ALU.mult), [kabi, kli], [kt_])
            dve(lambda e, cfr=cfr, nr=nr, lr=lr, t_=t_: e.scalar_tensor_tensor(out=cfr[:], in0=nr[:], scalar=lr[:, 0:1], in1=t_[:], op0=ALU.mult, op1=ALU.add), [knr, klr, kt_], [kcfr])
            dve(lambda e, cfr=cfr, den=den: e.tensor_tensor(out=cfr[:], in0=cfr[:], in1=den[:], op=ALU.mult), [kcfr, kden], [kcfr])
            dve(lambda e, t_=t_, nr=nr, li=li: e.tensor_tensor(out=t_[:], in0=nr[:], in1=li[:], op=ALU.mult), [knr, kli], [kt_])
            dve(lambda e, cfi=cfi, abi=abi, lr=lr, t_=t_: e.scalar_tensor_tensor(out=cfi[:], in0=abi[:], scalar=lr[:, 0:1], in1=t_[:], op0=ALU.mult, op1=ALU.subtract), [kabi, klr, kt_], [kcfi])
            dve(lambda e, cfi=cfi, den=den: e.tensor_tensor(out=cfi[:], in0=cfi[:], in1=den[:], op=ALU.mult), [kcfi, kden], [kcfi])
            dve(lambda e, ncfi=ncfi, cfi=cfi: e.tensor_scalar(out=ncfi[:], in0=cfi[:], scalar1=-1.0, scalar2=None, op0=ALU.mult), [kcfi], [kncfi])
            Br, kBr = new([128, 16])
            Bi, kBi = new([128, 16])
            P.dma(Br[:], bre[g0:g0 + 2].rearrange("g n q -> (g n) q"), writes=[kBr])
            P.dma(Bi[:], bim[g0:g0 + 2].rearrange("g n q -> (g n) q"), writes=[kBi])
            q.LB = []
            for part in range(2):
                Bb, kBb = new([128, 16])
                if part == 0:
                    dve(lambda e, Bb=Bb, Br=Br, cfr=cfr: e.tensor_scalar(out=Bb[:], in0=Br[:], scalar1=cfr[:, 0:1], scalar2=None, op0=ALU.mult), [kBr, kcfr], [kBb])
                    dve(lambda e, Bb=Bb, Bi=Bi, ncfi=ncfi: e.scalar_tensor_tensor(out=Bb[:], in0=Bi[:], scalar=ncfi[:, 0:1], in1=Bb[:], op0=ALU.mult, op1=ALU.add), [kBi, kncfi, kBb], [kBb])
                else:
                    dve(lambda e, Bb=Bb, Bi=Bi, cfr=cfr: e.tensor_scalar(out=Bb[:], in0=Bi[:], scalar1=cfr[:, 0:1], scalar2=None, op0=ALU.mult), [kBi, kcfr], [kBb])
                    dve(lambda e, Bb=Bb, Br=Br, cfi=cfi: e.scalar_tensor_tensor(out=Bb[:], in0=Br[:], scalar=cfi[:, 0:1], in1=Bb[:], op0=ALU.mult, op1=ALU.add), [kBr, kcfi, kBb], [kBb])
                Z, kZ = new([128, 128], BF16)
                P.op('pool', lambda e, Z=Z: e.memset(Z[:], 0.0), writes=[kZ])
                dve(lambda e, Z=Z, Bb=Bb, col0=col0: e.tensor_copy(out=Z[0:64, col0:col0 + 16], in_=Bb[0:64, :]), [kBb, kZ], [kZ])
                dve(lambda e, Z=Z, Bb=Bb, col0=col0: e.tensor_copy(out=Z[64:128, col0 + 16:col0 + 32], in_=Bb[64:128, :]), [kBb, kZ], [kZ])
                LB, kLB = new([128, 128], BF16)
                pz = tmp['pst'][0]
                P.op('pe', lambda e, Z=Z, pz=pz: e.transpose(out=pz[:, 0:128], in_=Z[:], identity=c.ident[:]), reads=[kZ, 'ident'], writes=[('pst0', 0)])
                P.op('act', lambda e, LB=LB, pz=pz: e.activation(out=LB[:], in_=pz[:, 0:128], func=AF.Copy), reads=[('pst0', 0)], writes=[kLB])
                q.LB.append((LB, kLB))
            q.LC = []
            for part, src in ((0, cre), (1, cim)):
                Cf, kCf = new([128, 32])
                P.op('pool', lambda e, Cf=Cf: e.memset(Cf[:], 0.0), writes=[kCf])
                P.dma(Cf[0:64, 0:16], src[g0].rearrange("p n -> n p"), reads=[kCf], writes=[(kCf, 0)], allow_slow_non_contiguous=True)
                P.dma(Cf[64:128, 16:32], src[g0 + 1].rearrange("p n -> n p"), reads=[kCf], writes=[(kCf, 1)], allow_slow_non_contiguous=True)
                LC, kLC = new([128, 32], BF16)
                P.op('act', lambda e, LC=LC, Cf=Cf, part=part: e.activation(out=LC[:], in_=Cf[:], func=AF.Copy, scale=(1.0 if part == 0 else -1.0)),
                     reads=[kCf, (kCf, 0), (kCf, 1)], writes=[kLC])
                q.LC.append((LC, kLC))
            LD, kLD = new([128, 32], BF16)
            dch, kdch = (dch0, kd0) if kt == 0 else (dch1, kd1)
            K = 128 if kt == 0 else 64
            dve(lambda e, LD=LD, dch=dch, K=K, col0=col0: e.tensor_scalar(out=LD[0:K, :], in0=c.ident[0:K, col0:col0 + 32], scalar1=dch[0:K, 0:1], scalar2=None, op0=ALU.mult),
                ['ident', kdch], [kLD])
            q.LD, q.kLD, q.K, q.kt = LD, kLD, K, kt
            ang, kang = new([128, T])
            dve(lambda e, ang=ang, th=th: e.tensor_scalar(out=ang[:], in0=iof[:], scalar1=th[:, 0:1], scalar2=None, op0=ALU.mult), [kio, kth], [kang])
            q.st, q.kst, q.ct, q.kct = sincos(ang, kang, T)
            q.rt, q.krt = new([128, T])
            dve(lambda e, q=q, rho=rho: e.tensor_scalar(out=q.rt[:], in0=ones_t[:], scalar1=rho[:, 0:1], scalar2=None, op0=ALU.mult), [kones, krho], [q.krt])
            angT, kangT = new([128, 1])
            dve(lambda e, angT=angT, th=th: e.tensor_scalar(out=angT[:], in0=th[:], scalar1=float(T), scalar2=None, op0=ALU.mult), [kth], [kangT])
            q.sT, q.ksT, q.cT, q.kcT = sincos(angT, kangT, 1)
            q.nsT, q.knsT = new([128, 1])
            dve(lambda e, q=q: e.tensor_scalar(out=q.nsT[:], in0=q.sT[:], scalar1=-1.0, scalar2=None, op0=ALU.mult), [q.ksT], [q.knsT])
            q.wrc, q.kwrc = new([128, 1])
            q.wic, q.kwic = new([128, 1])
            P.op('pool', lambda e, q=q: e.memset(q.wrc[:], 0.0), writes=[q.kwrc])
            P.op('pool', lambda e, q=q: e.memset(q.wic[:], 0.0), writes=[q.kwic])
            q.tc_, q.ktc = new([128, 1])
            prm.append(q)

        xtt = [P.sb([128, D], F32) for _ in range(2)]
        hT = P.sb([128, 8, T], BF16, "hT")
        uT = [P.sb([128, T], BF16, "uT0"), P.sb([128, T], BF16, "uT1")]
        psu = P.ps([128, T], F32)
        psb = [[P.ps([128, T], F32) for _ in range(2)] for _ in range(2)]
        psy = P.ps([128, T], F32)
        W = []
        for b in range(2):
            w = Ctx()
            for nm in ('bur', 'bui', 'a1', 'a2', 'a3', 'a4', 'inr', 'ini', 'wr', 'wi'):
                setattr(w, nm, P.sb([128, T], F32))
            w.sr = P.sb([128, T], BF16)
            w.si = P.sb([128, T], BF16)
            w.ys = P.sb([32, T], F32)
            W.append(w)
        cnt = dict(x=0, p=0)
        for f in range(nfr):
            t0 = f * T
            for s in range(4):
                bx = cnt['x'] % 2
                cnt['x'] += 1
                P.dma(xtt[bx][:], xb[t0 + s * 128:t0 + (s + 1) * 128, :], writes=[f'xtt{bx}'])
                norm_T(P, c, xtt[bx][:], f'xtt{bx}', 128, gain, 'gain', hT, ('hT', s), s * 128, tmp)
            hkeys = [('hT', s) for s in range(4)]
            for kt, (lo, hi) in enumerate(((0, 128), (128, 192))):
                n = hi - lo
                for k in range(8):
                    P.op('pe', lambda e, k=k, lo=lo, hi=hi, n=n: e.matmul(psu[0:n, :], lhsT=win_b[:, k, lo:hi], rhs=hT[:, k, :], start=(k == 0), stop=(k == 7)),
                         reads=['win'] + hkeys, writes=['psu'])
                P.op('act', lambda e, kt=kt, n=n: e.activation(out=uT[kt][0:n, :], in_=psu[0:n, :], func=AF.Copy), reads=['psu'], writes=[f'uT{kt}'])
            for pr in range(6):
                q = prm[pr]
                b = cnt['p'] % 2
                cnt['p'] += 1
                w = W[b]
                K, kt = q.K, q.kt
                kk = lambda nm: f'{nm}{b}'
                for part, dst in ((0, w.bur), (1, w.bui)):
                    LB, kLB = q.LB[part]
                    pb = psb[b][part]
                    P.op('pe', lambda e, LB=LB, pb=pb, K=K, kt=kt: e.matmul(pb[:], lhsT=LB[0:K, :], rhs=uT[kt][0:K, :], start=True, stop=True),
                         reads=[kLB, f'uT{kt}'], writes=[f'psb{b}{part}'])
                    P.op('act', lambda e, dst=dst, pb=pb: e.activation(out=dst[:], in_=pb[:], func=AF.Copy), reads=[f'psb{b}{part}'], writes=[kk('bur' if part == 0 else 'bui')])
                P.op('pool', lambda e, w=w, q=q: e.tensor_tensor(out=w.a1[:], in0=q.ct[:], in1=w.bur[:], op=ALU.mult), reads=[q.kct, kk('bur')], writes=[kk('a1')])
                P.op('pool', lambda e, w=w, q=q: e.tensor_tensor(out=w.a2[:], in0=q.st[:], in1=w.bui[:], op=ALU.mult), reads=[q.kst, kk('bui')], writes=[kk('a2')])
                dve(lambda e, w=w: e.tensor_tensor(out=w.inr[:], in0=w.a1[:], in1=w.a2[:], op=ALU.add), [kk('a1'), kk('a2')], [kk('inr')])
                P.op('pool', lambda e, w=w, q=q: e.tensor_tensor(out=w.a3[:], in0=q.ct[:], in1=w.bui[:], op=ALU.mult), reads=[q.kct, kk('bui')], writes=[kk('a3')])
                P.op('pool', lambda e, w=w, q=q: e.tensor_tensor(out=w.a4[:], in0=q.st[:], in1=w.bur[:], op=ALU.mult), reads=[q.kst, kk('bur')], writes=[kk('a4')])
                dve(lambda e, w=w: e.tensor_tensor(out=w.ini[:], in0=w.a3[:], in1=w.a4[:], op=ALU.subtract), [kk('a3'), kk('a4')], [kk('ini')])
                dve(lambda e, w=w, q=q: e.tensor_tensor_scan(out=w.wr[:], data0=q.rt[:], data1=w.inr[:], initial=q.wrc[:, 0:1], op0=ALU.mult, op1=ALU.add),
                    [q.krt, kk('inr'), q.kwrc], [kk('wr')])
                dve(lambda e, w=w, q=q: e.tensor_tensor_scan(out=w.wi[:], data0=q.rt[:], data1=w.ini[:], initial=q.wic[:, 0:1], op0=ALU.mult, op1=ALU.add),
                    [q.krt, kk('ini'), q.kwic], [kk('wi')])
                dve(lambda e, w=w, q=q: e.tensor_tensor(out=q.tc_[:], in0=w.wr[:, T - 1:T], in1=q.cT[:], op=ALU.mult), [kk('wr'), q.kcT], [q.ktc])
                dve(lambda e, w=w, q=q: e.scalar_tensor_tensor(out=q.wrc[:], in0=w.wi[:, T - 1:T], scalar=q.nsT[:, 0:1], in1=q.tc_[:], op0=ALU.mult, op1=ALU.add),
                    [kk('wi'), q.knsT, q.ktc], [q.kwrc])
                dve(lambda e, w=w, q=q: e.tensor_tensor(out=q.tc_[:], in0=w.wi[:, T - 1:T], in1=q.cT[:], op=ALU.mult), [kk('wi'), q.kcT], [q.ktc])
                dve(lambda e, w=w, q=q: e.scalar_tensor_tensor(out=q.wic[:], in0=w.wr[:, T - 1:T], scalar=q.sT[:, 0:1], in1=q.tc_[:], op0=ALU.mult, op1=ALU.add),
                    [kk('wr'), q.ksT, q.ktc], [q.kwic])
                P.op('pool', lambda e, w=w, q=q: e.tensor_tensor(out=w.a1[:], in0=q.ct[:], in1=w.wr[:], op=ALU.mult), reads=[q.kct, kk('wr')], writes=[kk('a1')])
                P.op('pool', lambda e, w=w, q=q: e.tensor_tensor(out=w.a2[:], in0=q.st[:], in1=w.wi[:], op=ALU.mult), reads=[q.kst, kk('wi')], writes=[kk('a2')])
                dve(lambda e, w=w: e.tensor_tensor(out=w.sr[:], in0=w.a1[:], in1=w.a2[:], op=ALU.subtract), [kk('a1'), kk('a2')], [kk('sr')])
                P.op('pool', lambda e, w=w, q=q: e.tensor_tensor(out=w.a3[:], in0=q.st[:], in1=w.wr[:], op=ALU.mult), reads=[q.kst, kk('wr')], writes=[kk('a3')])
                P.op('pool', lambda e, w=w, q=q: e.tensor_tensor(out=w.a4[:], in0=q.ct[:], in1=w.wi[:], op=ALU.mult), reads=[q.kct, kk('wi')], writes=[kk('a4')])
                dve(lambda e, w=w: e.tensor_tensor(out=w.si[:], in0=w.a3[:], in1=w.a4[:], op=ALU.add), [kk('a3'), kk('a4')], [kk('si')])
                P.op('pe', lambda e, w=w, q=q: e.matmul(psy[0:32, :], lhsT=q.LC[0][0][:], rhs=w.sr[:], start=True, stop=False), reads=[q.LC[0][1], kk('sr')], writes=['psy'])
                P.op('pe', lambda e, w=w, q=q: e.matmul(psy[0:32, :], lhsT=q.LC[1][0][:], rhs=w.si[:], start=False, stop=False), reads=[q.LC[1][1], kk('si')], writes=['psy'])
                P.op('pe', lambda e, q=q, K=K, kt=kt: e.matmul(psy[0:32, :], lhsT=q.LD[0:K, :], rhs=uT[kt][0:K, :], start=False, stop=True), reads=[q.kLD, f'uT{kt}'], writes=['psy'])
                P.op('act', lambda e, w=w: e.activation(out=w.ys[:], in_=psy[0:32, :], func=AF.Copy), reads=['psy'], writes=[kk('ys')])
                P.dma(yT[32 * pr:32 * pr + 32, t0:t0 + T], w.ys[:], reads=[kk('ys')], final=True)
        P.emit()
        print("s5 ops", {e: len(P.ops[e]) for e in P.ENGS}, "waits", P.nwaits)
    return nc


LN1E4_32 = math.log(10000.0) / 32.0


def make_sincos(P, new):
    sc_tmp = {}

    def sincos(ang, kang, n):
        s_, ks = new([128, n])
        c_, kc = new([128, n])
        for (o, ko, sh) in ((s_, ks, 0.0), (c_, kc, 0.5 * PI)):
            if n not in sc_tmp:
                sc_tmp[n] = (new([128, n]), new([128, n], I32), new([128, n]))
            (a_, ka), (ki, kki), (kf, kkf) = sc_tmp[n]
            d = lambda fn, r, w: P.op('dve', fn, reads=r, writes=w)
            d(lambda e, a_=a_, sh=sh: e.tensor_scalar(out=a_[:], in0=ang[:], scalar1=sh, scalar2=None, op0=ALU.add), [kang], [ka])
            d(lambda e, a_=a_, kf=kf: e.tensor_scalar(out=kf[:], in0=a_[:], scalar1=1.0 / (2 * PI), scalar2=None, op0=ALU.mult), [ka], [kkf])
            d(lambda e, ki=ki, kf=kf: e.tensor_copy(out=ki[:], in_=kf[:]), [kkf], [kki])
            d(lambda e, ki=ki, kf=kf: e.tensor_copy(out=kf[:], in_=ki[:]), [kki], [kkf])
            d(lambda e, o=o, kf=kf, a_=a_: e.scalar_tensor_tensor(out=o[:], in0=kf[:], scalar=-2 * PI, in1=a_[:], op0=ALU.mult, op1=ALU.add), [kkf, ka], [ko])
            d(lambda e, o=o, kf=kf: e.tensor_scalar(out=kf[:], in0=o[:], scalar1=PI, scalar2=2 * PI, op0=ALU.is_gt, op1=ALU.mult), [ko], [kkf])
            d(lambda e, o=o, kf=kf: e.tensor_tensor(out=o[:], in0=o[:], in1=kf[:], op=ALU.subtract), [ko, kkf], [ko])
            d(lambda e, o=o, kf=kf: e.tensor_scalar(out=kf[:], in0=o[:], scalar1=-PI, scalar2=2 * PI, op0=ALU.is_lt, op1=ALU.mult), [ko], [kkf])
            d(lambda e, o=o, kf=kf: e.tensor_tensor(out=o[:], in0=o[:], in1=kf[:], op=ALU.add), [ko, kkf], [ko])
            P.op('act', lambda e, o=o: e.activation(out=o[:], in_=o[:], func=AF.Sin), reads=[ko], writes=[ko])
        return s_, ks, c_, kc
    return sincos


def build_qkv(NT):
    nc = bass.Bass("TRN2", target_bir_lowering=False)
    x = nc.dram_tensor("x", [NT, D], F32, kind="ExternalInput").ap()
    ln = nc.dram_tensor("ln", [D], F32, kind="ExternalInput").ap()
    kvn = nc.dram_tensor("kvn", [D], F32, kind="ExternalInput").ap()
    wd = {nm: nc.dram_tensor(nm, [D, SW], F32, kind="ExternalInput").ap() for nm in ("wqa", "wqb", "wka", "wkb", "wv")}
    pos = nc.dram_tensor("pos", [NT], F32, kind="ExternalInput").ap()
    QT = nc.dram_tensor("QT", [SW, NT], F32, kind="ExternalOutput").ap()
    KT = nc.dram_tensor("KT", [SW, NT], F32, kind="ExternalOutput").ap()
    V = nc.dram_tensor("V", [NT, SW], F32, kind="ExternalOutput").ap()
    KM = nc.dram_tensor("KM", [SW, NT // 256], F32, kind="ExternalOutput").ap()
    T = 512
    with contextlib.ExitStack() as st:
        P = Prog(nc, st)
        c = setup_common(P)
        tmp = make_norm_tmp(P)
        gq = load_bcast(P, ln, D, "gq")
        gk = load_bcast(P, kvn, D, "gk")
        wb = {nm: load_w_bf16(P, wd[nm].rearrange("(k p) n -> p k n", p=128), [128, 8, SW], nm) for nm in wd}
        uid = [0]

        def new(shape, dt=F32):
            uid[0] += 1
            nm = f"t{uid[0]}"
            return P.sb(shape, dt, nm), nm
        sincos = make_sincos(P, new)
        dve = lambda fn, r, w: P.op('dve', fn, reads=r, writes=w)
        pi_, kpi = new([128, 1], I32)
        pj, kpj = new([128, 1], I32)
        pf, kpf = new([128, 1])
        invf, kinv = new([128, 1])
        sgn, ksgn = new([128, 1])
        P.op('pool', lambda e: e.iota(pi_[:], pattern=[[0, 1]], base=0, channel_multiplier=1), writes=[kpi])
        dve(lambda e: e.tensor_single_scalar(out=pj[:], in_=pi_[:], scalar=31, op=ALU.bitwise_and), [kpi], [kpj])
        dve(lambda e: e.tensor_copy(out=pf[:], in_=pj[:]), [kpj], [kpf])
        P.op('act', lambda e: e.activation(out=invf[:], in_=pf[:], func=AF.Exp, scale=-LN1E4_32), reads=[kpf], writes=[kinv])
        dve(lambda e: e.tensor_single_scalar(out=pj[:], in_=pi_[:], scalar=32, op=ALU.bitwise_and), [kpi, kpf], [kpj])
        dve(lambda e: e.tensor_copy(out=sgn[:], in_=pj[:]), [kpj], [ksgn])
        dve(lambda e: e.tensor_scalar(out=sgn[:], in0=sgn[:], scalar1=1.0 / 16.0, scalar2=-1.0, op0=ALU.mult, op1=ALU.add), [ksgn], [ksgn])
        posb, kposb = new([128, T])
        ang, kang = new([128, T])
        xtt = [P.sb([128, D], F32) for _ in range(2)]
        hq = P.sb([128, 8, T], BF16, "hq")
        hk = P.sb([128, 8, T], BF16, "hk")
        psA = [P.ps([128, T], F32) for _ in range(2)]
        psB = [P.ps([128, T], F32) for _ in range(2)]
        psV = [P.ps([128, T], F32) for _ in range(2)]
        ta = [P.sb([128, T], F32) for _ in range(2)]
        tb = [P.sb([128, T], F32) for _ in range(2)]
        kms = [P.sb([128, 2], F32) for _ in range(2)]
        vo = [P.sb([128, SW], F32) for _ in range(2)]
        cnt = dict(x=0, a=0, v=0)
        for ti in range(NT // T):
            t0 = ti * T
            P.dma(posb[:], pos[t0:t0 + T].partition_broadcast(128), writes=[kposb])
            dve(lambda e: e.tensor_scalar(out=ang[:], in0=posb[:], scalar1=invf[:, 0:1], scalar2=None, op0=ALU.mult), [kposb, kinv], [kang])
            sn, ksn, cs, kcs = sincos(ang, kang, T)
            dve(lambda e, sn=sn: e.tensor_scalar(out=sn[:], in0=sn[:], scalar1=sgn[:, 0:1], scalar2=None, op0=ALU.mult), [ksn, ksgn], [ksn])
            for s in range(4):
                bx = cnt['x'] % 2
                cnt['x'] += 1
                P.dma(xtt[bx][:], x[t0 + s * 128:t0 + (s + 1) * 128, :], writes=[f'xtt{bx}'])
                norm_T(P, c, xtt[bx][:], f'xtt{bx}', 128, gq, 'gq', hq, ('hq', s), s * 128, tmp)
                norm_T(P, c, xtt[bx][:], f'xtt{bx}', 128, gk, 'gk', hk, ('hk', s), s * 128, tmp)
            for (hh, hn, wa, wb_, out, scale, is_k) in ((hq, 'hq', 'wqa', 'wqb', QT, 0.125, False), (hk, 'hk', 'wka', 'wkb', KT, 1.0, True)):
                hkeys = [(hn, s) for s in range(4)]
                for ch in range(6):
                    b = cnt['a'] % 2
                    cnt['a'] += 1
                    for (ps, w_, pn) in ((psA[b], wa, 'psA'), (psB[b], wb_, 'psB')):
                        for k in range(8):
                            P.op('pe', lambda e, ps=ps, w_=w_, k=k, ch=ch, hh=hh: e.matmul(ps[:], lhsT=wb[w_][:, k, ch * 128:(ch + 1) * 128], rhs=hh[:, k, :], start=(k == 0), stop=(k == 7)),
                                 reads=[w_] + hkeys, writes=[f'{pn}{b}'])
                    dve(lambda e, b=b, cs=cs: e.tensor_tensor(out=ta[b][:], in0=psA[b][:], in1=cs[:], op=ALU.mult), [f'psA{b}', kcs], [f'ta{b}'])
                    dve(lambda e, b=b, sn=sn: e.tensor_tensor(out=tb[b][:], in0=psB[b][:], in1=sn[:], op=ALU.mult), [f'psB{b}', ksn], [f'tb{b}'])
                    dve(lambda e, b=b: e.tensor_tensor(out=ta[b][:], in0=ta[b][:], in1=tb[b][:], op=ALU.add), [f'ta{b}', f'tb{b}'], [f'ta{b}'])
                    if scale != 1.0:
                        P.op('act', lambda e, b=b, scale=scale: e.activation(out=ta[b][:], in_=ta[b][:], func=AF.Copy, scale=scale), reads=[f'ta{b}'], writes=[f'ta{b}'])
                    P.dma(out[ch * 128:(ch + 1) * 128, t0:t0 + T], ta[b][:], reads=[f'ta{b}'], final=True)
                    if is_k:
                        dve(lambda e, b=b: e.tensor_reduce(out=kms[b][:], in_=ta[b][:].rearrange("p (n k) -> p n k", k=256), axis=AX.X, op=ALU.add), [f'ta{b}'], [f'kms{b}'])
                        P.dma(KM[ch * 128:(ch + 1) * 128, ti * 2:ti * 2 + 2], kms[b][:], reads=[f'kms{b}'], final=True)
            hkeys = [('hk', s) for s in range(4)]
            for s in range(4):
                vb = cnt['v'] % 2
                cnt['v'] += 1
                for (lo, hi, half) in ((0, 512, 0), (512, 768, 1)):
                    for k in range(8):
                        P.op('pe', lambda e, k=k, s=s, lo=lo, hi=hi, half=half: e.matmul(psV[half][:, 0:hi - lo], lhsT=hk[:, k, s * 128:(s + 1) * 128], rhs=wb['wv'][:, k, lo:hi], start=(k == 0), stop=(k == 7)),
                             reads=['wv'] + hkeys, writes=[f'psV{half}'])
                    P.op('act', lambda e, vb=vb, lo=lo, hi=hi, half=half: e.activation(out=vo[vb][:, lo:hi], in_=psV[half][:, 0:hi - lo], func=AF.Copy), reads=[f'psV{half}'], writes=[(f'vo{vb}', half)])
                P.dma(V[t0 + s * 128:t0 + (s + 1) * 128, :], vo[vb][:], reads=[(f'vo{vb}', 0), (f'vo{vb}', 1)], final=True)
        P.emit()
        print("qkv ops", {e: len(P.ops[e]) for e in P.ENGS}, "waits", P.nwaits)
    return nc


def build_moba(NB, NH=6):
    nc = bass.Bass("TRN2", target_bir_lowering=False)
    NK = NB * 256
    QT = nc.dram_tensor("QT", [NH * 64, NB * 128], F32, kind="ExternalInput").ap()
    KT = nc.dram_tensor("KT", [NH * 64, NK], F32, kind="ExternalInput").ap()
    V = nc.dram_tensor("V", [NK, NH * 64], F32, kind="ExternalInput").ap()
    KM = nc.dram_tensor("KM", [NH * 64, NB], F32, kind="ExternalInput").ap()
    cmask = nc.dram_tensor("cmask", [128, 256], F32, kind="ExternalInput").ap()
    O = nc.dram_tensor("O", [NB * 128, NH * 64], F32, kind="ExternalOutput").ap()
    NBP = max(NB, 8)
    with contextlib.ExitStack() as st:
        P = Prog(nc, st)
        c = setup_common(P)
        dve = lambda fn, r, w: P.op('dve', fn, reads=r, writes=w)
        cm = P.sb([128, 256], F32, "cm")
        P.dma(cm[:], cmask, writes=['cm'])
        qb = P.sb([64, NB * 128], BF16, "qb")
        kb = P.sb([64, NK], BF16, "kb")
        vb = P.sb([128, NB * 2, 64], BF16, "vb")
        kmb = P.sb([64, NB], BF16, "kmb")
        stg = [P.sb([128, 2048], F32) for _ in range(2)]
        psS = [P.ps([128, 512], F32) for _ in range(2)]
        psT = [P.ps([128, 512], BF16) for _ in range(2)]
        psO = P.ps([128, 64], F32)
        psG = P.ps([128, NBP], F32)
        gt = P.sb([128, NBP], F32, "gt")
        bias = P.sb([128, NBP], F32, "bias")
        mx8 = P.sb([128, 8], F32, "mx8")
        lsum = P.sb([128, NBP + 1], F32, "lsum")
        lt = P.sb([128, 1], F32, "lt")
        Pb = [P.sb([128, 512], BF16) for _ in range(2)]
        PTb = [P.sb([128, 512], BF16) for _ in range(2)]
        so = [P.sb([128, 256], F32) for _ in range(2)]
        Ot = [P.sb([128, NH * 64], F32) for _ in range(2)]
        cnt = dict(s=0, g=0)

        def cast_in(dst_fn, src_fn, rows, total, wkey):
            CH = 2048
            for o in range(0, total, CH):
                n = min(CH, total - o)
                b = cnt['s'] % 2
                cnt['s'] += 1
                P.dma(stg[b][0:rows, 0:n], src_fn(o, n), writes=[f'stg{b}'])
                P.op('pool', lambda e, b=b, o=o, n=n: e.tensor_copy(out=dst_fn(o, n), in_=stg[b][0:rows, 0:n]), reads=[f'stg{b}'], writes=[wkey])

        for h in range(NH):
            r0 = h * 64
            cast_in(lambda o, n: qb[:, o:o + n], lambda o, n: QT[r0:r0 + 64, o:o + n], 64, NB * 128, 'qb')
            cast_in(lambda o, n: kb[:, o:o + n], lambda o, n: KT[r0:r0 + 64, o:o + n], 64, NK, 'kb')
            cast_in(lambda o, n: kmb[:, o:o + n], lambda o, n: KM[r0:r0 + 64, o:o + n], 64, NB, 'kmb')
            vsrc = V[:, r0:r0 + 64].rearrange("(t p) d -> p t d", p=128)
            TCH = 32
            for o in range(0, NB * 2, TCH):
                n = min(TCH, NB * 2 - o)
                b = cnt['s'] % 2
                cnt['s'] += 1
                sv = stg[b][:, 0:n * 64].rearrange("p (t d) -> p t d", d=64)
                P.dma(sv, vsrc[:, o:o + n, :], writes=[f'stg{b}'])
                P.op('pool', lambda e, sv=sv, o=o, n=n: e.tensor_copy(out=vb[:, o:o + n, :], in_=sv), reads=[f'stg{b}'], writes=['vb'])
            for n in range(NB):
                ob = n % 2
                qt = qb[:, n * 128:(n + 1) * 128]
                nl = 0
                mm = []
                if n > 3:
                    P.op('pool', lambda e: e.memset(gt[:], -1e30), writes=['gt'])
                    P.op('pe', lambda e, qt=qt, n=n: e.matmul(psG[:, 0:n], lhsT=qt, rhs=kmb[:, 0:n], start=True, stop=True), reads=['qb', 'kmb'], writes=['psG'])
                    P.op('act', lambda e, n=n: e.activation(out=gt[:, 0:n], in_=psG[:, 0:n], func=AF.Copy), reads=['psG', 'gt'], writes=['gt'])
                    dve(lambda e: e.max(out=mx8[:], in_=gt[:]), ['gt'], ['mx8'])
                    dve(lambda e, n=n: e.tensor_scalar(out=bias[:, 0:n], in0=gt[:, 0:n], scalar1=mx8[:, 2:3], scalar2=None, op0=ALU.is_ge), ['gt', 'mx8'], ['bias'])
                    dve(lambda e, n=n: e.tensor_scalar(out=bias[:, 0:n], in0=bias[:, 0:n], scalar1=30000.0, scalar2=-30000.0, op0=ALU.mult, op1=ALU.add), ['bias'], ['bias'])
                elif n > 0:
                    P.op('pool', lambda e: e.memset(bias[:], 0.0), writes=['bias'])
                P.op('pool', lambda e: e.memset(lsum[:], 0.0), writes=['lsum'])
                groups = [(g * 2, min(2, n - g * 2)) for g in range((n + 1) // 2)]
                ntiles_total = sum(nb_ * 2 for (_, nb_) in groups) + 2
                pvc = [0]

                def do_pv(gb_, nb_, kt0):
                    for j in range(nb_ * 2):
                        idx = pvc[0]
                        pvc[0] += 1
                        kt = kt0 + j
                        P.op('pe', lambda e, gb_=gb_, j=j: e.transpose(out=psT[gb_][:, j * 128:(j + 1) * 128], in_=Pb[gb_][:, j * 128:(j + 1) * 128], identity=c.ident[:]),
                             reads=[(f'Pb{gb_}', j // 2), 'ident'], writes=[(f'psT{gb_}', j)])
                        if idx % 2 == 0:
                            P.op('dve', lambda e, gb_=gb_, j=j: e.tensor_copy(out=PTb[gb_][:, j * 128:(j + 1) * 128], in_=psT[gb_][:, j * 128:(j + 1) * 128]),
                                 reads=[(f'psT{gb_}', j)], writes=[(f'PTb{gb_}', j)])
                        else:
                            P.op('act', lambda e, gb_=gb_, j=j: e.activation(out=PTb[gb_][:, j * 128:(j + 1) * 128], in_=psT[gb_][:, j * 128:(j + 1) * 128], func=AF.Copy),
                                 reads=[(f'psT{gb_}', j)], writes=[(f'PTb{gb_}', j)])
                        P.op('pe', lambda e, gb_=gb_, j=j, kt=kt, idx=idx: e.matmul(psO[:], lhsT=PTb[gb_][:, j * 128:(j + 1) * 128], rhs=vb[:, kt, :], start=(idx == 0), stop=(idx == ntiles_total - 1)),
                             reads=[(f'PTb{gb_}', j), 'vb'], writes=['psO'])

                for (b0, nb_) in groups:
                    gb_ = cnt['g'] % 2
                    cnt['g'] += 1
                    P.op('pe', lambda e, qt=qt, b0=b0, nb_=nb_, gb_=gb_: e.matmul(psS[gb_][:, 0:nb_ * 256], lhsT=qt, rhs=kb[:, b0 * 256:(b0 + nb_) * 256], start=True, stop=True),
                         reads=['qb', 'kb'], writes=[f'psS{gb_}'])
                    for a in range(nb_):
                        P.op('act', lambda e, gb_=gb_, a=a, b0=b0: e.activation(out=Pb[gb_][:, a * 256:(a + 1) * 256], in_=psS[gb_][:, a * 256:(a + 1) * 256], func=AF.Exp,
                                                                          bias=bias[:, b0 + a:b0 + a + 1], accum_out=lsum[:, b0 + a:b0 + a + 1]),
                             reads=[f'psS{gb_}', 'bias', 'lsum'], writes=[(f'Pb{gb_}', a), 'lsum'])
                    do_pv(gb_, nb_, b0 * 2)
                gb_ = cnt['g'] % 2
                cnt['g'] += 1
                P.op('pe', lambda e, qt=qt, n=n, gb_=gb_: e.matmul(psS[gb_][:, 0:256], lhsT=qt, rhs=kb[:, n * 256:(n + 1) * 256], start=True, stop=True), reads=['qb', 'kb'], writes=[f'psS{gb_}'])
                dve(lambda e, gb_=gb_: e.tensor_tensor(out=so[gb_][:], in0=psS[gb_][:, 0:256], in1=cm[:], op=ALU.add), [f'psS{gb_}', 'cm'], [f'so{gb_}'])
                P.op('act', lambda e, gb_=gb_, n=n: e.activation(out=Pb[gb_][:, 0:256], in_=so[gb_][:], func=AF.Exp, accum_out=lsum[:, NBP:NBP + 1]),
                     reads=[f'so{gb_}', 'lsum'], writes=[(f'Pb{gb_}', 0), 'lsum'])
                do_pv(gb_, 1, n * 2)
                dve(lambda e: e.tensor_reduce(out=lt[:], in_=lsum[:], axis=AX.X, op=ALU.add), ['lsum'], ['lt'])
                dve(lambda e: e.reciprocal(out=lt[:], in_=lt[:]), ['lt'], ['lt'])
                P.op('act', lambda e, ob=ob, h=h: e.activation(out=Ot[ob][:, h * 64:(h + 1) * 64], in_=psO[:], func=AF.Copy, scale=lt[:, 0:1]), reads=['psO', 'lt'], writes=[(f'Ot{ob}', h)])
                P.dma(O[n * 128:(n + 1) * 128, h * 64:(h + 1) * 64], Ot[ob][:, h * 64:(h + 1) * 64], reads=[(f'Ot{ob}', h)], final=True)
        P.emit()
        print("moba ops", {e: len(P.ops[e]) for e in P.ENGS}, "waits", P.nwaits)
    return nc


def _run(nc, maps):
    return run_bass_kernel_spmd(nc, maps, core_ids=list(range(8))).results


def _c(a):
    return np.ascontiguousarray(a, dtype=np.float32)


def _perm(w):
    w = w.reshape(1024, 12, 2, 32)[:, :, ::-1, :]
    return _c(w.reshape(1024, 768))


def kernel(**inp):
    inp = {k: np.asarray(v, dtype=np.float32) for k, v in inp.items()}
    x = inp['x']
    B, S, NT = 2, 16384, 4096
    maps = []
    for i in range(8):
        b, gs = i // 4, i % 4
        G0 = 12 * gs
        maps.append(dict(xb=_c(x[b]), ln=_c(inp['ln_mix'][0]), win=_c(inp['w_in'][0][:, 192 * gs:192 * gs + 192]),
                         lre=_c(inp['s5_lambda_re'][0, G0:G0 + 12]), lim=_c(inp['s5_lambda_im'][0, G0:G0 + 12]), lst=_c(inp['s5_log_step'][0, G0:G0 + 12]),
                         bre=_c(inp['s5_b_re'][0, G0:G0 + 12]), bim=_c(inp['s5_b_im'][0, G0:G0 + 12]), cre=_c(inp['s5_c_re'][0, G0:G0 + 12]),
                         cim=_c(inp['s5_c_im'][0, G0:G0 + 12]), dd=_c(inp['s5_d'][0, G0:G0 + 12])))
    res = _run(build_s5(S), maps)
    yT = [np.concatenate([res[b * 4 + gs]['yT'] for gs in range(4)], 0) for b in range(B)]
    nc_tail_glu = build_tail(NT, True)
    maps = []
    for i in range(8):
        b, r = i // 4, i % 4
        sl = slice(r * NT, (r + 1) * NT)
        maps.append(dict(x=_c(x[b, sl]), sT=_c(yT[b][:, sl]), mem=_c(inp['mem'][b]), ln=_c(inp['ln_mix'][0]), wq=_c(inp['w_in'][0][:, 768:]),
                         mng=_c(inp['mem_norm'][0]), wmkv=_c(inp['w_mem_kv'][0]), wglu=_c(inp['s5_w_glu'][0]), wout=_c(inp['w_out'][0])))
    res = _run(nc_tail_glu, maps)
    xmid = np.stack([np.concatenate([res[b * 4 + r]['y'] for r in range(4)], 0) for b in range(B)])

    def ffn(xm, l, final):
        maps = []
        for i in range(8):
            b, r = i // 4, i % 4
            sl = slice(r * NT, (r + 1) * NT)
            xh = np.zeros((2, 1024), np.float32) if r == 0 else xm[b, r * NT - 2:r * NT]
            maps.append(dict(xm=_c(xm[b, sl]), xh=_c(xh), ln=_c(inp['ln_ffn'][l]), wup=_c(inp['w_up'][l]), cw=_c(inp['conv_w'][l]), cb=_c(inp['conv_b'][l]),
                             wdown=_c(inp['w_down'][l]), fng=_c(inp['final_norm'])))
        res = _run(build_ffn(NT, final), maps)
        return np.stack([np.concatenate([res[b * 4 + r]['y'] for r in range(4)], 0) for b in range(B)])

    x1 = ffn(xmid, 0, False)
    wq = _c(inp['w_in'][1][:, :768]); wk = _c(inp['w_kv'][:, :768]); wv = _c(inp['w_kv'][:, 768:])
    wqb, wkb = _perm(wq), _perm(wk)
    maps = []
    for i in range(8):
        b, r = i // 4, i % 4
        sl = slice(r * NT, (r + 1) * NT)
        maps.append(dict(x=_c(x1[b, sl]), ln=_c(inp['ln_mix'][1]), kvn=_c(inp['kv_norm']), wqa=wq, wqb=wqb, wka=wk, wkb=wkb, wv=wv,
                         pos=np.arange(r * NT, (r + 1) * NT, dtype=np.float32)))
    res = _run(build_qkv(NT), maps)
    QT = [np.concatenate([res[b * 4 + r]['QT'] for r in range(4)], 1) for b in range(B)]
    KT = [np.concatenate([res[b * 4 + r]['KT'] for r in range(4)], 1) for b in range(B)]
    V = [np.concatenate([res[b * 4 + r]['V'] for r in range(4)], 0) for b in range(B)]
    KM = [np.concatenate([res[b * 4 + r]['KM'] for r in range(4)], 1) for b in range(B)]
    NB = S // 256
    maps = []
    colsj = [np.concatenate([np.arange(n * 256 + j * 128, n * 256 + j * 128 + 128) for n in range(NB)]) for j in range(2)]
    for i in range(8):
        b, j, hh = i // 4, (i % 4) % 2, (i % 4) // 2
        rows = slice(hh * 384, hh * 384 + 384)
        cm = np.where(np.arange(256)[None, :] <= (128 * j + np.arange(128))[:, None], 0.0, -30000.0).astype(np.float32)
        maps.append(dict(QT=_c(QT[b][rows][:, colsj[j]]), KT=_c(KT[b][rows]), V=_c(V[b][:, rows]), KM=_c(KM[b][rows]), cmask=cm))
    res = _run(build_moba(NB, 6), maps)
    attn = np.zeros((B, S, 768), np.float32)
    for i in range(8):
        b, j, hh = i // 4, (i % 4) % 2, (i % 4) // 2
        attn[b][colsj[j], hh * 384:hh * 384 + 384] = res[i]['O']
    maps = []
    for i in range(8):
        b, r = i // 4, i % 4
        sl = slice(r * NT, (r + 1) * NT)
        maps.append(dict(x=_c(x1[b, sl]), sT=_c(attn[b, sl].T), mem=_c(inp['mem'][b]), ln=_c(inp['ln_mix'][1]), wq=_c(inp['w_in'][1][:, 768:]),
                         mng=_c(inp['mem_norm'][1]), wmkv=_c(inp['w_mem_kv'][1]), wglu=_c(inp['s5_w_glu'][0]), wout=_c(inp['w_out'][1])))
    res = _run(build_tail(NT, False), maps)
    xmid1 = np.stack([np.concatenate([res[b * 4 + r]['y'] for r in range(4)], 0) for b in range(B)])
    out = ffn(xmid1, 1, True)
    return out.astype(np.float32)
```

```python
import contextlib
import numpy as np
import concourse.bass as bass
import concourse.mybir as mybir
from concourse.bass_utils import run_bass_kernel_spmd

F32 = mybir.dt.float32
BF16 = mybir.dt.bfloat16
AF = mybir.ActivationFunctionType
ALU = mybir.AluOpType
AX = mybir.AxisListType


class Prog:
    ENGS = ['pe', 'act', 'dve', 'pool', 'sp']
    NSLOT = 8

    def __init__(self, nc, stack, sync_same=('act', 'dve', 'pool')):
        self.nc = nc
        self.stack = stack
        self.ops = {e: [] for e in self.ENGS}
        self.lw = {}
        self.rd = {}
        self.sync_same = set(sync_same)
        self.ndma = {e: 0 for e in self.ENGS}
        self._n = 0
        self.final = []

    def sb(self, shape, dt, name=None):
        self._n += 1
        return self.stack.enter_context(self.nc.sbuf_tensor(name or f"sb{self._n}", list(shape), dt))

    def ps(self, shape, dt, name=None):
        self._n += 1
        return self.stack.enter_context(self.nc.psum_tensor(name or f"ps{self._n}", list(shape), dt))

    def op(self, eng, fn, reads=(), writes=(), dma=False):
        idx = len(self.ops[eng])
        deps = set()
        for k in reads:
            if k in self.lw:
                deps.add(self.lw[k])
        for k in writes:
            if k in self.lw:
                deps.add(self.lw[k])
            for r in self.rd.get(k, ()):
                deps.add(r)
        deps.discard((eng, idx))
        rec = dict(fn=fn, deps=deps, needed=False, dma=dma, eng=eng)
        if dma:
            rec['dma_i'] = self.ndma[eng]
            self.ndma[eng] += 1
        self.ops[eng].append(rec)
        for k in writes:
            self.lw[k] = (eng, idx)
            self.rd[k] = []
        for k in reads:
            self.rd.setdefault(k, []).append((eng, idx))
        return (eng, idx)

    def dma(self, out, in_, reads=(), writes=(), eng='sp', final=False, **kw):
        r = self.op(eng, lambda e: e.dma_start(out=out, in_=in_, **kw), reads, writes, dma=True)
        if final:
            self.final.append(r)
        return r

    def emit(self):
        nc = self.nc
        for e in self.ENGS:
            for i, rec in enumerate(self.ops[e]):
                nd = set()
                for (de, di) in rec['deps']:
                    drec = self.ops[de][di]
                    if de == e and not drec['dma']:
                        if rec['dma']:
                            pass
                        elif e not in self.sync_same:
                            continue
                    nd.add((de, di))
                rec['deps'] = nd
                for (de, di) in nd:
                    self.ops[de][di]['needed'] = True
        final = list(self.final)
        for (de, di) in final:
            self.ops[de][di]['needed'] = True
        csem = {e: self.stack.enter_context(nc.semaphore(f"c_{e}")) for e in self.ENGS}
        dsem = {}
        for e in self.ENGS:
            if self.ndma[e]:
                dsem[e] = [self.stack.enter_context(nc.semaphore(f"d_{e}{s}")) for s in range(self.NSLOT)]
        for e in self.ENGS:
            c = 0
            for rec in self.ops[e]:
                if rec['dma']:
                    s = rec['dma_i'] % self.NSLOT
                    rec['sem'] = dsem[e][s]
                    rec['val'] = 16 * (rec['dma_i'] // self.NSLOT + 1)
                else:
                    if rec['needed']:
                        c += 1
                    rec['sem'] = csem[e]
                    rec['val'] = c
        engobj = {'pe': 'tensor', 'act': 'scalar', 'dve': 'vector', 'pool': 'gpsimd', 'sp': 'sync'}
        nwaits = [0]

        def run(e, eng):
            wm = {}

            def wait(sem, val):
                k = id(sem)
                if wm.get(k, 0) >= val:
                    return
                eng.wait_ge(sem, val)
                wm[k] = val
                nwaits[0] += 1

            for rec in self.ops[e]:
                for (de, di) in sorted(rec['deps']):
                    d = self.ops[de][di]
                    wait(d['sem'], d['val'])
                if rec['dma'] and rec['val'] > 16:
                    wait(rec['sem'], rec['val'] - 16)
                ins = rec['fn'](eng)
                if rec['dma']:
                    ins.then_inc(rec['sem'], 16)
                elif rec['needed']:
                    ins.then_inc(rec['sem'], 1)
            if e == 'sp':
                for (de, di) in final:
                    d = self.ops[de][di]
                    wait(d['sem'], d['val'])

        with nc.Block() as block:
            for e in self.ENGS:
                if not self.ops[e] and e != 'sp':
                    continue
                getattr(block, engobj[e])(lambda eng, e=e: run(e, eng))
        self.nwaits = nwaits[0]


D = 1024
DFF = 2816
NCH = DFF // 128
EPS = 1e-6


class Ctx:
    pass


def setup_common(P):
    c = Ctx()
    c.ident = P.sb([128, 128], BF16, "ident")
    P.op('pool', lambda e: e.memset(c.ident[:], 0.0), writes=['ident'])
    P.op('pool', lambda e: e.affine_select(out=c.ident[:], in_=c.ident[:], pattern=[[-1, 128]],
                                           compare_op=ALU.not_equal, fill=1.0, base=0, channel_multiplier=1),
         reads=['ident'], writes=['ident'])
    c.n = 0
    return c


def load_bcast(P, dram_vec, n, name):
    t = P.sb([128, n], F32, name)
    P.dma(t[:], dram_vec.partition_broadcast(128), writes=[name])
    return t


def norm_T(P, c, xt, xkey, np_, gain_bc, gkey, hT, hkey, col0, tmp):
    i = c.n
    c.n += 1
    b = i % 2
    junk, ss, xb, pst = tmp['junk'][b], tmp['ss'][b], tmp['xb'][b], tmp['pst'][b]
    kj, ks, kx, kp = f'junk{b}', f'ss{b}', f'xb{b}', f'pst{b}'
    P.op('act', lambda e: e.activation(out=junk[:np_, :], in_=xt, func=AF.Square, accum_out=ss[:np_, :]),
         reads=[xkey], writes=[kj, ks])
    P.op('dve', lambda e: e.tensor_scalar(out=ss[:np_, :], in0=ss[:np_, :], scalar1=1.0 / D, scalar2=EPS, op0=ALU.mult, op1=ALU.add),
         reads=[ks], writes=[ks])
    P.op('act', lambda e: e.activation(out=ss[:np_, :], in_=ss[:np_, :], func=AF.Sqrt), reads=[ks], writes=[ks])
    P.op('dve', lambda e: e.reciprocal(out=ss[:np_, :], in_=ss[:np_, :]), reads=[ks], writes=[ks])
    P.op('dve', lambda e: e.scalar_tensor_tensor(out=xb[:np_, :], in0=xt, scalar=ss[:np_, 0:1], in1=gain_bc[:np_, :],
                                                 op0=ALU.mult, op1=ALU.mult), reads=[xkey, ks, gkey], writes=[kx])
    for k in range(8):
        P.op('pe', lambda e, k=k: e.transpose(out=pst[:, k * 128:k * 128 + np_], in_=xb[:np_, k * 128:(k + 1) * 128],
                                              identity=c.ident[:np_, :np_]), reads=[kx, 'ident'], writes=[(kp, k)])
    P.op('act', lambda e: e.activation(out=hT[:, :, col0:col0 + np_],
                                       in_=pst[:].rearrange("p (k t) -> p k t", k=8)[:, :, :np_], func=AF.Copy),
         reads=[(kp, k) for k in range(8)], writes=[hkey])
    return ss


def make_norm_tmp(P):
    return dict(junk=[P.sb([128, D], F32) for _ in range(2)], ss=[P.sb([128, 1], F32) for _ in range(2)],
                xb=[P.sb([128, D], BF16) for _ in range(2)], pst=[P.ps([128, 1024], BF16) for _ in range(2)])


def build_ffn(NT, final_norm):
    nc = bass.Bass("TRN2", target_bir_lowering=False)
    xm = nc.dram_tensor("xm", [NT, D], F32, kind="ExternalInput").ap()
    xh = nc.dram_tensor("xh", [2, D], F32, kind="ExternalInput").ap()
    ln = nc.dram_tensor("ln", [D], F32, kind="ExternalInput").ap()
    wup = nc.dram_tensor("wup", [NCH, 128, 8 * 256], F32, kind="ExternalInput").ap()
    cw = nc.dram_tensor("cw", [3, 2 * DFF], F32, kind="ExternalInput").ap()
    cb = nc.dram_tensor("cb", [2 * DFF], F32, kind="ExternalInput").ap()
    wdown = nc.dram_tensor("wdown", [DFF, D], F32, kind="ExternalInput").ap()
    fng = nc.dram_tensor("fng", [D], F32, kind="ExternalInput").ap()
    y = nc.dram_tensor("y", [NT, D], F32, kind="ExternalOutput").ap()
    T = 512
    ntile = NT // T
    with contextlib.ExitStack() as st:
        P = Prog(nc, st)
        c = setup_common(P)
        tmp = make_norm_tmp(P)
        gain = load_bcast(P, ln, D, "gain")
        fgain = load_bcast(P, fng, D, "fgain") if final_norm else None
        cwt = P.sb([128, 3, 2 * NCH], F32, "cwt")
        cbt = P.sb([128, 2 * NCH], F32, "cbt")
        P.dma(cwt[:], cw.rearrange("t (c p) -> p t c", p=128), writes=['cwt'], allow_slow_non_contiguous=True)
        P.dma(cbt[:], cb.rearrange("(c p) -> p c", p=128), writes=['cbt'], allow_slow_non_contiguous=True)
        wd = P.sb([128, NCH, D], BF16, "wd")
        wdst = [P.sb([128, D], F32) for _ in range(2)]
        for j in range(NCH):
            b = j % 2
            P.dma(wdst[b][:], wdown[j * 128:(j + 1) * 128, :], writes=[f'wdst{b}'])
            P.op('pool', lambda e, j=j, b=b: e.tensor_copy(out=wd[:, j, :], in_=wdst[b][:]), reads=[f'wdst{b}'], writes=[('wd', j)])
        xtt = [P.sb([128, D], F32) for _ in range(2)]
        xres = P.sb([128, 4, D], F32, "xres")
        hT = P.sb([128, 8, T], BF16, "hT")
        hTh = P.sb([128, 8, 2], BF16, "hTh")
        Hs = P.sb([128, 2 * NCH, 2], F32, "Hs")
        wst = [P.sb([128, 8, 256], F32) for _ in range(2)]
        wbf = [P.sb([128, 8, 256], BF16) for _ in range(2)]
        Ug = [P.sb([128, T + 2], F32) for _ in range(2)]
        Uv = [P.sb([128, T + 2], F32) for _ in range(2)]
        ag = [P.sb([128, T], F32) for _ in range(2)]
        av = [P.sb([128, T], F32) for _ in range(2)]
        hid = P.sb([128, NCH, T], BF16, "hid")
        psg = [P.ps([128, T], F32) for _ in range(2)]
        psv = [P.ps([128, T], F32) for _ in range(2)]
        pso = [P.ps([128, 512], F32) for _ in range(2)]
        yo = [P.sb([128, D], F32) for _ in range(2)]
        junk2 = P.sb([128, D], F32, "junk2")
        ss2 = [P.sb([128, 1], F32) for _ in range(2)]
        cnt = dict(w=0, o=0, x=0)

        def load_w(j):
            b = cnt['w'] % 2
            cnt['w'] += 1
            P.dma(wst[b][:].rearrange("p k n -> p (k n)"), wup[j], writes=[(f'wst{b}', 0), (f'wst{b}', 1)])
            P.op('pool', lambda e: e.tensor_copy(out=wbf[b][:], in_=wst[b][:]), reads=[(f'wst{b}', 0), (f'wst{b}', 1)], writes=[f'wbf{b}'])
            return b

        b0 = cnt['x'] % 2
        cnt['x'] += 1
        P.dma(xtt[b0][:2, :], xh, writes=[f'xtt{b0}'])
        norm_T(P, c, xtt[b0][:2, :], f'xtt{b0}', 2, gain, 'gain', hTh, 'hTh', 0, tmp)
        for j in range(NCH):
            wb = load_w(j)
            pb = j % 2
            for k in range(8):
                P.op('pe', lambda e, k=k, wb=wb, pb=pb: e.matmul(psg[pb][:, 0:2], lhsT=wbf[wb][:, k, 0:128], rhs=hTh[:, k, :], start=(k == 0), stop=(k == 7)),
                     reads=[f'wbf{wb}', 'hTh'], writes=[f'psg{pb}'])
            for k in range(8):
                P.op('pe', lambda e, k=k, wb=wb, pb=pb: e.matmul(psv[pb][:, 0:2], lhsT=wbf[wb][:, k, 128:256], rhs=hTh[:, k, :], start=(k == 0), stop=(k == 7)),
                     reads=[f'wbf{wb}', 'hTh'], writes=[f'psv{pb}'])
            P.op('act', lambda e, j=j, pb=pb: e.activation(out=Hs[:, j, :], in_=psg[pb][:, 0:2], func=AF.Copy), reads=[f'psg{pb}'], writes=[('Hs', j)])
            P.op('act', lambda e, j=j, pb=pb: e.activation(out=Hs[:, NCH + j, :], in_=psv[pb][:, 0:2], func=AF.Copy), reads=[f'psv{pb}'], writes=[('Hs', NCH + j)])

        for ti in range(ntile):
            t0 = ti * T
            for s in range(4):
                bx = cnt['x'] % 2
                cnt['x'] += 1
                P.dma(xtt[bx][:], xm[t0 + s * 128:t0 + (s + 1) * 128, :], writes=[f'xtt{bx}'])
                P.op('pool', lambda e, s=s, bx=bx: e.tensor_copy(out=xres[:, s, :], in_=xtt[bx][:]), reads=[f'xtt{bx}'], writes=[('xres', s)])
                norm_T(P, c, xtt[bx][:], f'xtt{bx}', 128, gain, 'gain', hT, ('hT', s), s * 128, tmp)
            hkeys = [('hT', s) for s in range(4)]
            for j in range(NCH):
                wb = load_w(j)
                pb = j % 2
                for k in range(8):
                    P.op('pe', lambda e, k=k, wb=wb, pb=pb: e.matmul(psg[pb][:], lhsT=wbf[wb][:, k, 0:128], rhs=hT[:, k, :], start=(k == 0), stop=(k == 7)),
                         reads=[f'wbf{wb}'] + hkeys, writes=[f'psg{pb}'])
                for k in range(8):
                    P.op('pe', lambda e, k=k, wb=wb, pb=pb: e.matmul(psv[pb][:], lhsT=wbf[wb][:, k, 128:256], rhs=hT[:, k, :], start=(k == 0), stop=(k == 7)),
                         reads=[f'wbf{wb}'] + hkeys, writes=[f'psv{pb}'])
                for (U, ps, acc, ch, nm, eng2) in ((Ug[pb], psg[pb], ag[pb], j, 'g', 'dve'), (Uv[pb], psv[pb], av[pb], NCH + j, 'v', 'dve')):
                    uk, pk, ak = f'U{nm}{pb}', f'ps{nm}{pb}', f'a{nm}{pb}'
                    P.op('pool', lambda e, U=U, ch=ch: e.tensor_copy(out=U[:, 0:2], in_=Hs[:, ch, :]), reads=[('Hs', ch)], writes=[(uk, 0)])
                    P.op('act', lambda e, U=U, ps=ps: e.activation(out=U[:, 2:T + 2], in_=ps[:], func=AF.Copy), reads=[pk], writes=[(uk, 1)])
                    P.op('pool', lambda e, U=U, ch=ch: e.tensor_copy(out=Hs[:, ch, :], in_=U[:, T:T + 2]), reads=[(uk, 1)], writes=[('Hs', ch)])
                    P.op('dve', lambda e, U=U, acc=acc, ch=ch: e.tensor_scalar(out=acc[:], in0=U[:, 2:T + 2], scalar1=cwt[:, 2, ch:ch + 1], scalar2=cbt[:, ch:ch + 1],
                                                                             op0=ALU.mult, op1=ALU.add), reads=[(uk, 1), 'cwt', 'cbt'], writes=[ak])
                    P.op(eng2, lambda e, U=U, acc=acc, ch=ch: e.scalar_tensor_tensor(out=acc[:], in0=U[:, 1:T + 1], scalar=cwt[:, 1, ch:ch + 1], in1=acc[:],
                                                                                    op0=ALU.mult, op1=ALU.add), reads=[(uk, 0), (uk, 1), 'cwt', ak], writes=[ak])
                    P.op(eng2, lambda e, U=U, acc=acc, ch=ch: e.scalar_tensor_tensor(out=acc[:], in0=U[:, 0:T], scalar=cwt[:, 0, ch:ch + 1], in1=acc[:],
                                                                                    op0=ALU.mult, op1=ALU.add), reads=[(uk, 0), (uk, 1), 'cwt', ak], writes=[ak])
                P.op('act', lambda e, pb=pb: e.activation(out=ag[pb][:], in_=ag[pb][:], func=AF.Silu), reads=[f'ag{pb}'], writes=[f'ag{pb}'])
                P.op('dve', lambda e, pb=pb, j=j: e.tensor_tensor(out=hid[:, j, :], in0=ag[pb][:], in1=av[pb][:], op=ALU.mult),
                     reads=[f'ag{pb}', f'av{pb}'], writes=[('hid', j)])
            for s in range(4):
                ob = cnt['o'] % 2
                cnt['o'] += 1
                for half in range(2):
                    pb = half
                    for j in range(NCH):
                        P.op('pe', lambda e, j=j, s=s, half=half, pb=pb: e.matmul(pso[pb][:], lhsT=hid[:, j, s * 128:(s + 1) * 128],
                                                                                rhs=wd[:, j, half * 512:(half + 1) * 512], start=(j == 0), stop=(j == NCH - 1)),
                             reads=[('hid', j), ('wd', j)], writes=[f'pso{pb}'])
                    P.op('dve', lambda e, s=s, half=half, pb=pb, ob=ob: e.tensor_tensor(out=yo[ob][:, half * 512:(half + 1) * 512], in0=pso[pb][:],
                                                                                      in1=xres[:, s, half * 512:(half + 1) * 512], op=ALU.add),
                         reads=[f'pso{pb}', ('xres', s)], writes=[(f'yo{ob}', half)])
                yk = [(f'yo{ob}', 0), (f'yo{ob}', 1)]
                if final_norm:
                    sb_ = ss2[ob]
                    sk = f'ss2{ob}'
                    P.op('act', lambda e, ob=ob, sb_=sb_: e.activation(out=junk2[:], in_=yo[ob][:], func=AF.Square, accum_out=sb_[:]), reads=yk, writes=['junk2', sk])
                    P.op('dve', lambda e, sb_=sb_: e.tensor_scalar(out=sb_[:], in0=sb_[:], scalar1=1.0 / D, scalar2=EPS, op0=ALU.mult, op1=ALU.add), reads=[sk], writes=[sk])
                    P.op('act', lambda e, sb_=sb_: e.activation(out=sb_[:], in_=sb_[:], func=AF.Sqrt), reads=[sk], writes=[sk])
                    P.op('dve', lambda e, sb_=sb_: e.reciprocal(out=sb_[:], in_=sb_[:]), reads=[sk], writes=[sk])
                    P.op('dve', lambda e, ob=ob, sb_=sb_: e.scalar_tensor_tensor(out=yo[ob][:], in0=yo[ob][:], scalar=sb_[:, 0:1], in1=fgain[:], op0=ALU.mult, op1=ALU.mult),
                         reads=yk + [sk, 'fgain'], writes=yk)
                P.dma(y[t0 + s * 128:t0 + (s + 1) * 128, :], yo[ob][:], reads=yk, final=True)
        P.emit()
        print("ffn ops", {e: len(P.ops[e]) for e in P.ENGS}, "waits", P.nwaits)
    return nc


SW = 768


def load_w_bf16(P, wdram_view, shape, name, parts=128):
    w = P.sb(shape, BF16, "sbw_" + name)
    if not hasattr(P, "_stg"):
        P._stg = P.sb([128, 6144], F32, "wstage")
    stg = P._stg[0:shape[0], 0:shape[1] * shape[2]].rearrange("p (k n) -> p k n", k=shape[1])
    P.dma(stg, wdram_view, writes=["wstage"])
    P.op('pool', lambda e: e.tensor_copy(out=w[:], in_=stg), reads=["wstage"], writes=[name])
    return w


def build_tail(NT, glu):
    nc = bass.Bass("TRN2", target_bir_lowering=False)
    x = nc.dram_tensor("x", [NT, D], F32, kind="ExternalInput").ap()
    sT = nc.dram_tensor("sT", [SW, NT], F32, kind="ExternalInput").ap()
    mem = nc.dram_tensor("mem", [256, D], F32, kind="ExternalInput").ap()
    ln = nc.dram_tensor("ln", [D], F32, kind="ExternalInput").ap()
    wq = nc.dram_tensor("wq", [D, 256], F32, kind="ExternalInput").ap()
    mng = nc.dram_tensor("mng", [D], F32, kind="ExternalInput").ap()
    wmkv = nc.dram_tensor("wmkv", [D, 512], F32, kind="ExternalInput").ap()
    wglu = nc.dram_tensor("wglu", [SW, SW], F32, kind="ExternalInput").ap()
    wout = nc.dram_tensor("wout", [D, D], F32, kind="ExternalInput").ap()
    y = nc.dram_tensor("y", [NT, D], F32, kind="ExternalOutput").ap()
    T = 512
    ntile = NT // T
    with contextlib.ExitStack() as st:
        P = Prog(nc, st)
        c = setup_common(P)
        tmp = make_norm_tmp(P)
        gain = load_bcast(P, ln, D, "gain")
        mgain = load_bcast(P, mng, D, "mgain")
        wq_b = load_w_bf16(P, wq.rearrange("(k p) n -> p k n", p=128), [128, 8, 256], "wq")
        wm_b = load_w_bf16(P, wmkv.rearrange("(k p) n -> p k n", p=128), [128, 8, 512], "wm")
        wo_s = load_w_bf16(P, wout[0:SW, :].rearrange("(k p) n -> p k n", p=128), [128, 6, D], "wos")
        wo_m = load_w_bf16(P, wout[SW:D, :].rearrange("(h p) n -> p h n", p=64), [64, 4, D], "wom")
        wg_b = load_w_bf16(P, wglu.rearrange("(k p) n -> p k n", p=128), [128, 6, SW], "wg") if glu else None
        ones64 = P.sb([128, 64], BF16, "ones64")
        P.op('pool', lambda e: e.memset(ones64[:], 1.0), writes=['ones64'])
        psA = [P.ps([128, 512], F32) for _ in range(2)]
        psO = P.ps([128, 512], F32)
        psD = P.ps([128, 512], F32)
        pso = [P.ps([128, 512], F32) for _ in range(2)]
        xtt = [P.sb([128, D], F32) for _ in range(2)]
        xres = P.sb([128, 4, D], F32, "xres")
        hT = P.sb([128, 8, T], BF16, "hT")
        hTm = P.sb([128, 8, 256], BF16, "hTm")
        KmT = P.sb([128, 2, 256], BF16, "KmT")
        Vm = P.sb([128, 2, 256], BF16, "Vm")
        cnt = dict(x=0, a=0, o=0)
        for s in range(2):
            bx = cnt['x'] % 2
            cnt['x'] += 1
            P.dma(xtt[bx][:], mem[s * 128:(s + 1) * 128, :], writes=[f'xtt{bx}'])
            norm_T(P, c, xtt[bx][:], f'xtt{bx}', 128, mgain, 'mgain', hTm, ('hTm', s), s * 128, tmp)
        hmk = [('hTm', 0), ('hTm', 1)]
        for cch in range(2):
            pa = cnt['a'] % 2
            cnt['a'] += 1
            for k in range(8):
                P.op('pe', lambda e, k=k, pa=pa, cch=cch: e.matmul(psA[pa][:, 0:256], lhsT=wm_b[:, k, cch * 128:(cch + 1) * 128], rhs=hTm[:, k, :], start=(k == 0), stop=(k == 7)),
                     reads=['wm'] + hmk, writes=[f'psA{pa}'])
            P.op('act', lambda e, pa=pa, cch=cch: e.activation(out=KmT[:, cch, :], in_=psA[pa][:, 0:256], func=AF.Copy), reads=[f'psA{pa}'], writes=[('KmT', cch)])
        for s in range(2):
            pa = cnt['a'] % 2
            cnt['a'] += 1
            for k in range(8):
                P.op('pe', lambda e, k=k, pa=pa, s=s: e.matmul(psA[pa][:, 0:256], lhsT=hTm[:, k, s * 128:(s + 1) * 128], rhs=wm_b[:, k, 256:512], start=(k == 0), stop=(k == 7)),
                     reads=['wm'] + hmk, writes=[f'psA{pa}'])
            P.op('act', lambda e, pa=pa, s=s: e.activation(out=Vm[:, s, :], in_=psA[pa][:, 0:256], func=AF.Copy), reads=[f'psA{pa}'], writes=[('Vm', s)])
        qT = P.sb([128, 2, T], BF16, "qT")
        PT = [P.sb([128, T], BF16) for _ in range(2)]
        rec = P.sb([64, T], F32, "rec")
        memT = P.sb([64, 4, T], BF16, "memT")
        sin = P.sb([128, 6, T], F32, "sin")
        mixT = P.sb([128, 6, T], BF16, "mixT")
        if glu:
            gf = P.sb([128, 6, T], F32, "gf")
            gb = P.sb([128, 6, T], BF16, "gb")
            t1 = [P.sb([128, T], F32) for _ in range(2)]
            sg = [P.sb([128, T], F32) for _ in range(2)]
        yo = [P.sb([128, D], F32) for _ in range(2)]
        for ti in range(ntile):
            t0 = ti * T
            for s in range(4):
                bx = cnt['x'] % 2
                cnt['x'] += 1
                P.dma(xtt[bx][:], x[t0 + s * 128:t0 + (s + 1) * 128, :], writes=[f'xtt{bx}'])
                P.op('pool', lambda e, s=s, bx=bx: e.tensor_copy(out=xres[:, s, :], in_=xtt[bx][:]), reads=[f'xtt{bx}'], writes=[('xres', s)])
                norm_T(P, c, xtt[bx][:], f'xtt{bx}', 128, gain, 'gain', hT, ('hT', s), s * 128, tmp)
            hkeys = [('hT', s) for s in range(4)]
            for cch in range(2):
                pa = cnt['a'] % 2
                cnt['a'] += 1
                for k in range(8):
                    P.op('pe', lambda e, k=k, pa=pa, cch=cch: e.matmul(psA[pa][:], lhsT=wq_b[:, k, cch * 128:(cch + 1) * 128], rhs=hT[:, k, :], start=(k == 0), stop=(k == 7)),
                         reads=['wq'] + hkeys, writes=[f'psA{pa}'])
                P.op('act', lambda e, pa=pa, cch=cch: e.activation(out=qT[:, cch, :], in_=psA[pa][:], func=AF.Copy, scale=0.125), reads=[f'psA{pa}'], writes=[('qT', cch)])
            for h in range(4):
                cch, po = h // 2, (h % 2) * 64
                for ms in range(2):
                    pa = cnt['a'] % 2
                    cnt['a'] += 1
                    P.op('pe', lambda e, pa=pa, cch=cch, po=po, ms=ms: e.matmul(psA[pa][:], lhsT=KmT[po:po + 64, cch, ms * 128:(ms + 1) * 128], rhs=qT[po:po + 64, cch, :], start=True, stop=True),
                         reads=[('KmT', cch), ('qT', cch)], writes=[f'psA{pa}'])
                    P.op('act', lambda e, pa=pa, ms=ms: e.activation(out=PT[ms][:], in_=psA[pa][:], func=AF.Exp), reads=[f'psA{pa}'], writes=[f'PT{ms}'])
                for ms in range(2):
                    P.op('pe', lambda e, ms=ms, h=h: e.matmul(psO[0:64, :], lhsT=Vm[:, ms, h * 64:(h + 1) * 64], rhs=PT[ms][:], start=(ms == 0), stop=(ms == 1)),
                         reads=[('Vm', ms), f'PT{ms}'], writes=['psO'])
                for ms in range(2):
                    P.op('pe', lambda e, ms=ms: e.matmul(psD[0:64, :], lhsT=ones64[:], rhs=PT[ms][:], start=(ms == 0), stop=(ms == 1)),
                         reads=['ones64', f'PT{ms}'], writes=['psD'])
                P.op('dve', lambda e: e.reciprocal(out=rec[:], in_=psD[0:64, :]), reads=['psD'], writes=['rec'])
                P.op('dve', lambda e, h=h: e.tensor_tensor(out=memT[:, h, :], in0=psO[0:64, :], in1=rec[:], op=ALU.mult), reads=['psO', 'rec'], writes=[('memT', h)])
            for k in range(6):
                P.dma(sin[:, k, :], sT[k * 128:(k + 1) * 128, t0:t0 + T], writes=[('sin', k)])
            if glu:
                for k in range(6):
                    b = k % 2
                    P.op('act', lambda e, k=k, b=b: e.activation(out=t1[b][:], in_=sin[:, k, :], func=AF.Square), reads=[('sin', k)], writes=[f't1{b}'])
                    P.op('dve', lambda e, b=b: e.tensor_scalar(out=t1[b][:], in0=t1[b][:], scalar1=0.044715, scalar2=1.0, op0=ALU.mult, op1=ALU.add), reads=[f't1{b}'], writes=[f't1{b}'])
                    P.op('dve', lambda e, k=k, b=b: e.tensor_tensor(out=t1[b][:], in0=t1[b][:], in1=sin[:, k, :], op=ALU.mult), reads=[f't1{b}', ('sin', k)], writes=[f't1{b}'])
                    P.op('act', lambda e, b=b: e.activation(out=t1[b][:], in_=t1[b][:], func=AF.Sigmoid, scale=1.5957691216057308), reads=[f't1{b}'], writes=[f't1{b}'])
                    P.op('dve', lambda e, k=k, b=b: e.tensor_tensor(out=gf[:, k, :], in0=t1[b][:], in1=sin[:, k, :], op=ALU.mult), reads=[f't1{b}', ('sin', k)], writes=[('gf', k)])
                    P.op('pool', lambda e, k=k: e.tensor_copy(out=gb[:, k, :], in_=gf[:, k, :]), reads=[('gf', k)], writes=[('gb', k)])
                for cch in range(6):
                    pa = cnt['a'] % 2
                    cnt['a'] += 1
                    b = cch % 2
                    for k in range(6):
                        P.op('pe', lambda e, k=k, pa=pa, cch=cch: e.matmul(psA[pa][:], lhsT=wg_b[:, k, cch * 128:(cch + 1) * 128], rhs=gb[:, k, :], start=(k == 0), stop=(k == 5)),
                             reads=['wg'] + [('gb', kk) for kk in range(6)], writes=[f'psA{pa}'])
                    P.op('act', lambda e, pa=pa, b=b: e.activation(out=sg[b][:], in_=psA[pa][:], func=AF.Sigmoid), reads=[f'psA{pa}'], writes=[f'sg{b}'])
                    P.op('dve', lambda e, cch=cch, b=b: e.tensor_tensor(out=mixT[:, cch, :], in0=gf[:, cch, :], in1=sg[b][:], op=ALU.mult), reads=[('gf', cch), f'sg{b}'], writes=[('mixT', cch)])
            else:
                for k in range(6):
                    P.op('pool', lambda e, k=k: e.tensor_copy(out=mixT[:, k, :], in_=sin[:, k, :]), reads=[('sin', k)], writes=[('mixT', k)])
            for s in range(4):
                ob = cnt['o'] % 2
                cnt['o'] += 1
                for half in range(2):
                    pb = half
                    for k in range(6):
                        P.op('pe', lambda e, k=k, s=s, half=half, pb=pb: e.matmul(pso[pb][:], lhsT=mixT[:, k, s * 128:(s + 1) * 128], rhs=wo_s[:, k, half * 512:(half + 1) * 512], start=(k == 0), stop=False),
                             reads=[('mixT', k), 'wos'], writes=[f'pso{pb}'])
                    for h in range(4):
                        P.op('pe', lambda e, h=h, s=s, half=half, pb=pb: e.matmul(pso[pb][:], lhsT=memT[:, h, s * 128:(s + 1) * 128], rhs=wo_m[:, h, half * 512:(half + 1) * 512], start=False, stop=(h == 3)),
                             reads=[('memT', h), 'wom'], writes=[f'pso{pb}'])
                    P.op('dve', lambda e, s=s, half=half, pb=pb, ob=ob: e.tensor_tensor(out=yo[ob][:, half * 512:(half + 1) * 512], in0=pso[pb][:], in1=xres[:, s, half * 512:(half + 1) * 512], op=ALU.add),
                         reads=[f'pso{pb}', ('xres', s)], writes=[(f'yo{ob}', half)])
                P.dma(y[t0 + s * 128:t0 + (s + 1) * 128, :], yo[ob][:], reads=[(f'yo{ob}', 0), (f'yo{ob}', 1)], final=True)
        P.emit()
        print("tail ops", {e: len(P.ops[e]) for e in P.ENGS}, "waits", P.nwaits)
    return nc

import math

I32 = mybir.dt.int32
PI = math.pi


def build_s5(NTOK):
    nc = bass.Bass("TRN2", target_bir_lowering=False)
    xb = nc.dram_tensor("xb", [NTOK, D], F32, kind="ExternalInput").ap()
    ln = nc.dram_tensor("ln", [D], F32, kind="ExternalInput").ap()
    win = nc.dram_tensor("win", [D, 192], F32, kind="ExternalInput").ap()
    lre = nc.dram_tensor("lre", [12, 64], F32, kind="ExternalInput").ap()
    lim = nc.dram_tensor("lim", [12, 64], F32, kind="ExternalInput").ap()
    lst = nc.dram_tensor("lst", [12], F32, kind="ExternalInput").ap()
    bre = nc.dram_tensor("bre", [12, 64, 16], F32, kind="ExternalInput").ap()
    bim = nc.dram_tensor("bim", [12, 64, 16], F32, kind="ExternalInput").ap()
    cre = nc.dram_tensor("cre", [12, 16, 64], F32, kind="ExternalInput").ap()
    cim = nc.dram_tensor("cim", [12, 16, 64], F32, kind="ExternalInput").ap()
    dd = nc.dram_tensor("dd", [12, 16], F32, kind="ExternalInput").ap()
    yT = nc.dram_tensor("yT", [192, NTOK], F32, kind="ExternalOutput").ap()
    T = 512
    nfr = NTOK // T
    with contextlib.ExitStack() as st:
        P = Prog(nc, st)
        c = setup_common(P)
        tmp = make_norm_tmp(P)
        gain = load_bcast(P, ln, D, "gain")
        win_b = load_w_bf16(P, win.rearrange("(k p) n -> p k n", p=128), [128, 8, 192], "win")
        uid = [0]

        def new(shape, dt=F32):
            uid[0] += 1
            nm = f"t{uid[0]}"
            return P.sb(shape, dt, nm), nm

        def dve(fn, reads, writes):
            P.op('dve', fn, reads=reads, writes=writes)

        sc_tmp = {}

        def sincos(ang, kang, n):
            s_, ks = new([128, n])
            c_, kc = new([128, n])
            for (o, ko, sh) in ((s_, ks, 0.0), (c_, kc, 0.5 * PI)):
                if n not in sc_tmp:
                    sc_tmp[n] = (new([128, n]), new([128, n], I32), new([128, n]))
                (a_, ka), (ki, kki), (kf, kkf) = sc_tmp[n]
                dve(lambda e, a_=a_, sh=sh: e.tensor_scalar(out=a_[:], in0=ang[:], scalar1=sh, scalar2=None, op0=ALU.add), [kang], [ka])
                dve(lambda e, a_=a_, kf=kf: e.tensor_scalar(out=kf[:], in0=a_[:], scalar1=1.0 / (2 * PI), scalar2=None, op0=ALU.mult), [ka], [kkf])
                dve(lambda e, ki=ki, kf=kf: e.tensor_copy(out=ki[:], in_=kf[:]), [kkf], [kki])
                dve(lambda e, ki=ki, kf=kf: e.tensor_copy(out=kf[:], in_=ki[:]), [kki], [kkf])
                dve(lambda e, o=o, kf=kf, a_=a_: e.scalar_tensor_tensor(out=o[:], in0=kf[:], scalar=-2 * PI, in1=a_[:], op0=ALU.mult, op1=ALU.add), [kkf, ka], [ko])
                dve(lambda e, o=o, kf=kf: e.tensor_scalar(out=kf[:], in0=o[:], scalar1=PI, scalar2=2 * PI, op0=ALU.is_gt, op1=ALU.mult), [ko], [kkf])
                dve(lambda e, o=o, kf=kf: e.tensor_tensor(out=o[:], in0=o[:], in1=kf[:], op=ALU.subtract), [ko, kkf], [ko])
                dve(lambda e, o=o, kf=kf: e.tensor_scalar(out=kf[:], in0=o[:], scalar1=-PI, scalar2=2 * PI, op0=ALU.is_lt, op1=ALU.mult), [ko], [kkf])
                dve(lambda e, o=o, kf=kf: e.tensor_tensor(out=o[:], in0=o[:], in1=kf[:], op=ALU.add), [ko, kkf], [ko])
                P.op('act', lambda e, o=o: e.activation(out=o[:], in_=o[:], func=AF.Sin), reads=[ko], writes=[ko])
            return s_, ks, c_, kc

        io_i, kioi = new([128, T], I32)
        iof, kio = new([128, T])
        P.op('pool', lambda e: e.iota(io_i[:], pattern=[[1, T]], base=0, channel_multiplier=0), writes=[kioi])
        dve(lambda e: e.tensor_copy(out=iof[:], in_=io_i[:]), [kioi], [kio])
        ones_t, kones = new([128, T])
        P.op('pool', lambda e: e.memset(ones_t[:], 1.0), writes=[kones])
        dch0, kd0 = new([128, 1])
        dch1, kd1 = new([128, 1])
        ddf = dd.rearrange("g (p o) -> (g p) o", o=1)
        P.dma(dch0[:], ddf[0:128, :], writes=[kd0])
        P.dma(dch1[0:64, :], ddf[128:192, :], writes=[kd1])

        prm = []
        for pr in range(6):
            g0 = 2 * pr
            kt = pr // 4
            col0 = (32 * pr) % 128
            q = Ctx()
            lr, klr = new([128, 1])
            li, kli = new([128, 1])
            ls, kls = new([128, 1])
            P.dma(lr[:], lre[g0:g0 + 2, :].rearrange("g (n o) -> (g n) o", o=1), writes=[klr])
            P.dma(li[:], lim[g0:g0 + 2, :].rearrange("g (n o) -> (g n) o", o=1), writes=[kli])
            P.dma(ls[0:64, :], lst[g0:g0 + 1].partition_broadcast(64), writes=[(kls, 0)])
            P.dma(ls[64:128, :], lst[g0 + 1:g0 + 2].partition_broadcast(64), writes=[(kls, 1)])
            dt_, kdt = new([128, 1])
            P.op('act', lambda e, dt_=dt_, ls=ls: e.activation(out=dt_[:], in_=ls[:], func=AF.Exp), reads=[(kls, 0), (kls, 1)], writes=[kdt])
            rho, krho = new([128, 1])
            dve(lambda e, rho=rho, lr=lr, dt_=dt_: e.tensor_tensor(out=rho[:], in0=lr[:], in1=dt_[:], op=ALU.mult), [klr, kdt], [krho])
            P.op('act', lambda e, rho=rho: e.activation(out=rho[:], in_=rho[:], func=AF.Exp), reads=[krho], writes=[krho])
            th, kth = new([128, 1])
            dve(lambda e, th=th, li=li, dt_=dt_: e.tensor_tensor(out=th[:], in0=li[:], in1=dt_[:], op=ALU.mult), [kli, kdt], [kth])
            sn, ksn, cs, kcs = sincos(th, kth, 1)
            abr, kabr = new([128, 1])
            abi, kabi = new([128, 1])
            dve(lambda e, abr=abr, rho=rho, cs=cs: e.tensor_tensor(out=abr[:], in0=rho[:], in1=cs[:], op=ALU.mult), [krho, kcs], [kabr])
            dve(lambda e, abi=abi, rho=rho, sn=sn: e.tensor_tensor(out=abi[:], in0=rho[:], in1=sn[:], op=ALU.mult), [krho, ksn], [kabi])
            den, kden = new([128, 1])
            t_, kt_ = new([128, 1])
            dve(lambda e, den=den, lr=lr: e.tensor_tensor(out=den[:], in0=lr[:], in1=lr[:], op=ALU.mult), [klr], [kden])
            dve(lambda e, t_=t_, li=li: e.tensor_tensor(out=t_[:], in0=li[:], in1=li[:], op=ALU.mult), [kli], [kt_])
            dve(lambda e, den=den, t_=t_: e.tensor_tensor(out=den[:], in0=den[:], in1=t_[:], op=ALU.add), [kden, kt_], [kden])
            dve(lambda e, den=den: e.reciprocal(out=den[:], in_=den[:]), [kden], [kden])
            nr, knr = new([128, 1])
            dve(lambda e, nr=nr, abr=abr: e.tensor_scalar(out=nr[:], in0=abr[:], scalar1=-1.0, scalar2=None, op0=ALU.add), [kabr], [knr])
            cfr, kcfr = new([128, 1])
            cfi, kcfi = new([128, 1])
            ncfi, kncfi = new([128, 1])
            dve(lambda e, t_=t_, abi=abi, li=li: e.tensor_tensor(out=t_[:], in0=abi[:], in1=li[:], op=ALU.mult), [kabi, kli], [kt_])
            dve(lambda e, cfr=cfr, nr=nr, lr=lr, t_=t_: e.scalar_tensor_tensor(out=cfr[:], in0=nr[:], scalar=lr[:, 0:1], in1=t_[:], op0=ALU.mult, op1=ALU.add), [knr, klr, kt_], [kcfr])
            dve(lambda e, cfr=cfr, den=den: e.tensor_tensor(out=cfr[:], in0=cfr[:], in1=den[:], op=ALU.mult), [kcfr, kden], [kcfr])
            dve(lambda e, t_=t_, nr=nr, li=li: e.tensor_tensor(out=t_[:], in0=nr[:], in1=li[:], op=ALU.mult), [knr, kli], [kt_])
            dve(lambda e, cfi=cfi, abi=abi, lr=lr, t_=t_: e.scalar_tensor_tensor(out=cfi[:], in0=abi[:], scalar=lr[:, 0:1], in1=t_[:], op0=ALU.mult, op1=ALU.subtract), [kabi, klr, kt_], [kcfi])
            dve(lambda e, cfi=cfi, den=den: e.tensor_tensor(out=cfi[:], in0=cfi[:], in1=den[:], op=ALU.mult), [kcfi, kden], [kcfi])
            dve(lambda e, ncfi=ncfi, cfi=cfi: e.tensor_scalar(out=ncfi[:], in0=cfi[:], scalar1=-1.0, scalar2=None, op0=ALU.mult), [kcfi], [kncfi])
            Br, kBr = new([128, 16])
            Bi, kBi = new([128, 16])
            P.dma(Br[:], bre[g0:g0 + 2].rearrange("g n q -> (g n) q"), writes=[kBr])
            P.dma(Bi[:], bim[g0:g0 + 2].rearrange("g n q -> (g n) q"), writes=[kBi])
            q.LB = []
            for part in range(2):
                Bb, kBb = new([128, 16])
                if part == 0:
                    dve(lambda e, Bb=Bb, Br=Br, cfr=cfr: e.tensor_scalar(out=Bb[:], in0=Br[:], scalar1=cfr[:, 0:1], scalar2=None, op0=ALU.mult), [kBr, kcfr], [kBb])
                    dve(lambda e, Bb=Bb, Bi=Bi, ncfi=ncfi: e.scalar_tensor_tensor(out=Bb[:], in0=Bi[:], scalar=ncfi[:, 0:1], in1=Bb[:], op0=ALU.mult, op1=ALU.add), [kBi, kncfi, kBb], [kBb])
                else:
                    dve(lambda e, Bb=Bb, Bi=Bi, cfr=cfr: e.tensor_scalar(out=Bb[:], in0=Bi[:], scalar1=cfr[:, 0:1], scalar2=None, op0=ALU.mult), [kBi, kcfr], [kBb])
                    dve(lambda e, Bb=Bb, Br=Br, cfi=cfi: e.scalar_tensor_tensor(out=Bb[:], in0=Br[:], scalar=cfi[:, 0:1], in1=Bb[:], op0=ALU.mult, op1=ALU.add), [kBr, kcfi, kBb], [kBb])
                Z, kZ = new([128, 128], BF16)
                P.op('pool', lambda e, Z=Z: e.memset(Z[:], 0.0), writes=[kZ])
                dve(lambda e, Z=Z, Bb=Bb, col0=col0: e.tensor_copy(out=Z[0:64, col0:col0 + 16], in_=Bb[0:64, :]), [kBb, kZ], [kZ])
                dve(lambda e, Z=Z, Bb=Bb, col0=col0: e.tensor_copy(out=Z[64:128, col0 + 16:col0 + 32], in_=Bb[64:128, :]), [kBb, kZ], [kZ])
                LB, kLB = new([128, 128], BF16)
                pz = tmp['pst'][0]
                P.op('pe', lambda e, Z=Z, pz=pz: e.transpose(out=pz[:, 0:128], in_=Z[:], identity=c.ident[:]), reads=[kZ, 'ident'], writes=[('pst0', 0)])
                P.op('act', lambda e, LB=LB, pz=pz: e.activation(out=LB[:], in_=pz[:, 0:128], func=AF.Copy), reads=[('pst0', 0)], writes=[kLB])
                q.LB.append((LB, kLB))
            q.LC = []
            for part, src in ((0, cre), (1, cim)):
                Cf, kCf = new([128, 32])
                P.op('pool', lambda e, Cf=Cf: e.memset(Cf[:], 0.0), writes=[kCf])
                P.dma(Cf[0:64, 0:16], src[g0].rearrange("p n -> n p"), reads=[kCf], writes=[(kCf, 0)], allow_slow_non_contiguous=True)
                P.dma(Cf[64:128, 16:32], src[g0 + 1].rearrange("p n -> n p"), reads=[kCf], writes=[(kCf, 1)], allow_slow_non_contiguous=True)
                LC, kLC = new([128, 32], BF16)
                P.op('act', lambda e, LC=LC, Cf=Cf, part=part: e.activation(out=LC[:], in_=Cf[:], func=AF.Copy, scale=(1.0 if part == 0 else -1.0)),
                     reads=[kCf, (kCf, 0), (kCf, 1)], writes=[kLC])
                q.LC.append((LC, kLC))
            LD, kLD = new([128, 32], BF16)
            dch, kdch = (dch0, kd0) if kt == 0 else (dch1, kd1)
            K = 128 if kt == 0 else 64
            dve(lambda e, LD=LD, dch=dch, K=K, col0=col0: e.tensor_scalar(out=LD[0:K, :], in0=c.ident[0:K, col0:col0 + 32], scalar1=dch[0:K, 0:1], scalar2=None, op0=ALU.mult),
                ['ident', kdch], [kLD])
            q.LD, q.kLD, q.K, q.kt = LD, kLD, K, kt
            ang, kang = new([128, T])
            dve(lambda e, ang=ang, th=th: e.tensor_scalar(out=ang[:], in0=iof[:], scalar1=th[:, 0:1], scalar2=None, op0=ALU.mult), [kio, kth], [kang])
            q.st, q.kst, q.ct, q.kct = sincos(ang, kang, T)
            q.rt, q.krt = new([128, T])
            dve(lambda e, q=q, rho=rho: e.tensor_scalar(out=q.rt[:], in0=ones_t[:], scalar1=rho[:, 0:1], scalar2=None, op0=ALU.mult), [kones, krho], [q.krt])
            angT, kangT = new([128, 1])
            dve(lambda e, angT=angT, th=th: e.tensor_scalar(out=angT[:], in0=th[:], scalar1=float(T), scalar2=None, op0=ALU.mult), [kth], [kangT])
            q.sT, q.ksT, q.cT, q.kcT = sincos(angT, kangT, 1)
            q.nsT, q.knsT = new([128, 1])
            dve(lambda e, q=q: e.tensor_scalar(out=q.nsT[:], in0=q.sT[:], scalar1=-1.0, scalar2=None, op0=ALU.mult), [q.ksT], [q.knsT])
            q.wrc, q.kwrc = new([128, 1])
            q.wic, q.kwic = new([128, 1])
            P.op('pool', lambda e, q=q: e.memset(q.wrc[:], 0.0), writes=[q.kwrc])
            P.op('pool', lambda e, q=q: e.memset(q.wic[:], 0.0), writes=[q.kwic])
            q.tc_, q.ktc = new([128, 1])
            prm.append(q)

        xtt = [P.sb([128, D], F32) for _ in range(2)]
        hT = P.sb([128, 8, T], BF16, "hT")
        uT = [P.sb([128, T], BF16, "uT0"), P.sb([128, T], BF16, "uT1")]
        psu = P.ps([128, T], F32)
        psb = [[P.ps([128, T], F32) for _ in range(2)] for _ in range(2)]
        psy = P.ps([128, T], F32)
        W = []
        for b in range(2):
            w = Ctx()
            for nm in ('bur', 'bui', 'a1', 'a2', 'a3', 'a4', 'inr', 'ini', 'wr', 'wi'):
                setattr(w, nm, P.sb([128, T], F32))
            w.sr = P.sb([128, T], BF16)
            w.si = P.sb([128, T], BF16)
            w.ys = P.sb([32, T], F32)
            W.append(w)
        cnt = dict(x=0, p=0)
        for f in range(nfr):
            t0 = f * T
            for s in range(4):
                bx = cnt['x'] % 2
                cnt['x'] += 1
                P.dma(xtt[bx][:], xb[t0 + s * 128:t0 + (s + 1) * 128, :], writes=[f'xtt{bx}'])
                norm_T(P, c, xtt[bx][:], f'xtt{bx}', 128, gain, 'gain', hT, ('hT', s), s * 128, tmp)
            hkeys = [('hT', s) for s in range(4)]
            for kt, (lo, hi) in enumerate(((0, 128), (128, 192))):
                n = hi - lo
                for k in range(8):
                    P.op('pe', lambda e, k=k, lo=lo, hi=hi, n=n: e.matmul(psu[0:n, :], lhsT=win_b[:, k, lo:hi], rhs=hT[:, k, :], start=(k == 0), stop=(k == 7)),
                         reads=['win'] + hkeys, writes=['psu'])
                P.op('act', lambda e, kt=kt, n=n: e.activation(out=uT[kt][0:n, :], in_=psu[0:n, :], func=AF.Copy), reads=['psu'], writes=[f'uT{kt}'])
            for pr in range(6):
                q = prm[pr]
                b = cnt['p'] % 2
                cnt['p'] += 1
                w = W[b]
                K, kt = q.K, q.kt
                kk = lambda nm: f'{nm}{b}'
                for part, dst in ((0, w.bur), (1, w.bui)):
                    LB, kLB = q.LB[part]
                    pb = psb[b][part]
                    P.op('pe', lambda e, LB=LB, pb=pb, K=K, kt=kt: e.matmul(pb[:], lhsT=LB[0:K, :], rhs=uT[kt][0:K, :], start=True, stop=True),
                         reads=[kLB, f'uT{kt}'], writes=[f'psb{b}{part}'])
                    P.op('act', lambda e, dst=dst, pb=pb: e.activation(out=dst[:], in_=pb[:], func=AF.Copy), reads=[f'psb{b}{part}'], writes=[kk('bur' if part == 0 else 'bui')])
                P.op('pool', lambda e, w=w, q=q: e.tensor_tensor(out=w.a1[:], in0=q.ct[:], in1=w.bur[:], op=ALU.mult), reads=[q.kct, kk('bur')], writes=[kk('a1')])
                P.op('pool', lambda e, w=w, q=q: e.tensor_tensor(out=w.a2[:], in0=q.st[:], in1=w.bui[:], op=ALU.mult), reads=[q.kst, kk('bui')], writes=[kk('a2')])
                dve(lambda e, w=w: e.tensor_tensor(out=w.inr[:], in0=w.a1[:], in1=w.a2[:], op=ALU.add), [kk('a1'), kk('a2')], [kk('inr')])
                P.op('pool', lambda e, w=w, q=q: e.tensor_tensor(out=w.a3[:], in0=q.ct[:], in1=w.bui[:], op=ALU.mult), reads=[q.kct, kk('bui')], writes=[kk('a3')])
                P.op('pool', lambda e, w=w, q=q: e.tensor_tensor(out=w.a4[:], in0=q.st[:], in1=w.bur[:], op=ALU.mult), reads=[q.kst, kk('bur')], writes=[kk('a4')])
                dve(lambda e, w=w: e.tensor_tensor(out=w.ini[:], in0=w.a3[:], in1=w.a4[:], op=ALU.subtract), [kk('a3'), kk('a4')], [kk('ini')])
                dve(lambda e, w=w, q=q: e.tensor_tensor_scan(out=w.wr[:], data0=q.rt[:], data1=w.inr[:], initial=q.wrc[:, 0:1], op0=ALU.mult, op1=ALU.add),
                    [q.krt, kk('inr'), q.kwrc], [kk('wr')])
                dve(lambda e, w=w, q=q: e.tensor_tensor_scan(out=w.wi[:], data0=q.rt[:], data1=w.ini[:], initial=q.wic[:, 0:1], op0=ALU.mult, op1=ALU.add),
                    [q.krt, kk('ini'), q.kwic], [kk('wi')])
                dve(lambda e, w=w, q=q: e.tensor_tensor(out=q.tc_[:], in0=w.wr[:, T - 1:T], in1=q.cT[:], op=ALU.mult), [kk('wr'), q.kcT], [q.ktc])
                dve(lambda e, w=w, q=q: e.scalar_tensor_tensor(out=q.wrc[:], in0=w.wi[:, T - 1:T], scalar=q.nsT[:, 0:1], in1=q.tc_[:], op0=ALU.mult, op1=ALU.add),
                    [kk('wi'), q.knsT, q.ktc], [q.kwrc])
                dve(lambda e, w=w, q=q: e.tensor_tensor(out=q.tc_[:], in0=w.wi[:, T - 1:T], in1=q.cT[:], op=ALU.mult), [kk('wi'), q.kcT], [q.ktc])
                dve(lambda e, w=w, q=q: e.scalar_tensor_tensor(out=q.wic[:], in0=w.wr[:, T - 1:T], scalar=q.sT[:, 0:1], in1=q.tc_[:], op0=ALU.mult, op1=ALU.add),
                    [kk('wr'), q.ksT, q.ktc], [q.kwic])
                P.op('pool', lambda e, w=w, q=q: e.tensor_tensor(out=w.a1[:], in0=q.ct[:], in1=w.wr[:], op=ALU.mult), reads=[q.kct, kk('wr')], writes=[kk('a1')])
                P.op('pool', lambda e, w=w, q=q: e.tensor_tensor(out=w.a2[:], in0=q.st[:], in1=w.wi[:], op=ALU.mult), reads=[q.kst, kk('wi')], writes=[kk('a2')])
                dve(lambda e, w=w: e.tensor_tensor(out=w.sr[:], in0=w.a1[:], in1=w.a2[:], op=ALU.subtract), [kk('a1'), kk('a2')], [kk('sr')])
                P.op('pool', lambda e, w=w, q=q: e.tensor_tensor(out=w.a3[:], in0=q.st[:], in1=w.wr[:], op=ALU.mult), reads=[q.kst, kk('wr')], writes=[kk('a3')])
                P.op('pool', lambda e, w=w, q=q: e.tensor_tensor(out=w.a4[:], in0=q.ct[:], in1=w.wi[:], op=ALU.mult), reads=[q.kct, kk('wi')], writes=[kk('a4')])
                dve(lambda e, w=w: e.tensor_tensor(out=w.si[:], in0=w.a3[:], in1=w.a4[:], op=ALU.add), [kk('a3'), kk('a4')], [kk('si')])
                P.op('pe', lambda e, w=w, q=q: e.matmul(psy[0:32, :], lhsT=q.LC[0][0][:], rhs=w.sr[:], start=True, stop=False), reads=[q.LC[0][1], kk('sr')], writes=['psy'])
                P.op('pe', lambda e, w=w, q=q: e.matmul(psy[0:32, :], lhsT=q.LC[1][0][:], rhs=w.si[:], start=False, stop=False), reads=[q.LC[1][1], kk('si')], writes=['psy'])
                P.op('pe', lambda e, q=q, K=K, kt=kt: e.matmul(psy[0:32, :], lhsT=q.LD[0:K, :], rhs=uT[kt][0:K, :], start=False, stop=True), reads=[q.kLD, f'uT{kt}'], writes=['psy'])
                P.op('act', lambda e, w=w: e.activation(out=w.ys[:], in_=psy[0:32, :], func=AF.Copy), reads=['psy'], writes=[kk('ys')])
                P.dma(yT[32 * pr:32 * pr + 32, t0:t0 + T], w.ys[:], reads=[kk('ys')], final=True)
        P.emit()
        print("s5 ops", {e: len(P.ops[e]) for e in P.ENGS}, "waits", P.nwaits)
    return nc


LN1E4_32 = math.log(10000.0) / 32.0


def make_sincos(P, new):
    sc_tmp = {}

    def sincos(ang, kang, n):
        s_, ks = new([128, n])
        c_, kc = new([128, n])
        for (o, ko, sh) in ((s_, ks, 0.0), (c_, kc, 0.5 * PI)):
            if n not in sc_tmp:
                sc_tmp[n] = (new([128, n]), new([128, n], I32), new([128, n]))
            (a_, ka), (ki, kki), (kf, kkf) = sc_tmp[n]
            d = lambda fn, r, w: P.op('dve', fn, reads=r, writes=w)
            d(lambda e, a_=a_, sh=sh: e.tensor_scalar(out=a_[:], in0=ang[:], scalar1=sh, scalar2=None, op0=ALU.add), [kang], [ka])
            d(lambda e, a_=a_, kf=kf: e.tensor_scalar(out=kf[:], in0=a_[:], scalar1=1.0 / (2 * PI), scalar2=None, op0=ALU.mult), [ka], [kkf])
            d(lambda e, ki=ki, kf=kf: e.tensor_copy(out=ki[:], in_=kf[:]), [kkf], [kki])
            d(lambda e, ki=ki, kf=kf: e.tensor_copy(out=kf[:], in_=ki[:]), [kki], [kkf])
            d(lambda e, o=o, kf=kf, a_=a_: e.scalar_tensor_tensor(out=o[:], in0=kf[:], scalar=-2 * PI, in1=a_[:], op0=ALU.mult, op1=ALU.add), [kkf, ka], [ko])
            d(lambda e, o=o, kf=kf: e.tensor_scalar(out=kf[:], in0=o[:], scalar1=PI, scalar2=2 * PI, op0=ALU.is_gt, op1=ALU.mult), [ko], [kkf])
            d(lambda e, o=o, kf=kf: e.tensor_tensor(out=o[:], in0=o[:], in1=kf[:], op=ALU.subtract), [ko, kkf], [ko])
            d(lambda e, o=o, kf=kf: e.tensor_scalar(out=kf[:], in0=o[:], scalar1=-PI, scalar2=2 * PI, op0=ALU.is_lt, op1=ALU.mult), [ko], [kkf])
            d(lambda e, o=o, kf=kf: e.tensor_tensor(out=o[:], in0=o[:], in1=kf[:], op=ALU.add), [ko, kkf], [ko])
            P.op('act', lambda e, o=o: e.activation(out=o[:], in_=o[:], func=AF.Sin), reads=[ko], writes=[ko])
        return s_, ks, c_, kc
    return sincos


def build_qkv(NT):
    nc = bass.Bass("TRN2", target_bir_lowering=False)
    x = nc.dram_tensor("x", [NT, D], F32, kind="ExternalInput").ap()
    ln = nc.dram_tensor("ln", [D], F32, kind="ExternalInput").ap()
    kvn = nc.dram_tensor("kvn", [D], F32, kind="ExternalInput").ap()
    wd = {nm: nc.dram_tensor(nm, [D, SW], F32, kind="ExternalInput").ap() for nm in ("wqa", "wqb", "wka", "wkb", "wv")}
    pos = nc.dram_tensor("pos", [NT], F32, kind="ExternalInput").ap()
    QT = nc.dram_tensor("QT", [SW, NT], F32, kind="ExternalOutput").ap()
    KT = nc.dram_tensor("KT", [SW, NT], F32, kind="ExternalOutput").ap()
    V = nc.dram_tensor("V", [NT, SW], F32, kind="ExternalOutput").ap()
    KM = nc.dram_tensor("KM", [SW, NT // 256], F32, kind="ExternalOutput").ap()
    T = 512
    with contextlib.ExitStack() as st:
        P = Prog(nc, st)
        c = setup_common(P)
        tmp = make_norm_tmp(P)
        gq = load_bcast(P, ln, D, "gq")
        gk = load_bcast(P, kvn, D, "gk")
        wb = {nm: load_w_bf16(P, wd[nm].rearrange("(k p) n -> p k n", p=128), [128, 8, SW], nm) for nm in wd}
        uid = [0]

        def new(shape, dt=F32):
            uid[0] += 1
            nm = f"t{uid[0]}"
            return P.sb(shape, dt, nm), nm
        sincos = make_sincos(P, new)
        dve = lambda fn, r, w: P.op('dve', fn, reads=r, writes=w)
        pi_, kpi = new([128, 1], I32)
        pj, kpj = new([128, 1], I32)
        pf, kpf = new([128, 1])
        invf, kinv = new([128, 1])
        sgn, ksgn = new([128, 1])
        P.op('pool', lambda e: e.iota(pi_[:], pattern=[[0, 1]], base=0, channel_multiplier=1), writes=[kpi])
        dve(lambda e: e.tensor_single_scalar(out=pj[:], in_=pi_[:], scalar=31, op=ALU.bitwise_and), [kpi], [kpj])
        dve(lambda e: e.tensor_copy(out=pf[:], in_=pj[:]), [kpj], [kpf])
        P.op('act', lambda e: e.activation(out=invf[:], in_=pf[:], func=AF.Exp, scale=-LN1E4_32), reads=[kpf], writes=[kinv])
        dve(lambda e: e.tensor_single_scalar(out=pj[:], in_=pi_[:], scalar=32, op=ALU.bitwise_and), [kpi, kpf], [kpj])
        dve(lambda e: e.tensor_copy(out=sgn[:], in_=pj[:]), [kpj], [ksgn])
        dve(lambda e: e.tensor_scalar(out=sgn[:], in0=sgn[:], scalar1=1.0 / 16.0, scalar2=-1.0, op0=ALU.mult, op1=ALU.add), [ksgn], [ksgn])
        posb, kposb = new([128, T])
        ang, kang = new([128, T])
        xtt = [P.sb([128, D], F32) for _ in range(2)]
        hq = P.sb([128, 8, T], BF16, "hq")
        hk = P.sb([128, 8, T], BF16, "hk")
        psA = [P.ps([128, T], F32) for _ in range(2)]
        psB = [P.ps([128, T], F32) for _ in range(2)]
        psV = [P.ps([128, T], F32) for _ in range(2)]
        ta = [P.sb([128, T], F32) for _ in range(2)]
        tb = [P.sb([128, T], F32) for _ in range(2)]
        kms = [P.sb([128, 2], F32) for _ in range(2)]
        vo = [P.sb([128, SW], F32) for _ in range(2)]
        cnt = dict(x=0, a=0, v=0)
        for ti in range(NT // T):
            t0 = ti * T
            P.dma(posb[:], pos[t0:t0 + T].partition_broadcast(128), writes=[kposb])
            dve(lambda e: e.tensor_scalar(out=ang[:], in0=posb[:], scalar1=invf[:, 0:1], scalar2=None, op0=ALU.mult), [kposb, kinv], [kang])
            sn, ksn, cs, kcs = sincos(ang, kang, T)
            dve(lambda e, sn=sn: e.tensor_scalar(out=sn[:], in0=sn[:], scalar1=sgn[:, 0:1], scalar2=None, op0=ALU.mult), [ksn, ksgn], [ksn])
            for s in range(4):
                bx = cnt['x'] % 2
                cnt['x'] += 1
                P.dma(xtt[bx][:], x[t0 + s * 128:t0 + (s + 1) * 128, :], writes=[f'xtt{bx}'])
                norm_T(P, c, xtt[bx][:], f'xtt{bx}', 128, gq, 'gq', hq, ('hq', s), s * 128, tmp)
                norm_T(P, c, xtt[bx][:], f'xtt{bx}', 128, gk, 'gk', hk, ('hk', s), s * 128, tmp)
            for (hh, hn, wa, wb_, out, scale, is_k) in ((hq, 'hq', 'wqa', 'wqb', QT, 0.125, False), (hk, 'hk', 'wka', 'wkb', KT, 1.0, True)):
                hkeys = [(hn, s) for s in range(4)]
                for ch in range(6):
                    b = cnt['a'] % 2
                    cnt['a'] += 1
                    for (ps, w_, pn) in ((psA[b], wa, 'psA'), (psB[b], wb_, 'psB')):
                        for k in range(8):
                            P.op('pe', lambda e, ps=ps, w_=w_, k=k, ch=ch, hh=hh: e.matmul(ps[:], lhsT=wb[w_][:, k, ch * 128:(ch + 1) * 128], rhs=hh[:, k, :], start=(k == 0), stop=(k == 7)),
                                 reads=[w_] + hkeys, writes=[f'{pn}{b}'])
                    dve(lambda e, b=b, cs=cs: e.tensor_tensor(out=ta[b][:], in0=psA[b][:], in1=cs[:], op=ALU.mult), [f'psA{b}', kcs], [f'ta{b}'])
                    dve(lambda e, b=b, sn=sn: e.tensor_tensor(out=tb[b][:], in0=psB[b][:], in1=sn[:], op=ALU.mult), [f'psB{b}', ksn], [f'tb{b}'])
                    dve(lambda e, b=b: e.tensor_tensor(out=ta[b][:], in0=ta[b][:], in1=tb[b][:], op=ALU.add), [f'ta{b}', f'tb{b}'], [f'ta{b}'])
                    if scale != 1.0:
                        P.op('act', lambda e, b=b, scale=scale: e.activation(out=ta[b][:], in_=ta[b][:], func=AF.Copy, scale=scale), reads=[f'ta{b}'], writes=[f'ta{b}'])
                    P.dma(out[ch * 128:(ch + 1) * 128, t0:t0 + T], ta[b][:], reads=[f'ta{b}'], final=True)
                    if is_k:
                        dve(lambda e, b=b: e.tensor_reduce(out=kms[b][:], in_=ta[b][:].rearrange("p (n k) -> p n k", k=256), axis=AX.X, op=ALU.add), [f'ta{b}'], [f'kms{b}'])
                        P.dma(KM[ch * 128:(ch + 1) * 128, ti * 2:ti * 2 + 2], kms[b][:], reads=[f'kms{b}'], final=True)
            hkeys = [('hk', s) for s in range(4)]
            for s in range(4):
                vb = cnt['v'] % 2
                cnt['v'] += 1
                for (lo, hi, half) in ((0, 512, 0), (512, 768, 1)):
                    for k in range(8):
                        P.op('pe', lambda e, k=k, s=s, lo=lo, hi=hi, half=half: e.matmul(psV[half][:, 0:hi - lo], lhsT=hk[:, k, s * 128:(s + 1) * 128], rhs=wb['wv'][:, k, lo:hi], start=(k == 0), stop=(k == 7)),
                             reads=['wv'] + hkeys, writes=[f'psV{half}'])
                    P.op('act', lambda e, vb=vb, lo=lo, hi=hi, half=half: e.activation(out=vo[vb][:, lo:hi], in_=psV[half][:, 0:hi - lo], func=AF.Copy), reads=[f'psV{half}'], writes=[(f'vo{vb}', half)])
                P.dma(V[t0 + s * 128:t0 + (s + 1) * 128, :], vo[vb][:], reads=[(f'vo{vb}', 0), (f'vo{vb}', 1)], final=True)
        P.emit()
        print("qkv ops", {e: len(P.ops[e]) for e in P.ENGS}, "waits", P.nwaits)
    return nc


def build_moba(NB, NH=6):
    nc = bass.Bass("TRN2", target_bir_lowering=False)
    NK = NB * 256
    QT = nc.dram_tensor("QT", [NH * 64, NB * 128], F32, kind="ExternalInput").ap()
    KT = nc.dram_tensor("KT", [NH * 64, NK], F32, kind="ExternalInput").ap()
    V = nc.dram_tensor("V", [NK, NH * 64], F32, kind="ExternalInput").ap()
    KM = nc.dram_tensor("KM", [NH * 64, NB], F32, kind="ExternalInput").ap()
    cmaskT = nc.dram_tensor("cmaskT", [256, 128], F32, kind="ExternalInput").ap()
    O = nc.dram_tensor("O", [NB * 128, NH * 64], F32, kind="ExternalOutput").ap()
    NBP = max(NB, 8)
    NS = 3
    with contextlib.ExitStack() as st:
        P = Prog(nc, st)
        c = setup_common(P)
        dve = lambda fn, r, w: P.op('dve', fn, reads=r, writes=w)
        stg = [P.sb([128, 2048], F32) for _ in range(2)]
        cmT = P.sb([128, 2, 128], BF16, "cmT")
        P.dma(stg[0][:, 0:256].rearrange("p (j q) -> p j q", j=2), cmaskT.rearrange("(j p) q -> p j q", p=128), writes=['stg0'])
        P.op('pool', lambda e: e.tensor_copy(out=cmT[:], in_=stg[0][:, 0:256].rearrange("p (j q) -> p j q", j=2)), reads=['stg0'], writes=['cmT'])
        KE = P.sb([128, NK], BF16, "KE")
        E = KE[64:128, :]
        kb = KE[0:64, :]
        P.op('pool', lambda e: e.memset(E, 1.0), writes=['E'])
        P.op('pool', lambda e: e.affine_select(out=E, in_=E, pattern=[[1, NK]], compare_op=ALU.is_ge, fill=0.0, base=0, channel_multiplier=-256), reads=['E'], writes=['E'])
        P.op('pool', lambda e: e.affine_select(out=E, in_=E, pattern=[[-1, NK]], compare_op=ALU.is_ge, fill=0.0, base=255, channel_multiplier=256), reads=['E'], writes=['E'])
        QB = P.sb([128, NB * 128], BF16, "QB")
        qb = QB[0:64, :]
        biasT = QB[64:128, :].rearrange("p (n q) -> p n q", q=128)
        vb = P.sb([128, NB * 2, 65], BF16, "vb")
        P.op('pool', lambda e: e.memset(vb[:], 1.0), writes=['vb'])
        kmb = P.sb([64, NB], BF16, "kmb")
        psS = [P.ps([128, 512], F32) for _ in range(NS)]
        psO = [P.ps([128, 65], F32) for _ in range(2)]
        psG = P.ps([128, NBP], F32)
        psBT = P.ps([128, 128], BF16)
        gt = [P.sb([128, NBP], F32) for _ in range(2)]
        bq = [P.sb([128, 128], BF16) for _ in range(2)]
        mx8 = [P.sb([128, 8], F32) for _ in range(2)]
        lt = [P.sb([128, 1], F32) for _ in range(2)]
        Pb = [P.sb([128, 512], BF16) for _ in range(NS)]
        Ot = [P.sb([128, NH * 64], F32) for _ in range(2)]
        cnt = dict(s=1)

        def cast_in(dst_fn, src_fn, rows, total, wkey):
            CH = 2048
            for o in range(0, total, CH):
                n = min(CH, total - o)
                b = cnt['s'] % 2
                cnt['s'] += 1
                P.dma(stg[b][0:rows, 0:n], src_fn(o, n), writes=[f'stg{b}'])
                P.op('pool', lambda e, b=b, o=o, n=n: e.tensor_copy(out=dst_fn(o, n), in_=stg[b][0:rows, 0:n]), reads=[f'stg{b}'], writes=[wkey])

        for h in range(NH):
            r0 = h * 64
            cast_in(lambda o, n: qb[:, o:o + n], lambda o, n: QT[r0:r0 + 64, o:o + n], 64, NB * 128, 'qb')
            cast_in(lambda o, n: kb[:, o:o + n], lambda o, n: KT[r0:r0 + 64, o:o + n], 64, NK, 'kb')
            cast_in(lambda o, n: kmb[:, o:o + n], lambda o, n: KM[r0:r0 + 64, o:o + n], 64, NB, 'kmb')
            vsrc = V[:, r0:r0 + 64].rearrange("(t p) d -> p t d", p=128)
            TCH = 32
            for o in range(0, NB * 2, TCH):
                n = min(TCH, NB * 2 - o)
                b = cnt['s'] % 2
                cnt['s'] += 1
                sv = stg[b][:, 0:n * 64].rearrange("p (t d) -> p t d", d=64)
                P.dma(sv, vsrc[:, o:o + n, :], writes=[f'stg{b}'])
                P.op('pool', lambda e, sv=sv, o=o, n=n: e.tensor_copy(out=vb[:, o:o + n, 0:64], in_=sv), reads=[f'stg{b}'], writes=['vb'])
            P.op('pool', lambda e: e.memset(QB[64:128, :], 0.0), writes=['biasT'])
            for n in range(4, NB):
                gp = n % 2
                qt = qb[:, n * 128:(n + 1) * 128]
                P.op('pool', lambda e, gp=gp: e.memset(gt[gp][:], -1e30), writes=[f'gt{gp}'])
                P.op('pool', lambda e, gp=gp: e.memset(bq[gp][:], 0.0), writes=[f'bq{gp}'])
                P.op('pe', lambda e, qt=qt, n=n: e.matmul(psG[:, 0:n], lhsT=qt, rhs=kmb[:, 0:n], start=True, stop=True), reads=['qb', 'kmb'], writes=['psG'])
                P.op('act', lambda e, n=n, gp=gp: e.activation(out=gt[gp][:, 0:n], in_=psG[:, 0:n], func=AF.Copy), reads=['psG', f'gt{gp}'], writes=[f'gt{gp}'])
                dve(lambda e, gp=gp: e.max(out=mx8[gp][:], in_=gt[gp][:]), [f'gt{gp}'], [f'mx8{gp}'])
                dve(lambda e, n=n, gp=gp: e.tensor_scalar(out=gt[gp][:, 0:n], in0=gt[gp][:, 0:n], scalar1=mx8[gp][:, 2:3], scalar2=None, op0=ALU.is_ge),
                    [f'gt{gp}', f'mx8{gp}'], [f'gt{gp}'])
                dve(lambda e, n=n, gp=gp: e.tensor_scalar(out=bq[gp][:, 64:64 + n], in0=gt[gp][:, 0:n], scalar1=30000.0, scalar2=-30000.0, op0=ALU.mult, op1=ALU.add),
                    [f'gt{gp}', f'bq{gp}'], [f'bq{gp}'])
                P.op('pe', lambda e, gp=gp: e.transpose(out=psBT[:, :], in_=bq[gp][:, :], identity=c.ident[:]), reads=[f'bq{gp}', 'ident'], writes=['psBT'])
                P.op('act', lambda e, n=n: e.activation(out=biasT[:, n, :], in_=psBT[64:128, :], func=AF.Copy), reads=['psBT', 'biasT'], writes=[('biasT', n)])
            items = []
            for n in range(NB):
                groups = [(g * 2, min(2, n - g * 2)) for g in range((n + 1) // 2)]
                nt_total = sum(nb_ * 2 for (_, nb_) in groups) + 2
                base = 0
                for (b0, nb_) in groups:
                    items.append(dict(n=n, own=False, b0=b0, nb=nb_, t0=base, nt=nt_total))
                    base += nb_ * 2
                items.append(dict(n=n, own=True, b0=n, nb=1, t0=base, nt=nt_total))

            def stageA(i, it):
                sb_ = i % NS
                n = it['n']
                qt = qb[:, n * 128:(n + 1) * 128]
                ntk = it['nb'] * 2
                kt0 = it['b0'] * 2
                for j in range(ntk):
                    kt = kt0 + j
                    if it['own']:
                        P.op('pe', lambda e, qt=qt, kt=kt, j=j, sb_=sb_: e.matmul(psS[sb_][:, j * 128:(j + 1) * 128], lhsT=kb[:, kt * 128:(kt + 1) * 128], rhs=qt, start=True, stop=False),
                             reads=['qb', 'kb'], writes=[(f'psS{sb_}', j)])
                        P.op('pe', lambda e, j=j, sb_=sb_: e.matmul(psS[sb_][:, j * 128:(j + 1) * 128], lhsT=c.ident[:], rhs=cmT[:, j, :], start=False, stop=True),
                             reads=['ident', 'cmT'], writes=[(f'psS{sb_}', j)])
                    else:
                        P.op('pe', lambda e, kt=kt, j=j, sb_=sb_, n=n: e.matmul(psS[sb_][:, j * 128:(j + 1) * 128], lhsT=KE[:, kt * 128:(kt + 1) * 128], rhs=QB[:, n * 128:(n + 1) * 128], start=True, stop=True),
                             reads=['qb', 'kb', 'E', ('biasT', n), 'biasT'], writes=[(f'psS{sb_}', j)])
                P.op('act', lambda e, sb_=sb_, ntk=ntk: e.activation(out=Pb[sb_][:, 0:ntk * 128], in_=psS[sb_][:, 0:ntk * 128], func=AF.Exp),
                     reads=[(f'psS{sb_}', j) for j in range(ntk)], writes=[f'Pb{sb_}'])

            def stageC(i, it):
                sb_ = i % NS
                n = it['n']
                ntk = it['nb'] * 2
                kt0 = it['b0'] * 2
                ob2 = n % 2
                for j in range(ntk):
                    idx = it['t0'] + j
                    P.op('pe', lambda e, sb_=sb_, j=j, kt=kt0 + j, idx=idx, nt=it['nt'], ob2=ob2: e.matmul(psO[ob2][:], lhsT=Pb[sb_][:, j * 128:(j + 1) * 128], rhs=vb[:, kt, :], start=(idx == 0), stop=(idx == nt - 1)),
                         reads=[f'Pb{sb_}', 'vb'], writes=[f'psO{ob2}'])
                if it['own']:
                    dve(lambda e, ob2=ob2: e.reciprocal(out=lt[ob2][:], in_=psO[ob2][:, 64:65]), [f'psO{ob2}'], [f'lt{ob2}'])
                    P.op('act', lambda e, ob2=ob2, h=h: e.activation(out=Ot[ob2][:, h * 64:(h + 1) * 64], in_=psO[ob2][:, 0:64], func=AF.Copy, scale=lt[ob2][:, 0:1]),
                         reads=[f'psO{ob2}', f'lt{ob2}'], writes=[(f'Ot{ob2}', h)])
                    P.dma(O[n * 128:(n + 1) * 128, h * 64:(h + 1) * 64], Ot[ob2][:, h * 64:(h + 1) * 64], reads=[(f'Ot{ob2}', h)], final=True)

            for i in range(len(items) + 1):
                if i < len(items):
                    stageA(i, items[i])
                if i >= 1:
                    stageC(i - 1, items[i - 1])
        P.emit()
        print("moba ops", {e: len(P.ops[e]) for e in P.ENGS}, "waits", P.nwaits)
    return nc


def _run(nc, maps):
    return run_bass_kernel_spmd(nc, maps, core_ids=list(range(8))).results


def _c(a):
    return np.ascontiguousarray(a, dtype=np.float32)


def _pack_wup(w):
    w3 = w.reshape(8, 128, 5632)
    g = w3[:, :, :2816].reshape(8, 128, 22, 128)
    v = w3[:, :, 2816:].reshape(8, 128, 22, 128)
    a = np.concatenate([g, v], -1)
    return _c(a.transpose(2, 1, 0, 3).reshape(22, 128, 8 * 256))


def _perm(w):
    w = w.reshape(1024, 12, 2, 32)[:, :, ::-1, :]
    return _c(w.reshape(1024, 768))


def kernel(**inp):
    inp = {k: np.asarray(v, dtype=np.float32) for k, v in inp.items()}
    x = inp['x']
    B, S, NT = 2, 16384, 4096
    maps = []
    for i in range(8):
        b, gs = i // 4, i % 4
        G0 = 12 * gs
        maps.append(dict(xb=_c(x[b]), ln=_c(inp['ln_mix'][0]), win=_c(inp['w_in'][0][:, 192 * gs:192 * gs + 192]),
                         lre=_c(inp['s5_lambda_re'][0, G0:G0 + 12]), lim=_c(inp['s5_lambda_im'][0, G0:G0 + 12]), lst=_c(inp['s5_log_step'][0, G0:G0 + 12]),
                         bre=_c(inp['s5_b_re'][0, G0:G0 + 12]), bim=_c(inp['s5_b_im'][0, G0:G0 + 12]), cre=_c(inp['s5_c_re'][0, G0:G0 + 12]),
                         cim=_c(inp['s5_c_im'][0, G0:G0 + 12]), dd=_c(inp['s5_d'][0, G0:G0 + 12])))
    res = _run(build_s5(S), maps)
    yT = [np.concatenate([res[b * 4 + gs]['yT'] for gs in range(4)], 0) for b in range(B)]
    nc_tail_glu = build_tail(NT, True)
    maps = []
    for i in range(8):
        b, r = i // 4, i % 4
        sl = slice(r * NT, (r + 1) * NT)
        maps.append(dict(x=_c(x[b, sl]), sT=_c(yT[b][:, sl]), mem=_c(inp['mem'][b]), ln=_c(inp['ln_mix'][0]), wq=_c(inp['w_in'][0][:, 768:]),
                         mng=_c(inp['mem_norm'][0]), wmkv=_c(inp['w_mem_kv'][0]), wglu=_c(inp['s5_w_glu'][0]), wout=_c(inp['w_out'][0])))
    res = _run(nc_tail_glu, maps)
    xmid = np.stack([np.concatenate([res[b * 4 + r]['y'] for r in range(4)], 0) for b in range(B)])

    def ffn(xm, l, final):
        maps = []
        for i in range(8):
            b, r = i // 4, i % 4
            sl = slice(r * NT, (r + 1) * NT)
            xh = np.zeros((2, 1024), np.float32) if r == 0 else xm[b, r * NT - 2:r * NT]
            maps.append(dict(xm=_c(xm[b, sl]), xh=_c(xh), ln=_c(inp['ln_ffn'][l]), wup=_pack_wup(inp['w_up'][l]), cw=_c(inp['conv_w'][l]), cb=_c(inp['conv_b'][l]),
                             wdown=_c(inp['w_down'][l]), fng=_c(inp['final_norm'])))
        res = _run(build_ffn(NT, final), maps)
        return np.stack([np.concatenate([res[b * 4 + r]['y'] for r in range(4)], 0) for b in range(B)])

    x1 = ffn(xmid, 0, False)
    wq = _c(inp['w_in'][1][:, :768]); wk = _c(inp['w_kv'][:, :768]); wv = _c(inp['w_kv'][:, 768:])
    wqb, wkb = _perm(wq), _perm(wk)
    maps = []
    for i in range(8):
        b, r = i // 4, i % 4
        sl = slice(r * NT, (r + 1) * NT)
        maps.append(dict(x=_c(x1[b, sl]), ln=_c(inp['ln_mix'][1]), kvn=_c(inp['kv_norm']), wqa=wq, wqb=wqb, wka=wk, wkb=wkb, wv=wv,
                         pos=np.arange(r * NT, (r + 1) * NT, dtype=np.float32)))
    res = _run(build_qkv(NT), maps)
    QT = [np.concatenate([res[b * 4 + r]['QT'] for r in range(4)], 1) for b in range(B)]
    KT = [np.concatenate([res[b * 4 + r]['KT'] for r in range(4)], 1) for b in range(B)]
    V = [np.concatenate([res[b * 4 + r]['V'] for r in range(4)], 0) for b in range(B)]
    KM = [np.concatenate([res[b * 4 + r]['KM'] for r in range(4)], 1) for b in range(B)]
    NB = S // 256
    maps = []
    colsj = [np.concatenate([np.arange(n * 256 + j * 128, n * 256 + j * 128 + 128) for n in range(NB)]) for j in range(2)]
    for i in range(8):
        b, j, hh = i // 4, (i % 4) % 2, (i % 4) // 2
        rows = slice(hh * 384, hh * 384 + 384)
        cm = np.where(np.arange(256)[None, :] <= (128 * j + np.arange(128))[:, None], 0.0, -30000.0).astype(np.float32)
        maps.append(dict(QT=_c(QT[b][rows][:, colsj[j]]), KT=_c(KT[b][rows]), V=_c(V[b][:, rows]), KM=_c(KM[b][rows]), cmaskT=_c(cm.T)))
    res = _run(build_moba(NB, 6), maps)
    attn = np.zeros((B, S, 768), np.float32)
    for i in range(8):
        b, j, hh = i // 4, (i % 4) % 2, (i % 4) // 2
        attn[b][colsj[j], hh * 384:hh * 384 + 384] = res[i]['O']
    maps = []
    for i in range(8):
        b, r = i // 4, i % 4
        sl = slice(r * NT, (r + 1) * NT)
        maps.append(dict(x=_c(x1[b, sl]), sT=_c(attn[b, sl].T), mem=_c(inp['mem'][b]), ln=_c(inp['ln_mix'][1]), wq=_c(inp['w_in'][1][:, 768:]),
                         mng=_c(inp['mem_norm'][1]), wmkv=_c(inp['w_mem_kv'][1]), wglu=_c(inp['s5_w_glu'][0]), wout=_c(inp['w_out'][1])))
    res = _run(build_tail(NT, False), maps)
    xmid1 = np.stack([np.concatenate([res[b * 4 + r]['y'] for r in range(4)], 0) for b in range(B)])
    out = ffn(xmid1, 1, True)
    return out.astype(np.float32)
```

```python
import contextlib
import numpy as np
import concourse.bass as bass
import concourse.mybir as mybir
from concourse.bass_utils import run_bass_kernel_spmd

F32 = mybir.dt.float32
BF16 = mybir.dt.bfloat16
AF = mybir.ActivationFunctionType
ALU = mybir.AluOpType
AX = mybir.AxisListType


class Prog:
    ENGS = ['pe', 'act', 'dve', 'pool', 'sp']
    NSLOT = 8

    def __init__(self, nc, stack, sync_same=('act', 'dve', 'pool')):
        self.nc = nc
        self.stack = stack
        self.ops = {e: [] for e in self.ENGS}
        self.lw = {}
        self.rd = {}
        self.sync_same = set(sync_same)
        self.ndma = {e: 0 for e in self.ENGS}
        self._n = 0
        self.final = []

    def sb(self, shape, dt, name=None):
        self._n += 1
        return self.stack.enter_context(self.nc.sbuf_tensor(name or f"sb{self._n}", list(shape), dt))

    def ps(self, shape, dt, name=None):
        self._n += 1
        return self.stack.enter_context(self.nc.psum_tensor(name or f"ps{self._n}", list(shape), dt))

    def op(self, eng, fn, reads=(), writes=(), dma=False):
        idx = len(self.ops[eng])
        deps = set()
        for k in reads:
            if k in self.lw:
                deps.add(self.lw[k])
        for k in writes:
            if k in self.lw:
                deps.add(self.lw[k])
            for r in self.rd.get(k, ()):
                deps.add(r)
        deps.discard((eng, idx))
        rec = dict(fn=fn, deps=deps, needed=False, dma=dma, eng=eng)
        if dma:
            rec['dma_i'] = self.ndma[eng]
            self.ndma[eng] += 1
        self.ops[eng].append(rec)
        for k in writes:
            self.lw[k] = (eng, idx)
            self.rd[k] = []
        for k in reads:
            self.rd.setdefault(k, []).append((eng, idx))
        return (eng, idx)

    def dma(self, out, in_, reads=(), writes=(), eng='sp', final=False, **kw):
        r = self.op(eng, lambda e: e.dma_start(out=out, in_=in_, **kw), reads, writes, dma=True)
        if final:
            self.final.append(r)
        return r

    def emit(self):
        nc = self.nc
        for e in self.ENGS:
            for i, rec in enumerate(self.ops[e]):
                nd = set()
                for (de, di) in rec['deps']:
                    drec = self.ops[de][di]
                    if de == e and not drec['dma']:
                        if rec['dma']:
                            pass
                        elif e not in self.sync_same:
                            continue
                    nd.add((de, di))
                rec['deps'] = nd
                for (de, di) in nd:
                    self.ops[de][di]['needed'] = True
        final = list(self.final)
        for (de, di) in final:
            self.ops[de][di]['needed'] = True
        csem = {e: self.stack.enter_context(nc.semaphore(f"c_{e}")) for e in self.ENGS}
        dsem = {}
        for e in self.ENGS:
            if self.ndma[e]:
                dsem[e] = [self.stack.enter_context(nc.semaphore(f"d_{e}{s}")) for s in range(self.NSLOT)]
        for e in self.ENGS:
            c = 0
            for rec in self.ops[e]:
                if rec['dma']:
                    s = rec['dma_i'] % self.NSLOT
                    rec['sem'] = dsem[e][s]
                    rec['val'] = 16 * (rec['dma_i'] // self.NSLOT + 1)
                else:
                    if rec['needed']:
                        c += 1
                    rec['sem'] = csem[e]
                    rec['val'] = c
        engobj = {'pe': 'tensor', 'act': 'scalar', 'dve': 'vector', 'pool': 'gpsimd', 'sp': 'sync'}
        nwaits = [0]

        def run(e, eng):
            wm = {}

            def wait(sem, val):
                k = id(sem)
                if wm.get(k, 0) >= val:
                    return
                eng.wait_ge(sem, val)
                wm[k] = val
                nwaits[0] += 1

            for rec in self.ops[e]:
                for (de, di) in sorted(rec['deps']):
                    d = self.ops[de][di]
                    wait(d['sem'], d['val'])
                if rec['dma'] and rec['val'] > 16:
                    wait(rec['sem'], rec['val'] - 16)
                ins = rec['fn'](eng)
                if rec['dma']:
                    ins.then_inc(rec['sem'], 16)
                elif rec['needed']:
                    ins.then_inc(rec['sem'], 1)
            if e == 'sp':
                for (de, di) in final:
                    d = self.ops[de][di]
                    wait(d['sem'], d['val'])

        with nc.Block() as block:
            for e in self.ENGS:
                if not self.ops[e] and e != 'sp':
                    continue
                getattr(block, engobj[e])(lambda eng, e=e: run(e, eng))
        self.nwaits = nwaits[0]


D = 1024
DFF = 2816
NCH = DFF // 128
EPS = 1e-6


class Ctx:
    pass


def setup_common(P):
    c = Ctx()
    c.ident = P.sb([128, 128], BF16, "ident")
    P.op('pool', lambda e: e.memset(c.ident[:], 0.0), writes=['ident'])
    P.op('pool', lambda e: e.affine_select(out=c.ident[:], in_=c.ident[:], pattern=[[-1, 128]],
                                           compare_op=ALU.not_equal, fill=1.0, base=0, channel_multiplier=1),
         reads=['ident'], writes=['ident'])
    c.n = 0
    return c


def load_bcast(P, dram_vec, n, name):
    t = P.sb([128, n], F32, name)
    P.dma(t[:], dram_vec.partition_broadcast(128), writes=[name])
    return t


def norm_T(P, c, xt, xkey, np_, gain_bc, gkey, hT, hkey, col0, tmp):
    i = c.n
    c.n += 1
    b = i % 2
    junk, ss, xb, pst = tmp['junk'][b], tmp['ss'][b], tmp['xb'][b], tmp['pst'][b]
    kj, ks, kx, kp = f'junk{b}', f'ss{b}', f'xb{b}', f'pst{b}'
    P.op('act', lambda e: e.activation(out=junk[:np_, :], in_=xt, func=AF.Square, accum_out=ss[:np_, :]),
         reads=[xkey], writes=[kj, ks])
    P.op('dve', lambda e: e.tensor_scalar(out=ss[:np_, :], in0=ss[:np_, :], scalar1=1.0 / D, scalar2=EPS, op0=ALU.mult, op1=ALU.add),
         reads=[ks], writes=[ks])
    P.op('act', lambda e: e.activation(out=ss[:np_, :], in_=ss[:np_, :], func=AF.Sqrt), reads=[ks], writes=[ks])
    P.op('dve', lambda e: e.reciprocal(out=ss[:np_, :], in_=ss[:np_, :]), reads=[ks], writes=[ks])
    P.op('dve', lambda e: e.scalar_tensor_tensor(out=xb[:np_, :], in0=xt, scalar=ss[:np_, 0:1], in1=gain_bc[:np_, :],
                                                 op0=ALU.mult, op1=ALU.mult), reads=[xkey, ks, gkey], writes=[kx])
    for k in range(8):
        P.op('pe', lambda e, k=k: e.transpose(out=pst[:, k * 128:k * 128 + np_], in_=xb[:np_, k * 128:(k + 1) * 128],
                                              identity=c.ident[:np_, :np_]), reads=[kx, 'ident'], writes=[(kp, k)])
    P.op('act', lambda e: e.activation(out=hT[:, :, col0:col0 + np_],
                                       in_=pst[:].rearrange("p (k t) -> p k t", k=8)[:, :, :np_], func=AF.Copy),
         reads=[(kp, k) for k in range(8)], writes=[hkey])
    return ss


def make_norm_tmp(P):
    return dict(junk=[P.sb([128, D], F32) for _ in range(2)], ss=[P.sb([128, 1], F32) for _ in range(2)],
                xb=[P.sb([128, D], BF16) for _ in range(2)], pst=[P.ps([128, 1024], BF16) for _ in range(2)])


def build_ffn(NT, final_norm):
    nc = bass.Bass("TRN2", target_bir_lowering=False)
    xm = nc.dram_tensor("xm", [NT, D], F32, kind="ExternalInput").ap()
    xh = nc.dram_tensor("xh", [2, D], F32, kind="ExternalInput").ap()
    ln = nc.dram_tensor("ln", [D], F32, kind="ExternalInput").ap()
    wup = nc.dram_tensor("wup", [NCH, 128, 8 * 256], F32, kind="ExternalInput").ap()
    cw = nc.dram_tensor("cw", [3, 2 * DFF], F32, kind="ExternalInput").ap()
    cb = nc.dram_tensor("cb", [2 * DFF], F32, kind="ExternalInput").ap()
    wdown = nc.dram_tensor("wdown", [DFF, D], F32, kind="ExternalInput").ap()
    fng = nc.dram_tensor("fng", [D], F32, kind="ExternalInput").ap()
    y = nc.dram_tensor("y", [NT, D], F32, kind="ExternalOutput").ap()
    T = 512
    ntile = NT // T
    with contextlib.ExitStack() as st:
        P = Prog(nc, st)
        c = setup_common(P)
        tmp = make_norm_tmp(P)
        gain = load_bcast(P, ln, D, "gain")
        fgain = load_bcast(P, fng, D, "fgain") if final_norm else None
        cwt = P.sb([128, 3, 2 * NCH], F32, "cwt")
        cbt = P.sb([128, 2 * NCH], F32, "cbt")
        P.dma(cwt[:], cw.rearrange("t (c p) -> p t c", p=128), writes=['cwt'], allow_slow_non_contiguous=True)
        P.dma(cbt[:], cb.rearrange("(c p) -> p c", p=128), writes=['cbt'], allow_slow_non_contiguous=True)
        wd = P.sb([128, NCH, D], BF16, "wd")
        wdst = [P.sb([128, D], F32) for _ in range(2)]
        for j in range(NCH):
            b = j % 2
            P.dma(wdst[b][:], wdown[j * 128:(j + 1) * 128, :], writes=[f'wdst{b}'])
            P.op('pool', lambda e, j=j, b=b: e.tensor_copy(out=wd[:, j, :], in_=wdst[b][:]), reads=[f'wdst{b}'], writes=[('wd', j)])
        xtt = [P.sb([128, D], F32) for _ in range(2)]
        xres = P.sb([128, 4, D], F32, "xres")
        hT = P.sb([128, 8, T], BF16, "hT")
        hTh = P.sb([128, 8, 2], BF16, "hTh")
        Hs = P.sb([128, 2 * NCH, 2], F32, "Hs")
        wst = [P.sb([128, 8, 256], F32) for _ in range(2)]
        wbf = [P.sb([128, 8, 256], BF16) for _ in range(2)]
        Ug = [P.sb([128, T + 2], F32) for _ in range(2)]
        Uv = [P.sb([128, T + 2], F32) for _ in range(2)]
        ag = [P.sb([128, T], F32) for _ in range(2)]
        av = [P.sb([128, T], F32) for _ in range(2)]
        hid = P.sb([128, NCH, T], BF16, "hid")
        psg = [P.ps([128, T], F32) for _ in range(2)]
        psv = [P.ps([128, T], F32) for _ in range(2)]
        pso = [P.ps([128, 512], F32) for _ in range(2)]
        yo = [P.sb([128, D], F32) for _ in range(2)]
        junk2 = P.sb([128, D], F32, "junk2")
        ss2 = [P.sb([128, 1], F32) for _ in range(2)]
        cnt = dict(w=0, o=0, x=0)

        def load_w(j):
            b = cnt['w'] % 2
            cnt['w'] += 1
            P.dma(wst[b][:].rearrange("p k n -> p (k n)"), wup[j], writes=[(f'wst{b}', 0), (f'wst{b}', 1)])
            P.op('pool', lambda e: e.tensor_copy(out=wbf[b][:], in_=wst[b][:]), reads=[(f'wst{b}', 0), (f'wst{b}', 1)], writes=[f'wbf{b}'])
            return b

        b0 = cnt['x'] % 2
        cnt['x'] += 1
        P.dma(xtt[b0][:2, :], xh, writes=[f'xtt{b0}'])
        norm_T(P, c, xtt[b0][:2, :], f'xtt{b0}', 2, gain, 'gain', hTh, 'hTh', 0, tmp)
        for j in range(NCH):
            wb = load_w(j)
            pb = j % 2
            for k in range(8):
                P.op('pe', lambda e, k=k, wb=wb, pb=pb: e.matmul(psg[pb][:, 0:2], lhsT=wbf[wb][:, k, 0:128], rhs=hTh[:, k, :], start=(k == 0), stop=(k == 7)),
                     reads=[f'wbf{wb}', 'hTh'], writes=[f'psg{pb}'])
            for k in range(8):
                P.op('pe', lambda e, k=k, wb=wb, pb=pb: e.matmul(psv[pb][:, 0:2], lhsT=wbf[wb][:, k, 128:256], rhs=hTh[:, k, :], start=(k == 0), stop=(k == 7)),
                     reads=[f'wbf{wb}', 'hTh'], writes=[f'psv{pb}'])
            P.op('act', lambda e, j=j, pb=pb: e.activation(out=Hs[:, j, :], in_=psg[pb][:, 0:2], func=AF.Copy), reads=[f'psg{pb}'], writes=[('Hs', j)])
            P.op('act', lambda e, j=j, pb=pb: e.activation(out=Hs[:, NCH + j, :], in_=psv[pb][:, 0:2], func=AF.Copy), reads=[f'psv{pb}'], writes=[('Hs', NCH + j)])

        for ti in range(ntile):
            t0 = ti * T
            for s in range(4):
                bx = cnt['x'] % 2
                cnt['x'] += 1
                P.dma(xtt[bx][:], xm[t0 + s * 128:t0 + (s + 1) * 128, :], writes=[f'xtt{bx}'])
                P.op('pool', lambda e, s=s, bx=bx: e.tensor_copy(out=xres[:, s, :], in_=xtt[bx][:]), reads=[f'xtt{bx}'], writes=[('xres', s)])
                norm_T(P, c, xtt[bx][:], f'xtt{bx}', 128, gain, 'gain', hT, ('hT', s), s * 128, tmp)
            hkeys = [('hT', s) for s in range(4)]
            for j in range(NCH):
                wb = load_w(j)
                pb = j % 2
                for k in range(8):
                    P.op('pe', lambda e, k=k, wb=wb, pb=pb: e.matmul(psg[pb][:], lhsT=wbf[wb][:, k, 0:128], rhs=hT[:, k, :], start=(k == 0), stop=(k == 7)),
                         reads=[f'wbf{wb}'] + hkeys, writes=[f'psg{pb}'])
                for k in range(8):
                    P.op('pe', lambda e, k=k, wb=wb, pb=pb: e.matmul(psv[pb][:], lhsT=wbf[wb][:, k, 128:256], rhs=hT[:, k, :], start=(k == 0), stop=(k == 7)),
                         reads=[f'wbf{wb}'] + hkeys, writes=[f'psv{pb}'])
                for (U, ps, acc, ch, nm, eng2) in ((Ug[pb], psg[pb], ag[pb], j, 'g', 'dve'), (Uv[pb], psv[pb], av[pb], NCH + j, 'v', 'dve')):
                    uk, pk, ak = f'U{nm}{pb}', f'ps{nm}{pb}', f'a{nm}{pb}'
                    P.op('pool', lambda e, U=U, ch=ch: e.tensor_copy(out=U[:, 0:2], in_=Hs[:, ch, :]), reads=[('Hs', ch)], writes=[(uk, 0)])
                    P.op('act', lambda e, U=U, ps=ps: e.activation(out=U[:, 2:T + 2], in_=ps[:], func=AF.Copy), reads=[pk], writes=[(uk, 1)])
                    P.op('pool', lambda e, U=U, ch=ch: e.tensor_copy(out=Hs[:, ch, :], in_=U[:, T:T + 2]), reads=[(uk, 1)], writes=[('Hs', ch)])
                    P.op('dve', lambda e, U=U, acc=acc, ch=ch: e.tensor_scalar(out=acc[:], in0=U[:, 2:T + 2], scalar1=cwt[:, 2, ch:ch + 1], scalar2=cbt[:, ch:ch + 1],
                                                                             op0=ALU.mult, op1=ALU.add), reads=[(uk, 1), 'cwt', 'cbt'], writes=[ak])
                    P.op(eng2, lambda e, U=U, acc=acc, ch=ch: e.scalar_tensor_tensor(out=acc[:], in0=U[:, 1:T + 1], scalar=cwt[:, 1, ch:ch + 1], in1=acc[:],
                                                                                    op0=ALU.mult, op1=ALU.add), reads=[(uk, 0), (uk, 1), 'cwt', ak], writes=[ak])
                    P.op(eng2, lambda e, U=U, acc=acc, ch=ch: e.scalar_tensor_tensor(out=acc[:], in0=U[:, 0:T], scalar=cwt[:, 0, ch:ch + 1], in1=acc[:],
                                                                                    op0=ALU.mult, op1=ALU.add), reads=[(uk, 0), (uk, 1), 'cwt', ak], writes=[ak])
                P.op('act', lambda e, pb=pb: e.activation(out=ag[pb][:], in_=ag[pb][:], func=AF.Silu), reads=[f'ag{pb}'], writes=[f'ag{pb}'])
                P.op('dve', lambda e, pb=pb, j=j: e.tensor_tensor(out=hid[:, j, :], in0=ag[pb][:], in1=av[pb][:], op=ALU.mult),
                     reads=[f'ag{pb}', f'av{pb}'], writes=[('hid', j)])
            for s in range(4):
                ob = cnt['o'] % 2
                cnt['o'] += 1
                for half in range(2):
                    pb = half
                    for j in range(NCH):
                        P.op('pe', lambda e, j=j, s=s, half=half, pb=pb: e.matmul(pso[pb][:], lhsT=hid[:, j, s * 128:(s + 1) * 128],
                                                                                rhs=wd[:, j, half * 512:(half + 1) * 512], start=(j == 0), stop=(j == NCH - 1)),
                             reads=[('hid', j), ('wd', j)], writes=[f'pso{pb}'])
                    P.op('dve', lambda e, s=s, half=half, pb=pb, ob=ob: e.tensor_tensor(out=yo[ob][:, half * 512:(half + 1) * 512], in0=pso[pb][:],
                                                                                      in1=xres[:, s, half * 512:(half + 1) * 512], op=ALU.add),
                         reads=[f'pso{pb}', ('xres', s)], writes=[(f'yo{ob}', half)])
                yk = [(f'yo{ob}', 0), (f'yo{ob}', 1)]
                if final_norm:
                    sb_ = ss2[ob]
                    sk = f'ss2{ob}'
                    P.op('act', lambda e, ob=ob, sb_=sb_: e.activation(out=junk2[:], in_=yo[ob][:], func=AF.Square, accum_out=sb_[:]), reads=yk, writes=['junk2', sk])
                    P.op('dve', lambda e, sb_=sb_: e.tensor_scalar(out=sb_[:], in0=sb_[:], scalar1=1.0 / D, scalar2=EPS, op0=ALU.mult, op1=ALU.add), reads=[sk], writes=[sk])
                    P.op('act', lambda e, sb_=sb_: e.activation(out=sb_[:], in_=sb_[:], func=AF.Sqrt), reads=[sk], writes=[sk])
                    P.op('dve', lambda e, sb_=sb_: e.reciprocal(out=sb_[:], in_=sb_[:]), reads=[sk], writes=[sk])
                    P.op('dve', lambda e, ob=ob, sb_=sb_: e.scalar_tensor_tensor(out=yo[ob][:], in0=yo[ob][:], scalar=sb_[:, 0:1], in1=fgain[:], op0=ALU.mult, op1=ALU.mult),
                         reads=yk + [sk, 'fgain'], writes=yk)
                P.dma(y[t0 + s * 128:t0 + (s + 1) * 128, :], yo[ob][:], reads=yk, final=True)
        P.emit()
        print("ffn ops", {e: len(P.ops[e]) for e in P.ENGS}, "waits", P.nwaits)
    return nc


SW = 768


def load_w_bf16(P, wdram_view, shape, name, parts=128):
    w = P.sb(shape, BF16, "sbw_" + name)
    if not hasattr(P, "_stg"):
        P._stg = P.sb([128, 6144], F32, "wstage")
    stg = P._stg[0:shape[0], 0:shape[1] * shape[2]].rearrange("p (k n) -> p k n", k=shape[1])
    P.dma(stg, wdram_view, writes=["wstage"])
    P.op('pool', lambda e: e.tensor_copy(out=w[:], in_=stg), reads=["wstage"], writes=[name])
    return w


def build_tail(NT, glu):
    nc = bass.Bass("TRN2", target_bir_lowering=False)
    x = nc.dram_tensor("x", [NT, D], F32, kind="ExternalInput").ap()
    sT = nc.dram_tensor("sT", [SW, NT], F32, kind="ExternalInput").ap()
    mem = nc.dram_tensor("mem", [256, D], F32, kind="ExternalInput").ap()
    ln = nc.dram_tensor("ln", [D], F32, kind="ExternalInput").ap()
    wq = nc.dram_tensor("wq", [D, 256], F32, kind="ExternalInput").ap()
    mng = nc.dram_tensor("mng", [D], F32, kind="ExternalInput").ap()
    wmkv = nc.dram_tensor("wmkv", [D, 512], F32, kind="ExternalInput").ap()
    wglu = nc.dram_tensor("wglu", [SW, SW], F32, kind="ExternalInput").ap()
    wout = nc.dram_tensor("wout", [D, D], F32, kind="ExternalInput").ap()
    y = nc.dram_tensor("y", [NT, D], F32, kind="ExternalOutput").ap()
    T = 512
    ntile = NT // T
    with contextlib.ExitStack() as st:
        P = Prog(nc, st)
        c = setup_common(P)
        tmp = make_norm_tmp(P)
        gain = load_bcast(P, ln, D, "gain")
        mgain = load_bcast(P, mng, D, "mgain")
        wq_b = load_w_bf16(P, wq.rearrange("(k p) n -> p k n", p=128), [128, 8, 256], "wq")
        wm_b = load_w_bf16(P, wmkv.rearrange("(k p) n -> p k n", p=128), [128, 8, 512], "wm")
        wo_s = load_w_bf16(P, wout[0:SW, :].rearrange("(k p) n -> p k n", p=128), [128, 6, D], "wos")
        wo_m = load_w_bf16(P, wout[SW:D, :].rearrange("(h p) n -> p h n", p=64), [64, 4, D], "wom")
        wg_b = load_w_bf16(P, wglu.rearrange("(k p) n -> p k n", p=128), [128, 6, SW], "wg") if glu else None
        ones64 = P.sb([128, 64], BF16, "ones64")
        P.op('pool', lambda e: e.memset(ones64[:], 1.0), writes=['ones64'])
        psA = [P.ps([128, 512], F32) for _ in range(2)]
        psO = P.ps([128, 512], F32)
        psD = P.ps([128, 512], F32)
        pso = [P.ps([128, 512], F32) for _ in range(2)]
        xtt = [P.sb([128, D], F32) for _ in range(2)]
        xres = P.sb([128, 4, D], F32, "xres")
        hT = P.sb([128, 8, T], BF16, "hT")
        hTm = P.sb([128, 8, 256], BF16, "hTm")
        KmT = P.sb([128, 2, 256], BF16, "KmT")
        Vm = P.sb([128, 2, 256], BF16, "Vm")
        cnt = dict(x=0, a=0, o=0)
        for s in range(2):
            bx = cnt['x'] % 2
            cnt['x'] += 1
            P.dma(xtt[bx][:], mem[s * 128:(s + 1) * 128, :], writes=[f'xtt{bx}'])
            norm_T(P, c, xtt[bx][:], f'xtt{bx}', 128, mgain, 'mgain', hTm, ('hTm', s), s * 128, tmp)
        hmk = [('hTm', 0), ('hTm', 1)]
        for cch in range(2):
            pa = cnt['a'] % 2
            cnt['a'] += 1
            for k in range(8):
                P.op('pe', lambda e, k=k, pa=pa, cch=cch: e.matmul(psA[pa][:, 0:256], lhsT=wm_b[:, k, cch * 128:(cch + 1) * 128], rhs=hTm[:, k, :], start=(k == 0), stop=(k == 7)),
                     reads=['wm'] + hmk, writes=[f'psA{pa}'])
            P.op('act', lambda e, pa=pa, cch=cch: e.activation(out=KmT[:, cch, :], in_=psA[pa][:, 0:256], func=AF.Copy), reads=[f'psA{pa}'], writes=[('KmT', cch)])
        for s in range(2):
            pa = cnt['a'] % 2
            cnt['a'] += 1
            for k in range(8):
                P.op('pe', lambda e, k=k, pa=pa, s=s: e.matmul(psA[pa][:, 0:256], lhsT=hTm[:, k, s * 128:(s + 1) * 128], rhs=wm_b[:, k, 256:512], start=(k == 0), stop=(k == 7)),
                     reads=['wm'] + hmk, writes=[f'psA{pa}'])
            P.op('act', lambda e, pa=pa, s=s: e.activation(out=Vm[:, s, :], in_=psA[pa][:, 0:256], func=AF.Copy), reads=[f'psA{pa}'], writes=[('Vm', s)])
        qT = P.sb([128, 2, T], BF16, "qT")
        PT = [P.sb([128, T], BF16) for _ in range(2)]
        rec = P.sb([64, T], F32, "rec")
        memT = P.sb([64, 4, T], BF16, "memT")
        sin = P.sb([128, 6, T], F32, "sin")
        mixT = P.sb([128, 6, T], BF16, "mixT")
        if glu:
            gf = P.sb([128, 6, T], F32, "gf")
            gb = P.sb([128, 6, T], BF16, "gb")
            t1 = [P.sb([128, T], F32) for _ in range(2)]
            sg = [P.sb([128, T], F32) for _ in range(2)]
        yo = [P.sb([128, D], F32) for _ in range(2)]
        for ti in range(ntile):
            t0 = ti * T
            for s in range(4):
                bx = cnt['x'] % 2
                cnt['x'] += 1
                P.dma(xtt[bx][:], x[t0 + s * 128:t0 + (s + 1) * 128, :], writes=[f'xtt{bx}'])
                P.op('pool', lambda e, s=s, bx=bx: e.tensor_copy(out=xres[:, s, :], in_=xtt[bx][:]), reads=[f'xtt{bx}'], writes=[('xres', s)])
                norm_T(P, c, xtt[bx][:], f'xtt{bx}', 128, gain, 'gain', hT, ('hT', s), s * 128, tmp)
            hkeys = [('hT', s) for s in range(4)]
            for cch in range(2):
                pa = cnt['a'] % 2
                cnt['a'] += 1
                for k in range(8):
                    P.op('pe', lambda e, k=k, pa=pa, cch=cch: e.matmul(psA[pa][:], lhsT=wq_b[:, k, cch * 128:(cch + 1) * 128], rhs=hT[:, k, :], start=(k == 0), stop=(k == 7)),
                         reads=['wq'] + hkeys, writes=[f'psA{pa}'])
                P.op('act', lambda e, pa=pa, cch=cch: e.activation(out=qT[:, cch, :], in_=psA[pa][:], func=AF.Copy, scale=0.125), reads=[f'psA{pa}'], writes=[('qT', cch)])
            for h in range(4):
                cch, po = h // 2, (h % 2) * 64
                for ms in range(2):
                    pa = cnt['a'] % 2
                    cnt['a'] += 1
                    P.op('pe', lambda e, pa=pa, cch=cch, po=po, ms=ms: e.matmul(psA[pa][:], lhsT=KmT[po:po + 64, cch, ms * 128:(ms + 1) * 128], rhs=qT[po:po + 64, cch, :], start=True, stop=True),
                         reads=[('KmT', cch), ('qT', cch)], writes=[f'psA{pa}'])
                    P.op('act', lambda e, pa=pa, ms=ms: e.activation(out=PT[ms][:], in_=psA[pa][:], func=AF.Exp), reads=[f'psA{pa}'], writes=[f'PT{ms}'])
                for ms in range(2):
                    P.op('pe', lambda e, ms=ms, h=h: e.matmul(psO[0:64, :], lhsT=Vm[:, ms, h * 64:(h + 1) * 64], rhs=PT[ms][:], start=(ms == 0), stop=(ms == 1)),
                         reads=[('Vm', ms), f'PT{ms}'], writes=['psO'])
                for ms in range(2):
                    P.op('pe', lambda e, ms=ms: e.matmul(psD[0:64, :], lhsT=ones64[:], rhs=PT[ms][:], start=(ms == 0), stop=(ms == 1)),
                         reads=['ones64', f'PT{ms}'], writes=['psD'])
                P.op('dve', lambda e: e.reciprocal(out=rec[:], in_=psD[0:64, :]), reads=['psD'], writes=['rec'])
                P.op('dve', lambda e, h=h: e.tensor_tensor(out=memT[:, h, :], in0=psO[0:64, :], in1=rec[:], op=ALU.mult), reads=['psO', 'rec'], writes=[('memT', h)])
            for k in range(6):
                P.dma(sin[:, k, :], sT[k * 128:(k + 1) * 128, t0:t0 + T], writes=[('sin', k)])
            if glu:
                for k in range(6):
                    b = k % 2
                    P.op('act', lambda e, k=k, b=b: e.activation(out=t1[b][:], in_=sin[:, k, :], func=AF.Square), reads=[('sin', k)], writes=[f't1{b}'])
                    P.op('dve', lambda e, b=b: e.tensor_scalar(out=t1[b][:], in0=t1[b][:], scalar1=0.044715, scalar2=1.0, op0=ALU.mult, op1=ALU.add), reads=[f't1{b}'], writes=[f't1{b}'])
                    P.op('dve', lambda e, k=k, b=b: e.tensor_tensor(out=t1[b][:], in0=t1[b][:], in1=sin[:, k, :], op=ALU.mult), reads=[f't1{b}', ('sin', k)], writes=[f't1{b}'])
                    P.op('act', lambda e, b=b: e.activation(out=t1[b][:], in_=t1[b][:], func=AF.Sigmoid, scale=1.5957691216057308), reads=[f't1{b}'], writes=[f't1{b}'])
                    P.op('dve', lambda e, k=k, b=b: e.tensor_tensor(out=gf[:, k, :], in0=t1[b][:], in1=sin[:, k, :], op=ALU.mult), reads=[f't1{b}', ('sin', k)], writes=[('gf', k)])
                    P.op('pool', lambda e, k=k: e.tensor_copy(out=gb[:, k, :], in_=gf[:, k, :]), reads=[('gf', k)], writes=[('gb', k)])
                for cch in range(6):
                    pa = cnt['a'] % 2
                    cnt['a'] += 1
                    b = cch % 2
                    for k in range(6):
                        P.op('pe', lambda e, k=k, pa=pa, cch=cch: e.matmul(psA[pa][:], lhsT=wg_b[:, k, cch * 128:(cch + 1) * 128], rhs=gb[:, k, :], start=(k == 0), stop=(k == 5)),
                             reads=['wg'] + [('gb', kk) for kk in range(6)], writes=[f'psA{pa}'])
                    P.op('act', lambda e, pa=pa, b=b: e.activation(out=sg[b][:], in_=psA[pa][:], func=AF.Sigmoid), reads=[f'psA{pa}'], writes=[f'sg{b}'])
                    P.op('dve', lambda e, cch=cch, b=b: e.tensor_tensor(out=mixT[:, cch, :], in0=gf[:, cch, :], in1=sg[b][:], op=ALU.mult), reads=[('gf', cch), f'sg{b}'], writes=[('mixT', cch)])
            else:
                for k in range(6):
                    P.op('pool', lambda e, k=k: e.tensor_copy(out=mixT[:, k, :], in_=sin[:, k, :]), reads=[('sin', k)], writes=[('mixT', k)])
            for s in range(4):
                ob = cnt['o'] % 2
                cnt['o'] += 1
                for half in range(2):
                    pb = half
                    for k in range(6):
                        P.op('pe', lambda e, k=k, s=s, half=half, pb=pb: e.matmul(pso[pb][:], lhsT=mixT[:, k, s * 128:(s + 1) * 128], rhs=wo_s[:, k, half * 512:(half + 1) * 512], start=(k == 0), stop=False),
                             reads=[('mixT', k), 'wos'], writes=[f'pso{pb}'])
                    for h in range(4):
                        P.op('pe', lambda e, h=h, s=s, half=half, pb=pb: e.matmul(pso[pb][:], lhsT=memT[:, h, s * 128:(s + 1) * 128], rhs=wo_m[:, h, half * 512:(half + 1) * 512], start=False, stop=(h == 3)),
                             reads=[('memT', h), 'wom'], writes=[f'pso{pb}'])
                    P.op('dve', lambda e, s=s, half=half, pb=pb, ob=ob: e.tensor_tensor(out=yo[ob][:, half * 512:(half + 1) * 512], in0=pso[pb][:], in1=xres[:, s, half * 512:(half + 1) * 512], op=ALU.add),
                         reads=[f'pso{pb}', ('xres', s)], writes=[(f'yo{ob}', half)])
                P.dma(y[t0 + s * 128:t0 + (s + 1) * 128, :], yo[ob][:], reads=[(f'yo{ob}', 0), (f'yo{ob}', 1)], final=True)
        P.emit()
        print("tail ops", {e: len(P.ops[e]) for e in P.ENGS}, "waits", P.nwaits)
    return nc

import math

I32 = mybir.dt.int32
PI = math.pi


def build_s5(NTOK):
    nc = bass.Bass("TRN2", target_bir_lowering=False)
    xb = nc.dram_tensor("xb", [NTOK, D], F32, kind="ExternalInput").ap()
    ln = nc.dram_tensor("ln", [D], F32, kind="ExternalInput").ap()
    win = nc.dram_tensor("win", [D, 192], F32, kind="ExternalInput").ap()
    lre = nc.dram_tensor("lre", [12, 64], F32, kind="ExternalInput").ap()
    lim = nc.dram_tensor("lim", [12, 64], F32, kind="ExternalInput").ap()
    lst = nc.dram_tensor("lst", [12], F32, kind="ExternalInput").ap()
    bre = nc.dram_tensor("bre", [12, 64, 16], F32, kind="ExternalInput").ap()
    bim = nc.dram_tensor("bim", [12, 64, 16], F32, kind="ExternalInput").ap()
    cre = nc.dram_tensor("cre", [12, 16, 64], F32, kind="ExternalInput").ap()
    cim = nc.dram_tensor("cim", [12, 16, 64], F32, kind="ExternalInput").ap()
    dd = nc.dram_tensor("dd", [12, 16], F32, kind="ExternalInput").ap()
    yT = nc.dram_tensor("yT", [192, NTOK], F32, kind="ExternalOutput").ap()
    T = 512
    nfr = NTOK // T
    with contextlib.ExitStack() as st:
        P = Prog(nc, st)
        c = setup_common(P)
        tmp = make_norm_tmp(P)
        gain = load_bcast(P, ln, D, "gain")
        win_b = load_w_bf16(P, win.rearrange("(k p) n -> p k n", p=128), [128, 8, 192], "win")
        uid = [0]

        def new(shape, dt=F32):
            uid[0] += 1
            nm = f"t{uid[0]}"
            return P.sb(shape, dt, nm), nm

        def dve(fn, reads, writes):
            P.op('dve', fn, reads=reads, writes=writes)

        sc_tmp = {}

        def sincos(ang, kang, n):
            s_, ks = new([128, n])
            c_, kc = new([128, n])
            for (o, ko, sh) in ((s_, ks, 0.0), (c_, kc, 0.5 * PI)):
                if n not in sc_tmp:
                    sc_tmp[n] = (new([128, n]), new([128, n], I32), new([128, n]))
                (a_, ka), (ki, kki), (kf, kkf) = sc_tmp[n]
                dve(lambda e, a_=a_, sh=sh: e.tensor_scalar(out=a_[:], in0=ang[:], scalar1=sh, scalar2=None, op0=ALU.add), [kang], [ka])
                dve(lambda e, a_=a_, kf=kf: e.tensor_scalar(out=kf[:], in0=a_[:], scalar1=1.0 / (2 * PI), scalar2=None, op0=ALU.mult), [ka], [kkf])
                dve(lambda e, ki=ki, kf=kf: e.tensor_copy(out=ki[:], in_=kf[:]), [kkf], [kki])
                dve(lambda e, ki=ki, kf=kf: e.tensor_copy(out=kf[:], in_=ki[:]), [kki], [kkf])
                dve(lambda e, o=o, kf=kf, a_=a_: e.scalar_tensor_tensor(out=o[:], in0=kf[:], scalar=-2 * PI, in1=a_[:], op0=ALU.mult, op1=ALU.add), [kkf, ka], [ko])
                dve(lambda e, o=o, kf=kf: e.tensor_scalar(out=kf[:], in0=o[:], scalar1=PI, scalar2=2 * PI, op0=ALU.is_gt, op1=ALU.mult), [ko], [kkf])
                dve(lambda e, o=o, kf=kf: e.tensor_tensor(out=o[:], in0=o[:], in1=kf[:], op=ALU.subtract), [ko, kkf], [ko])
                dve(lambda e, o=o, kf=kf: e.tensor_scalar(out=kf[:], in0=o[:], scalar1=-PI, scalar2=2 * PI, op0=ALU.is_lt, op1=ALU.mult), [ko], [kkf])
                dve(lambda e, o=o, kf=kf: e.tensor_tensor(out=o[:], in0=o[:], in1=kf[:], op=ALU.add), [ko, kkf], [ko])
                P.op('act', lambda e, o=o: e.activation(out=o[:], in_=o[:], func=AF.Sin), reads=[ko], writes=[ko])
            return s_, ks, c_, kc

        io_i, kioi = new([128, T], I32)
        iof, kio = new([128, T])
        P.op('pool', lambda e: e.iota(io_i[:], pattern=[[1, T]], base=0, channel_multiplier=0), writes=[kioi])
        dve(lambda e: e.tensor_copy(out=iof[:], in_=io_i[:]), [kioi], [kio])
        ones_t, kones = new([128, T])
        P.op('pool', lambda e: e.memset(ones_t[:], 1.0), writes=[kones])
        dch0, kd0 = new([128, 1])
        dch1, kd1 = new([128, 1])
        ddf = dd.rearrange("g (p o) -> (g p) o", o=1)
        P.dma(dch0[:], ddf[0:128, :], writes=[kd0])
        P.dma(dch1[0:64, :], ddf[128:192, :], writes=[kd1])

        prm = []
        for pr in range(6):
            g0 = 2 * pr
            kt = pr // 4
            col0 = (32 * pr) % 128
            q = Ctx()
            lr, klr = new([128, 1])
            li, kli = new([128, 1])
            ls, kls = new([128, 1])
            P.dma(lr[:], lre[g0:g0 + 2, :].rearrange("g (n o) -> (g n) o", o=1), writes=[klr])
            P.dma(li[:], lim[g0:g0 + 2, :].rearrange("g (n o) -> (g n) o", o=1), writes=[kli])
            P.dma(ls[0:64, :], lst[g0:g0 + 1].partition_broadcast(64), writes=[(kls, 0)])
            P.dma(ls[64:128, :], lst[g0 + 1:g0 + 2].partition_broadcast(64), writes=[(kls, 1)])
            dt_, kdt = new([128, 1])
            P.op('act', lambda e, dt_=dt_, ls=ls: e.activation(out=dt_[:], in_=ls[:], func=AF.Exp), reads=[(kls, 0), (kls, 1)], writes=[kdt])
            rho, krho = new([128, 1])
            dve(lambda e, rho=rho, lr=lr, dt_=dt_: e.tensor_tensor(out=rho[:], in0=lr[:], in1=dt_[:], op=ALU.mult), [klr, kdt], [krho])
            P.op('act', lambda e, rho=rho: e.activation(out=rho[:], in_=rho[:], func=AF.Exp), reads=[krho], writes=[krho])
            th, kth = new([128, 1])
            dve(lambda e, th=th, li=li, dt_=dt_: e.tensor_tensor(out=th[:], in0=li[:], in1=dt_[:], op=ALU.mult), [kli, kdt], [kth])
            sn, ksn, cs, kcs = sincos(th, kth, 1)
            abr, kabr = new([128, 1])
            abi, kabi = new([128, 1])
            dve(lambda e, abr=abr, rho=rho, cs=cs: e.tensor_tensor(out=abr[:], in0=rho[:], in1=cs[:], op=ALU.mult), [krho, kcs], [kabr])
            dve(lambda e, abi=abi, rho=rho, sn=sn: e.tensor_tensor(out=abi[:], in0=rho[:], in1=sn[:], op=ALU.mult), [krho, ksn], [kabi])
            den, kden = new([128, 1])
            t_, kt_ = new([128, 1])
            dve(lambda e, den=den, lr=lr: e.tensor_tensor(out=den[:], in0=lr[:], in1=lr[:], op=ALU.mult), [klr], [kden])
            dve(lambda e, t_=t_, li=li: e.tensor_tensor(out=t_[:], in0=li[:], in1=li[:], op=ALU.mult), [kli], [kt_])
            dve(lambda e, den=den, t_=t_: e.tensor_tensor(out=den[:], in0=den[:], in1=t_[:], op=ALU.add), [kden, kt_], [kden])
            dve(lambda e, den=den: e.reciprocal(out=den[:], in_=den[:]), [kden], [kden])
            nr, knr = new([128, 1])
            dve(lambda e, nr=nr, abr=abr: e.tensor_scalar(out=nr[:], in0=abr[:], scalar1=-1.0, scalar2=None, op0=ALU.add), [kabr], [knr])
            cfr, kcfr = new([128, 1])
            cfi, kcfi = new([128, 1])
            ncfi, kncfi = new([128, 1])
            dve(lambda e, t_=t_, abi=abi, li=li: e.tensor_tensor(out=t_[:], in0=abi[:], in1=li[:], op=ALU.mult), [kabi, kli], [kt_])
            dve(lambda e, cfr=cfr, nr=nr, lr=lr, t_=t_: e.scalar_tensor_tensor(out=cfr[:], in0=nr[:], scalar=lr[:, 0:1], in1=t_[:], op0=ALU.mult, op1=ALU.add), [knr, klr, kt_], [kcfr])
            dve(lambda e, cfr=cfr, den=den: e.tensor_tensor(out=cfr[:], in0=cfr[:], in1=den[:], op=ALU.mult), [kcfr, kden], [kcfr])
            dve(lambda e, t_=t_, nr=nr, li=li: e.tensor_tensor(out=t_[:], in0=nr[:], in1=li[:], op=ALU.mult), [knr, kli], [kt_])
            dve(lambda e, cfi=cfi, abi=abi, lr=lr, t_=t_: e.scalar_tensor_tensor(out=cfi[:], in0=abi[:], scalar=lr[:, 0:1], in1=t_[:], op0=ALU.mult, op1=ALU.subtract), [kabi, klr, kt_], [kcfi])
            dve(lambda e, cfi=cfi, den=den: e.tensor_tensor(out=cfi[:], in0=cfi[:], in1=den[:], op=ALU.mult), [kcfi, kden], [kcfi])
            dve(lambda e, ncfi=ncfi, cfi=cfi: e.tensor_scalar(out=ncfi[:], in0=cfi[:], scalar1=-1.0, scalar2=None, op0=ALU.mult), [kcfi], [kncfi])
            Br, kBr = new([128, 16])
            Bi, kBi = new([128, 16])
            P.dma(Br[:], bre[g0:g0 + 2].rearrange("g n q -> (g n) q"), writes=[kBr])
            P.dma(Bi[:], bim[g0:g0 + 2].rearrange("g n q -> (g n) q"), writes=[kBi])
            q.LB = []
            for part in range(2):
                Bb, kBb = new([128, 16])
                if part == 0:
                    dve(lambda e, Bb=Bb, Br=Br, cfr=cfr: e.tensor_scalar(out=Bb[:], in0=Br[:], scalar1=cfr[:, 0:1], scalar2=None, op0=ALU.mult), [kBr, kcfr], [kBb])
                    dve(lambda e, Bb=Bb, Bi=Bi, ncfi=ncfi: e.scalar_tensor_tensor(out=Bb[:], in0=Bi[:], scalar=ncfi[:, 0:1], in1=Bb[:], op0=ALU.mult, op1=ALU.add), [kBi, kncfi, kBb], [kBb])
                else:
                    dve(lambda e, Bb=Bb, Bi=Bi, cfr=cfr: e.tensor_scalar(out=Bb[:], in0=Bi[:], scalar1=cfr[:, 0:1], scalar2=None, op0=ALU.mult), [kBi, kcfr], [kBb])
                    dve(lambda e, Bb=Bb, Br=Br, cfi=cfi: e.scalar_tensor_tensor(out=Bb[:], in0=Br[:], scalar=cfi[:, 0:1], in1=Bb[:], op0=ALU.mult, op1=ALU.add), [kBr, kcfi, kBb], [kBb])
                Z, kZ = new([128, 128], BF16)
                P.op('pool', lambda e, Z=Z: e.memset(Z[:], 0.0), writes=[kZ])
                dve(lambda e, Z=Z, Bb=Bb, col0=col0: e.tensor_copy(out=Z[0:64, col0:col0 + 16], in_=Bb[0:64, :]), [kBb, kZ], [kZ])
                dve(lambda e, Z=Z, Bb=Bb, col0=col0: e.tensor_copy(out=Z[64:128, col0 + 16:col0 + 32], in_=Bb[64:128, :]), [kBb, kZ], [kZ])
                LB, kLB = new([128, 128], BF16)
                pz = tmp['pst'][0]
                P.op('pe', lambda e, Z=Z, pz=pz: e.transpose(out=pz[:, 0:128], in_=Z[:], identity=c.ident[:]), reads=[kZ, 'ident'], writes=[('pst0', 0)])
                P.op('act', lambda e, LB=LB, pz=pz: e.activation(out=LB[:], in_=pz[:, 0:128], func=AF.Copy), reads=[('pst0', 0)], writes=[kLB])
                q.LB.append((LB, kLB))
            q.LC = []
            for part, src in ((0, cre), (1, cim)):
                Cf, kCf = new([128, 32])
                P.op('pool', lambda e, Cf=Cf: e.memset(Cf[:], 0.0), writes=[kCf])
                P.dma(Cf[0:64, 0:16], src[g0].rearrange("p n -> n p"), reads=[kCf], writes=[(kCf, 0)], allow_slow_non_contiguous=True)
                P.dma(Cf[64:128, 16:32], src[g0 + 1].rearrange("p n -> n p"), reads=[kCf], writes=[(kCf, 1)], allow_slow_non_contiguous=True)
                LC, kLC = new([128, 32], BF16)
                P.op('act', lambda e, LC=LC, Cf=Cf, part=part: e.activation(out=LC[:], in_=Cf[:], func=AF.Copy, scale=(1.0 if part == 0 else -1.0)),
                     reads=[kCf, (kCf, 0), (kCf, 1)], writes=[kLC])
                q.LC.append((LC, kLC))
            LD, kLD = new([128, 32], BF16)
            dch, kdch = (dch0, kd0) if kt == 0 else (dch1, kd1)
            K = 128 if kt == 0 else 64
            dve(lambda e, LD=LD, dch=dch, K=K, col0=col0: e.tensor_scalar(out=LD[0:K, :], in0=c.ident[0:K, col0:col0 + 32], scalar1=dch[0:K, 0:1], scalar2=None, op0=ALU.mult),
                ['ident', kdch], [kLD])
            q.LD, q.kLD, q.K, q.kt = LD, kLD, K, kt
            ang, kang = new([128, T])
            dve(lambda e, ang=ang, th=th: e.tensor_scalar(out=ang[:], in0=iof[:], scalar1=th[:, 0:1], scalar2=None, op0=ALU.mult), [kio, kth], [kang])
            q.st, q.kst, q.ct, q.kct = sincos(ang, kang, T)
            q.rt, q.krt = new([128, T])
            dve(lambda e, q=q, rho=rho: e.tensor_scalar(out=q.rt[:], in0=ones_t[:], scalar1=rho[:, 0:1], scalar2=None, op0=ALU.mult), [kones, krho], [q.krt])
            angT, kangT = new([128, 1])
            dve(lambda e, angT=angT, th=th: e.tensor_scalar(out=angT[:], in0=th[:], scalar1=float(T), scalar2=None, op0=ALU.mult), [kth], [kangT])
            q.sT, q.ksT, q.cT, q.kcT = sincos(angT, kangT, 1)
            q.nsT, q.knsT = new([128, 1])
            dve(lambda e, q=q: e.tensor_scalar(out=q.nsT[:], in0=q.sT[:], scalar1=-1.0, scalar2=None, op0=ALU.mult), [q.ksT], [q.knsT])
            q.wrc, q.kwrc = new([128, 1])
            q.wic, q.kwic = new([128, 1])
            P.op('pool', lambda e, q=q: e.memset(q.wrc[:], 0.0), writes=[q.kwrc])
            P.op('pool', lambda e, q=q: e.memset(q.wic[:], 0.0), writes=[q.kwic])
            q.tc_, q.ktc = new([128, 1])
            prm.append(q)

        xtt = [P.sb([128, D], F32) for _ in range(2)]
        hT = P.sb([128, 8, T], BF16, "hT")
        uT = [P.sb([128, T], BF16, "uT0"), P.sb([128, T], BF16, "uT1")]
        psu = P.ps([128, T], F32)
        psb = [[P.ps([128, T], F32) for _ in range(2)] for _ in range(2)]
        psy = P.ps([128, T], F32)
        W = []
        for b in range(2):
            w = Ctx()
            for nm in ('bur', 'bui', 'a1', 'a2', 'a3', 'a4', 'inr', 'ini', 'wr', 'wi'):
                setattr(w, nm, P.sb([128, T], F32))
            w.sr = P.sb([128, T], BF16)
            w.si = P.sb([128, T], BF16)
            w.ys = P.sb([32, T], F32)
            W.append(w)
        cnt = dict(x=0, p=0)
        for f in range(nfr):
            t0 = f * T
            for s in range(4):
                bx = cnt['x'] % 2
                cnt['x'] += 1
                P.dma(xtt[bx][:], xb[t0 + s * 128:t0 + (s + 1) * 128, :], writes=[f'xtt{bx}'])
                norm_T(P, c, xtt[bx][:], f'xtt{bx}', 128, gain, 'gain', hT, ('hT', s), s * 128, tmp)
            hkeys = [('hT', s) for s in range(4)]
            for kt, (lo, hi) in enumerate(((0, 128), (128, 192))):
                n = hi - lo
                for k in range(8):
                    P.op('pe', lambda e, k=k, lo=lo, hi=hi, n=n: e.matmul(psu[0:n, :], lhsT=win_b[:, k, lo:hi], rhs=hT[:, k, :], start=(k == 0), stop=(k == 7)),
                         reads=['win'] + hkeys, writes=['psu'])
                P.op('act', lambda e, kt=kt, n=n: e.activation(out=uT[kt][0:n, :], in_=psu[0:n, :], func=AF.Copy), reads=['psu'], writes=[f'uT{kt}'])
            for pr in range(6):
                q = prm[pr]
                b = cnt['p'] % 2
                cnt['p'] += 1
                w = W[b]
                K, kt = q.K, q.kt
                kk = lambda nm: f'{nm}{b}'
                for part, dst in ((0, w.bur), (1, w.bui)):
                    LB, kLB = q.LB[part]
                    pb = psb[b][part]
                    P.op('pe', lambda e, LB=LB, pb=pb, K=K, kt=kt: e.matmul(pb[:], lhsT=LB[0:K, :], rhs=uT[kt][0:K, :], start=True, stop=True),
                         reads=[kLB, f'uT{kt}'], writes=[f'psb{b}{part}'])
                    P.op('act', lambda e, dst=dst, pb=pb: e.activation(out=dst[:], in_=pb[:], func=AF.Copy), reads=[f'psb{b}{part}'], writes=[kk('bur' if part == 0 else 'bui')])
                P.op('pool', lambda e, w=w, q=q: e.tensor_tensor(out=w.a1[:], in0=q.ct[:], in1=w.bur[:], op=ALU.mult), reads=[q.kct, kk('bur')], writes=[kk('a1')])
                P.op('dve', lambda e, w=w, q=q: e.tensor_tensor(out=w.a2[:], in0=q.st[:], in1=w.bui[:], op=ALU.mult), reads=[q.kst, kk('bui')], writes=[kk('a2')])
                dve(lambda e, w=w: e.tensor_tensor(out=w.inr[:], in0=w.a1[:], in1=w.a2[:], op=ALU.add), [kk('a1'), kk('a2')], [kk('inr')])
                P.op('pool', lambda e, w=w, q=q: e.tensor_tensor(out=w.a3[:], in0=q.ct[:], in1=w.bui[:], op=ALU.mult), reads=[q.kct, kk('bui')], writes=[kk('a3')])
                P.op('dve', lambda e, w=w, q=q: e.tensor_tensor(out=w.a4[:], in0=q.st[:], in1=w.bur[:], op=ALU.mult), reads=[q.kst, kk('bur')], writes=[kk('a4')])
                dve(lambda e, w=w: e.tensor_tensor(out=w.ini[:], in0=w.a3[:], in1=w.a4[:], op=ALU.subtract), [kk('a3'), kk('a4')], [kk('ini')])
                dve(lambda e, w=w, q=q: e.tensor_tensor_scan(out=w.wr[:], data0=q.rt[:], data1=w.inr[:], initial=q.wrc[:, 0:1], op0=ALU.mult, op1=ALU.add),
                    [q.krt, kk('inr'), q.kwrc], [kk('wr')])
                dve(lambda e, w=w, q=q: e.tensor_tensor_scan(out=w.wi[:], data0=q.rt[:], data1=w.ini[:], initial=q.wic[:, 0:1], op0=ALU.mult, op1=ALU.add),
                    [q.krt, kk('ini'), q.kwic], [kk('wi')])
                dve(lambda e, w=w, q=q: e.tensor_tensor(out=q.tc_[:], in0=w.wr[:, T - 1:T], in1=q.cT[:], op=ALU.mult), [kk('wr'), q.kcT], [q.ktc])
                dve(lambda e, w=w, q=q: e.scalar_tensor_tensor(out=q.wrc[:], in0=w.wi[:, T - 1:T], scalar=q.nsT[:, 0:1], in1=q.tc_[:], op0=ALU.mult, op1=ALU.add),
                    [kk('wi'), q.knsT, q.ktc], [q.kwrc])
                dve(lambda e, w=w, q=q: e.tensor_tensor(out=q.tc_[:], in0=w.wi[:, T - 1:T], in1=q.cT[:], op=ALU.mult), [kk('wi'), q.kcT], [q.ktc])
                dve(lambda e, w=w, q=q: e.scalar_tensor_tensor(out=q.wic[:], in0=w.wr[:, T - 1:T], scalar=q.sT[:, 0:1], in1=q.tc_[:], op0=ALU.mult, op1=ALU.add),
                    [kk('wr'), q.ksT, q.ktc], [q.kwic])
                P.op('pool', lambda e, w=w, q=q: e.tensor_tensor(out=w.a1[:], in0=q.ct[:], in1=w.wr[:], op=ALU.mult), reads=[q.kct, kk('wr')], writes=[kk('a1')])
                P.op('dve', lambda e, w=w, q=q: e.tensor_tensor(out=w.a2[:], in0=q.st[:], in1=w.wi[:], op=ALU.mult), reads=[q.kst, kk('wi')], writes=[kk('a2')])
                dve(lambda e, w=w: e.tensor_tensor(out=w.sr[:], in0=w.a1[:], in1=w.a2[:], op=ALU.subtract), [kk('a1'), kk('a2')], [kk('sr')])
                P.op('dve', lambda e, w=w, q=q: e.tensor_tensor(out=w.a3[:], in0=q.st[:], in1=w.wr[:], op=ALU.mult), reads=[q.kst, kk('wr')], writes=[kk('a3')])
                P.op('dve', lambda e, w=w, q=q: e.tensor_tensor(out=w.a4[:], in0=q.ct[:], in1=w.wi[:], op=ALU.mult), reads=[q.kct, kk('wi')], writes=[kk('a4')])
                dve(lambda e, w=w: e.tensor_tensor(out=w.si[:], in0=w.a3[:], in1=w.a4[:], op=ALU.add), [kk('a3'), kk('a4')], [kk('si')])
                P.op('pe', lambda e, w=w, q=q: e.matmul(psy[0:32, :], lhsT=q.LC[0][0][:], rhs=w.sr[:], start=True, stop=False), reads=[q.LC[0][1], kk('sr')], writes=['psy'])
                P.op('pe', lambda e, w=w, q=q: e.matmul(psy[0:32, :], lhsT=q.LC[1][0][:], rhs=w.si[:], start=False, stop=False), reads=[q.LC[1][1], kk('si')], writes=['psy'])
                P.op('pe', lambda e, q=q, K=K, kt=kt: e.matmul(psy[0:32, :], lhsT=q.LD[0:K, :], rhs=uT[kt][0:K, :], start=False, stop=True), reads=[q.kLD, f'uT{kt}'], writes=['psy'])
                P.op('act', lambda e, w=w: e.activation(out=w.ys[:], in_=psy[0:32, :], func=AF.Copy), reads=['psy'], writes=[kk('ys')])
                P.dma(yT[32 * pr:32 * pr + 32, t0:t0 + T], w.ys[:], reads=[kk('ys')], final=True)
        P.emit()
        print("s5 ops", {e: len(P.ops[e]) for e in P.ENGS}, "waits", P.nwaits)
    return nc


LN1E4_32 = math.log(10000.0) / 32.0


def make_sincos(P, new):
    sc_tmp = {}

    def sincos(ang, kang, n):
        s_, ks = new([128, n])
        c_, kc = new([128, n])
        for (o, ko, sh) in ((s_, ks, 0.0), (c_, kc, 0.5 * PI)):
            if n not in sc_tmp:
                sc_tmp[n] = (new([128, n]), new([128, n], I32), new([128, n]))
            (a_, ka), (ki, kki), (kf, kkf) = sc_tmp[n]
            d = lambda fn, r, w: P.op('dve', fn, reads=r, writes=w)
            d(lambda e, a_=a_, sh=sh: e.tensor_scalar(out=a_[:], in0=ang[:], scalar1=sh, scalar2=None, op0=ALU.add), [kang], [ka])
            d(lambda e, a_=a_, kf=kf: e.tensor_scalar(out=kf[:], in0=a_[:], scalar1=1.0 / (2 * PI), scalar2=None, op0=ALU.mult), [ka], [kkf])
            d(lambda e, ki=ki, kf=kf: e.tensor_copy(out=ki[:], in_=kf[:]), [kkf], [kki])
            d(lambda e, ki=ki, kf=kf: e.tensor_copy(out=kf[:], in_=ki[:]), [kki], [kkf])
            d(lambda e, o=o, kf=kf, a_=a_: e.scalar_tensor_tensor(out=o[:], in0=kf[:], scalar=-2 * PI, in1=a_[:], op0=ALU.mult, op1=ALU.add), [kkf, ka], [ko])
            d(lambda e, o=o, kf=kf: e.tensor_scalar(out=kf[:], in0=o[:], scalar1=PI, scalar2=2 * PI, op0=ALU.is_gt, op1=ALU.mult), [ko], [kkf])
            d(lambda e, o=o, kf=kf: e.tensor_tensor(out=o[:], in0=o[:], in1=kf[:], op=ALU.subtract), [ko, kkf], [ko])
            d(lambda e, o=o, kf=kf: e.tensor_scalar(out=kf[:], in0=o[:], scalar1=-PI, scalar2=2 * PI, op0=ALU.is_lt, op1=ALU.mult), [ko], [kkf])
            d(lambda e, o=o, kf=kf: e.tensor_tensor(out=o[:], in0=o[:], in1=kf[:], op=ALU.add), [ko, kkf], [ko])
            P.op('act', lambda e, o=o: e.activation(out=o[:], in_=o[:], func=AF.Sin), reads=[ko], writes=[ko])
        return s_, ks, c_, kc
    return sincos


def build_qkv(NT):
    nc = bass.Bass("TRN2", target_bir_lowering=False)
    x = nc.dram_tensor("x", [NT, D], F32, kind="ExternalInput").ap()
    ln = nc.dram_tensor("ln", [D], F32, kind="ExternalInput").ap()
    kvn = nc.dram_tensor("kvn", [D], F32, kind="ExternalInput").ap()
    wd = {nm: nc.dram_tensor(nm, [D, SW], F32, kind="ExternalInput").ap() for nm in ("wqa", "wqb", "wka", "wkb", "wv")}
    pos = nc.dram_tensor("pos", [NT], F32, kind="ExternalInput").ap()
    QT = nc.dram_tensor("QT", [SW, NT], F32, kind="ExternalOutput").ap()
    KT = nc.dram_tensor("KT", [SW, NT], F32, kind="ExternalOutput").ap()
    V = nc.dram_tensor("V", [NT, SW], F32, kind="ExternalOutput").ap()
    KM = nc.dram_tensor("KM", [SW, NT // 256], F32, kind="ExternalOutput").ap()
    T = 512
    with contextlib.ExitStack() as st:
        P = Prog(nc, st)
        c = setup_common(P)
        tmp = make_norm_tmp(P)
        gq = load_bcast(P, ln, D, "gq")
        gk = load_bcast(P, kvn, D, "gk")
        wb = {nm: load_w_bf16(P, wd[nm].rearrange("(k p) n -> p k n", p=128), [128, 8, SW], nm) for nm in wd}
        uid = [0]

        def new(shape, dt=F32):
            uid[0] += 1
            nm = f"t{uid[0]}"
            return P.sb(shape, dt, nm), nm
        sincos = make_sincos(P, new)
        dve = lambda fn, r, w: P.op('dve', fn, reads=r, writes=w)
        pi_, kpi = new([128, 1], I32)
        pj, kpj = new([128, 1], I32)
        pf, kpf = new([128, 1])
        invf, kinv = new([128, 1])
        sgn, ksgn = new([128, 1])
        P.op('pool', lambda e: e.iota(pi_[:], pattern=[[0, 1]], base=0, channel_multiplier=1), writes=[kpi])
        dve(lambda e: e.tensor_single_scalar(out=pj[:], in_=pi_[:], scalar=31, op=ALU.bitwise_and), [kpi], [kpj])
        dve(lambda e: e.tensor_copy(out=pf[:], in_=pj[:]), [kpj], [kpf])
        P.op('act', lambda e: e.activation(out=invf[:], in_=pf[:], func=AF.Exp, scale=-LN1E4_32), reads=[kpf], writes=[kinv])
        dve(lambda e: e.tensor_single_scalar(out=pj[:], in_=pi_[:], scalar=32, op=ALU.bitwise_and), [kpi, kpf], [kpj])
        dve(lambda e: e.tensor_copy(out=sgn[:], in_=pj[:]), [kpj], [ksgn])
        dve(lambda e: e.tensor_scalar(out=sgn[:], in0=sgn[:], scalar1=1.0 / 16.0, scalar2=-1.0, op0=ALU.mult, op1=ALU.add), [ksgn], [ksgn])
        posb, kposb = new([128, T])
        ang, kang = new([128, T])
        xtt = [P.sb([128, D], F32) for _ in range(2)]
        hq = P.sb([128, 8, T], BF16, "hq")
        hk = P.sb([128, 8, T], BF16, "hk")
        psA = [P.ps([128, T], F32) for _ in range(2)]
        psB = [P.ps([128, T], F32) for _ in range(2)]
        psV = [P.ps([128, T], F32) for _ in range(2)]
        ta = [P.sb([128, T], F32) for _ in range(2)]
        tb = [P.sb([128, T], F32) for _ in range(2)]
        kms = [P.sb([128, 2], F32) for _ in range(2)]
        vo = [P.sb([128, SW], F32) for _ in range(2)]
        cnt = dict(x=0, a=0, v=0)
        for ti in range(NT // T):
            t0 = ti * T
            P.dma(posb[:], pos[t0:t0 + T].partition_broadcast(128), writes=[kposb])
            dve(lambda e: e.tensor_scalar(out=ang[:], in0=posb[:], scalar1=invf[:, 0:1], scalar2=None, op0=ALU.mult), [kposb, kinv], [kang])
            sn, ksn, cs, kcs = sincos(ang, kang, T)
            dve(lambda e, sn=sn: e.tensor_scalar(out=sn[:], in0=sn[:], scalar1=sgn[:, 0:1], scalar2=None, op0=ALU.mult), [ksn, ksgn], [ksn])
            for s in range(4):
                bx = cnt['x'] % 2
                cnt['x'] += 1
                P.dma(xtt[bx][:], x[t0 + s * 128:t0 + (s + 1) * 128, :], writes=[f'xtt{bx}'])
                norm_T(P, c, xtt[bx][:], f'xtt{bx}', 128, gq, 'gq', hq, ('hq', s), s * 128, tmp)
                norm_T(P, c, xtt[bx][:], f'xtt{bx}', 128, gk, 'gk', hk, ('hk', s), s * 128, tmp)
            for (hh, hn, wa, wb_, out, scale, is_k) in ((hq, 'hq', 'wqa', 'wqb', QT, 0.125, False), (hk, 'hk', 'wka', 'wkb', KT, 1.0, True)):
                hkeys = [(hn, s) for s in range(4)]
                for ch in range(6):
                    b = cnt['a'] % 2
                    cnt['a'] += 1
                    for (ps, w_, pn) in ((psA[b], wa, 'psA'), (psB[b], wb_, 'psB')):
                        for k in range(8):
                            P.op('pe', lambda e, ps=ps, w_=w_, k=k, ch=ch, hh=hh: e.matmul(ps[:], lhsT=wb[w_][:, k, ch * 128:(ch + 1) * 128], rhs=hh[:, k, :], start=(k == 0), stop=(k == 7)),
                                 reads=[w_] + hkeys, writes=[f'{pn}{b}'])
                    dve(lambda e, b=b, cs=cs: e.tensor_tensor(out=ta[b][:], in0=psA[b][:], in1=cs[:], op=ALU.mult), [f'psA{b}', kcs], [f'ta{b}'])
                    dve(lambda e, b=b, sn=sn: e.tensor_tensor(out=tb[b][:], in0=psB[b][:], in1=sn[:], op=ALU.mult), [f'psB{b}', ksn], [f'tb{b}'])
                    dve(lambda e, b=b: e.tensor_tensor(out=ta[b][:], in0=ta[b][:], in1=tb[b][:], op=ALU.add), [f'ta{b}', f'tb{b}'], [f'ta{b}'])
                    if scale != 1.0:
                        P.op('act', lambda e, b=b, scale=scale: e.activation(out=ta[b][:], in_=ta[b][:], func=AF.Copy, scale=scale), reads=[f'ta{b}'], writes=[f'ta{b}'])
                    P.dma(out[ch * 128:(ch + 1) * 128, t0:t0 + T], ta[b][:], reads=[f'ta{b}'], final=True)
                    if is_k:
                        dve(lambda e, b=b: e.tensor_reduce(out=kms[b][:], in_=ta[b][:].rearrange("p (n k) -> p n k", k=256), axis=AX.X, op=ALU.add), [f'ta{b}'], [f'kms{b}'])
                        P.dma(KM[ch * 128:(ch + 1) * 128, ti * 2:ti * 2 + 2], kms[b][:], reads=[f'kms{b}'], final=True)
            hkeys = [('hk', s) for s in range(4)]
            for s in range(4):
                vb = cnt['v'] % 2
                cnt['v'] += 1
                for (lo, hi, half) in ((0, 512, 0), (512, 768, 1)):
                    for k in range(8):
                        P.op('pe', lambda e, k=k, s=s, lo=lo, hi=hi, half=half: e.matmul(psV[half][:, 0:hi - lo], lhsT=hk[:, k, s * 128:(s + 1) * 128], rhs=wb['wv'][:, k, lo:hi], start=(k == 0), stop=(k == 7)),
                             reads=['wv'] + hkeys, writes=[f'psV{half}'])
                    P.op('act', lambda e, vb=vb, lo=lo, hi=hi, half=half: e.activation(out=vo[vb][:, lo:hi], in_=psV[half][:, 0:hi - lo], func=AF.Copy), reads=[f'psV{half}'], writes=[(f'vo{vb}', half)])
                P.dma(V[t0 + s * 128:t0 + (s + 1) * 128, :], vo[vb][:], reads=[(f'vo{vb}', 0), (f'vo{vb}', 1)], final=True)
        P.emit()
        print("qkv ops", {e: len(P.ops[e]) for e in P.ENGS}, "waits", P.nwaits)
    return nc


def build_moba(NB, NH=6):
    nc = bass.Bass("TRN2", target_bir_lowering=False)
    NK = NB * 256
    QT = nc.dram_tensor("QT", [NH * 64, NB * 128], F32, kind="ExternalInput").ap()
    KT = nc.dram_tensor("KT", [NH * 64, NK], F32, kind="ExternalInput").ap()
    V = nc.dram_tensor("V", [NK, NH * 64], F32, kind="ExternalInput").ap()
    KM = nc.dram_tensor("KM", [NH * 64, NB], F32, kind="ExternalInput").ap()
    cmaskT = nc.dram_tensor("cmaskT", [256, 128], F32, kind="ExternalInput").ap()
    O = nc.dram_tensor("O", [NB * 128, NH * 64], F32, kind="ExternalOutput").ap()
    NBP = max(NB, 8)
    NS = 3
    with contextlib.ExitStack() as st:
        P = Prog(nc, st)
        c = setup_common(P)
        dve = lambda fn, r, w: P.op('dve', fn, reads=r, writes=w)
        stg = [P.sb([128, 2048], F32) for _ in range(2)]
        cmT = P.sb([128, 2, 128], BF16, "cmT")
        P.dma(stg[0][:, 0:256].rearrange("p (j q) -> p j q", j=2), cmaskT.rearrange("(j p) q -> p j q", p=128), writes=['stg0'])
        P.op('pool', lambda e: e.tensor_copy(out=cmT[:], in_=stg[0][:, 0:256].rearrange("p (j q) -> p j q", j=2)), reads=['stg0'], writes=['cmT'])
        KE = P.sb([128, NK], BF16, "KE")
        E = KE[64:128, :]
        kb = KE[0:64, :]
        P.op('pool', lambda e: e.memset(E, 1.0), writes=['E'])
        P.op('pool', lambda e: e.affine_select(out=E, in_=E, pattern=[[1, NK]], compare_op=ALU.is_ge, fill=0.0, base=0, channel_multiplier=-256), reads=['E'], writes=['E'])
        P.op('pool', lambda e: e.affine_select(out=E, in_=E, pattern=[[-1, NK]], compare_op=ALU.is_ge, fill=0.0, base=255, channel_multiplier=256), reads=['E'], writes=['E'])
        QB = P.sb([128, NB * 128], BF16, "QB")
        qb = QB[0:64, :]
        biasT = QB[64:128, :].rearrange("p (n q) -> p n q", q=128)
        vb = P.sb([128, NB * 2, 65], BF16, "vb")
        P.op('pool', lambda e: e.memset(vb[:], 1.0), writes=['vb'])
        kmb = P.sb([64, NB], BF16, "kmb")
        psS = [P.ps([128, 512], F32) for _ in range(NS)]
        psO = [P.ps([128, 65], F32) for _ in range(2)]
        psG = P.ps([128, NBP], F32)
        psBT = P.ps([128, 128], BF16)
        gt = [P.sb([128, NBP], F32) for _ in range(2)]
        bq = [P.sb([128, 128], BF16) for _ in range(2)]
        mx8 = [P.sb([128, 8], F32) for _ in range(2)]
        lt = [P.sb([128, 1], F32) for _ in range(2)]
        Pb = [P.sb([128, 512], BF16) for _ in range(NS)]
        Ot = [P.sb([128, NH * 64], F32) for _ in range(2)]
        cnt = dict(s=1)

        def cast_in(dst_fn, src_fn, rows, total, wkey):
            CH = 2048
            for o in range(0, total, CH):
                n = min(CH, total - o)
                b = cnt['s'] % 2
                cnt['s'] += 1
                P.dma(stg[b][0:rows, 0:n], src_fn(o, n), writes=[f'stg{b}'])
                P.op('pool', lambda e, b=b, o=o, n=n: e.tensor_copy(out=dst_fn(o, n), in_=stg[b][0:rows, 0:n]), reads=[f'stg{b}'], writes=[wkey])

        for h in range(NH):
            r0 = h * 64
            cast_in(lambda o, n: qb[:, o:o + n], lambda o, n: QT[r0:r0 + 64, o:o + n], 64, NB * 128, 'qb')
            cast_in(lambda o, n: kb[:, o:o + n], lambda o, n: KT[r0:r0 + 64, o:o + n], 64, NK, 'kb')
            cast_in(lambda o, n: kmb[:, o:o + n], lambda o, n: KM[r0:r0 + 64, o:o + n], 64, NB, 'kmb')
            vsrc = V[:, r0:r0 + 64].rearrange("(t p) d -> p t d", p=128)
            TCH = 32
            for o in range(0, NB * 2, TCH):
                n = min(TCH, NB * 2 - o)
                b = cnt['s'] % 2
                cnt['s'] += 1
                sv = stg[b][:, 0:n * 64].rearrange("p (t d) -> p t d", d=64)
                P.dma(sv, vsrc[:, o:o + n, :], writes=[f'stg{b}'])
                P.op('pool', lambda e, sv=sv, o=o, n=n: e.tensor_copy(out=vb[:, o:o + n, 0:64], in_=sv), reads=[f'stg{b}'], writes=['vb'])
            P.op('pool', lambda e: e.memset(QB[64:128, :], 0.0), writes=['biasT'])
            for n in range(4, NB):
                gp = n % 2
                qt = qb[:, n * 128:(n + 1) * 128]
                P.op('pool', lambda e, gp=gp: e.memset(gt[gp][:], -1e30), writes=[f'gt{gp}'])
                P.op('pool', lambda e, gp=gp: e.memset(bq[gp][:], 0.0), writes=[f'bq{gp}'])
                P.op('pe', lambda e, qt=qt, n=n: e.matmul(psG[:, 0:n], lhsT=qt, rhs=kmb[:, 0:n], start=True, stop=True), reads=['qb', 'kmb'], writes=['psG'])
                P.op('act', lambda e, n=n, gp=gp: e.activation(out=gt[gp][:, 0:n], in_=psG[:, 0:n], func=AF.Copy), reads=['psG', f'gt{gp}'], writes=[f'gt{gp}'])
                dve(lambda e, gp=gp: e.max(out=mx8[gp][:], in_=gt[gp][:]), [f'gt{gp}'], [f'mx8{gp}'])
                dve(lambda e, n=n, gp=gp: e.tensor_scalar(out=gt[gp][:, 0:n], in0=gt[gp][:, 0:n], scalar1=mx8[gp][:, 2:3], scalar2=None, op0=ALU.is_ge),
                    [f'gt{gp}', f'mx8{gp}'], [f'gt{gp}'])
                dve(lambda e, n=n, gp=gp: e.tensor_scalar(out=bq[gp][:, 64:64 + n], in0=gt[gp][:, 0:n], scalar1=30000.0, scalar2=-30000.0, op0=ALU.mult, op1=ALU.add),
                    [f'gt{gp}', f'bq{gp}'], [f'bq{gp}'])
                P.op('pe', lambda e, gp=gp: e.transpose(out=psBT[:, :], in_=bq[gp][:, :], identity=c.ident[:]), reads=[f'bq{gp}', 'ident'], writes=['psBT'])
                P.op('act', lambda e, n=n: e.activation(out=biasT[:, n, :], in_=psBT[64:128, :], func=AF.Copy), reads=['psBT', 'biasT'], writes=[('biasT', n)])
            items = []
            for n in range(NB):
                groups = [(g * 2, min(2, n - g * 2)) for g in range((n + 1) // 2)]
                nt_total = sum(nb_ * 2 for (_, nb_) in groups) + 2
                base = 0
                for (b0, nb_) in groups:
                    items.append(dict(n=n, own=False, b0=b0, nb=nb_, t0=base, nt=nt_total))
                    base += nb_ * 2
                items.append(dict(n=n, own=True, b0=n, nb=1, t0=base, nt=nt_total))

            def stageA(i, it):
                sb_ = i % NS
                n = it['n']
                qt = qb[:, n * 128:(n + 1) * 128]
                ntk = it['nb'] * 2
                kt0 = it['b0'] * 2
                for j in range(ntk):
                    kt = kt0 + j
                    if it['own']:
                        P.op('pe', lambda e, qt=qt, kt=kt, j=j, sb_=sb_: e.matmul(psS[sb_][:, j * 128:(j + 1) * 128], lhsT=kb[:, kt * 128:(kt + 1) * 128], rhs=qt, start=True, stop=False),
                             reads=['qb', 'kb'], writes=[(f'psS{sb_}', j)])
                        P.op('pe', lambda e, j=j, sb_=sb_: e.matmul(psS[sb_][:, j * 128:(j + 1) * 128], lhsT=c.ident[:], rhs=cmT[:, j, :], start=False, stop=True),
                             reads=['ident', 'cmT'], writes=[(f'psS{sb_}', j)])
                    else:
                        P.op('pe', lambda e, kt=kt, j=j, sb_=sb_, n=n: e.matmul(psS[sb_][:, j * 128:(j + 1) * 128], lhsT=KE[:, kt * 128:(kt + 1) * 128], rhs=QB[:, n * 128:(n + 1) * 128], start=True, stop=True),
                             reads=['qb', 'kb', 'E', ('biasT', n), 'biasT'], writes=[(f'psS{sb_}', j)])
                P.op('act', lambda e, sb_=sb_, ntk=ntk: e.activation(out=Pb[sb_][:, 0:ntk * 128], in_=psS[sb_][:, 0:ntk * 128], func=AF.Exp),
                     reads=[(f'psS{sb_}', j) for j in range(ntk)], writes=[f'Pb{sb_}'])

            def stageC(i, it):
                sb_ = i % NS
                n = it['n']
                ntk = it['nb'] * 2
                kt0 = it['b0'] * 2
                ob2 = n % 2
                for j in range(ntk):
                    idx = it['t0'] + j
                    P.op('pe', lambda e, sb_=sb_, j=j, kt=kt0 + j, idx=idx, nt=it['nt'], ob2=ob2: e.matmul(psO[ob2][:], lhsT=Pb[sb_][:, j * 128:(j + 1) * 128], rhs=vb[:, kt, :], start=(idx == 0), stop=(idx == nt - 1)),
                         reads=[f'Pb{sb_}', 'vb'], writes=[f'psO{ob2}'])
                if it['own']:
                    dve(lambda e, ob2=ob2: e.reciprocal(out=lt[ob2][:], in_=psO[ob2][:, 64:65]), [f'psO{ob2}'], [f'lt{ob2}'])
                    P.op('act', lambda e, ob2=ob2, h=h: e.activation(out=Ot[ob2][:, h * 64:(h + 1) * 64], in_=psO[ob2][:, 0:64], func=AF.Copy, scale=lt[ob2][:, 0:1]),
                         reads=[f'psO{ob2}', f'lt{ob2}'], writes=[(f'Ot{ob2}', h)])
                    P.dma(O[n * 128:(n + 1) * 128, h * 64:(h + 1) * 64], Ot[ob2][:, h * 64:(h + 1) * 64], reads=[(f'Ot{ob2}', h)], final=True)

            for i in range(len(items) + 1):
                if i < len(items):
                    stageA(i, items[i])
                if i >= 1:
                    stageC(i - 1, items[i - 1])
        P.emit()
        print("moba ops", {e: len(P.ops[e]) for e in P.ENGS}, "waits", P.nwaits)
    return nc


def _run(nc, maps):
    return run_bass_kernel_spmd(nc, maps, core_ids=list(range(8))).results


def _c(a):
    return np.ascontiguousarray(a, dtype=np.float32)


def _pack_wup(w):
    w3 = w.reshape(8, 128, 5632)
    g = w3[:, :, :2816].reshape(8, 128, 22, 128)
    v = w3[:, :, 2816:].reshape(8, 128, 22, 128)
    a = np.concatenate([g, v], -1)
    return _c(a.transpose(2, 1, 0, 3).reshape(22, 128, 8 * 256))


def _perm(w):
    w = w.reshape(1024, 12, 2, 32)[:, :, ::-1, :]
    return _c(w.reshape(1024, 768))


def kernel(**inp):
    inp = {k: np.asarray(v, dtype=np.float32) for k, v in inp.items()}
    x = inp['x']
    B, S, NT = 2, 16384, 4096
    maps = []
    for i in range(8):
        b, gs = i // 4, i % 4
        G0 = 12 * gs
        maps.append(dict(xb=_c(x[b]), ln=_c(inp['ln_mix'][0]), win=_c(inp['w_in'][0][:, 192 * gs:192 * gs + 192]),
                         lre=_c(inp['s5_lambda_re'][0, G0:G0 + 12]), lim=_c(inp['s5_lambda_im'][0, G0:G0 + 12]), lst=_c(inp['s5_log_step'][0, G0:G0 + 12]),
                         bre=_c(inp['s5_b_re'][0, G0:G0 + 12]), bim=_c(inp['s5_b_im'][0, G0:G0 + 12]), cre=_c(inp['s5_c_re'][0, G0:G0 + 12]),
                         cim=_c(inp['s5_c_im'][0, G0:G0 + 12]), dd=_c(inp['s5_d'][0, G0:G0 + 12])))
    res = _run(build_s5(S), maps)
    yT = [np.concatenate([res[b * 4 + gs]['yT'] for gs in range(4)], 0) for b in range(B)]
    nc_tail_glu = build_tail(NT, True)
    maps = []
    for i in range(8):
        b, r = i // 4, i % 4
        sl = slice(r * NT, (r + 1) * NT)
        maps.append(dict(x=_c(x[b, sl]), sT=_c(yT[b][:, sl]), mem=_c(inp['mem'][b]), ln=_c(inp['ln_mix'][0]), wq=_c(inp['w_in'][0][:, 768:]),
                         mng=_c(inp['mem_norm'][0]), wmkv=_c(inp['w_mem_kv'][0]), wglu=_c(inp['s5_w_glu'][0]), wout=_c(inp['w_out'][0])))
    res = _run(nc_tail_glu, maps)
    xmid = np.stack([np.concatenate([res[b * 4 + r]['y'] for r in range(4)], 0) for b in range(B)])

    def ffn(xm, l, final):
        maps = []
        for i in range(8):
            b, r = i // 4, i % 4
            sl = slice(r * NT, (r + 1) * NT)
            xh = np.zeros((2, 1024), np.float32) if r == 0 else xm[b, r * NT - 2:r * NT]
            maps.append(dict(xm=_c(xm[b, sl]), xh=_c(xh), ln=_c(inp['ln_ffn'][l]), wup=_pack_wup(inp['w_up'][l]), cw=_c(inp['conv_w'][l]), cb=_c(inp['conv_b'][l]),
                             wdown=_c(inp['w_down'][l]), fng=_c(inp['final_norm'])))
        res = _run(build_ffn(NT, final), maps)
        return np.stack([np.concatenate([res[b * 4 + r]['y'] for r in range(4)], 0) for b in range(B)])

    x1 = ffn(xmid, 0, False)
    wq = _c(inp['w_in'][1][:, :768]); wk = _c(inp['w_kv'][:, :768]); wv = _c(inp['w_kv'][:, 768:])
    wqb, wkb = _perm(wq), _perm(wk)
    maps = []
    for i in range(8):
        b, r = i // 4, i % 4
        sl = slice(r * NT, (r + 1) * NT)
        maps.append(dict(x=_c(x1[b, sl]), ln=_c(inp['ln_mix'][1]), kvn=_c(inp['kv_norm']), wqa=wq, wqb=wqb, wka=wk, wkb=wkb, wv=wv,
                         pos=np.arange(r * NT, (r + 1) * NT, dtype=np.float32)))
    res = _run(build_qkv(NT), maps)
    QT = [np.concatenate([res[b * 4 + r]['QT'] for r in range(4)], 1) for b in range(B)]
    KT = [np.concatenate([res[b * 4 + r]['KT'] for r in range(4)], 1) for b in range(B)]
    V = [np.concatenate([res[b * 4 + r]['V'] for r in range(4)], 0) for b in range(B)]
    KM = [np.concatenate([res[b * 4 + r]['KM'] for r in range(4)], 1) for b in range(B)]
    NB = S // 256
    maps = []
    colsj = [np.concatenate([np.arange(n * 256 + j * 128, n * 256 + j * 128 + 128) for n in range(NB)]) for j in range(2)]
    for i in range(8):
        b, j, hh = i // 4, (i % 4) % 2, (i % 4) // 2
        rows = slice(hh * 384, hh * 384 + 384)
        cm = np.where(np.arange(256)[None, :] <= (128 * j + np.arange(128))[:, None], 0.0, -30000.0).astype(np.float32)
        maps.append(dict(QT=_c(QT[b][rows][:, colsj[j]]), KT=_c(KT[b][rows]), V=_c(V[b][:, rows]), KM=_c(KM[b][rows]), cmaskT=_c(cm.T)))
    res = _run(build_moba(NB, 6), maps)
    attn = np.zeros((B, S, 768), np.float32)
    for i in range(8):
        b, j, hh = i // 4, (i % 4) % 2, (i % 4) // 2
        attn[b][colsj[j], hh * 384:hh * 384 + 384] = res[i]['O']
    maps = []
    for i in range(8):
        b, r = i // 4, i % 4
        sl = slice(r * NT, (r + 1) * NT)
        maps.append(dict(x=_c(x1[b, sl]), sT=_c(attn[b, sl].T), mem=_c(inp['mem'][b]), ln=_c(inp['ln_mix'][1]), wq=_c(inp['w_in'][1][:, 768:]),
                         mng=_c(inp['mem_norm'][1]), wmkv=_c(inp['w_mem_kv'][1]), wglu=_c(inp['s5_w_glu'][0]), wout=_c(inp['w_out'][1])))
    res = _run(build_tail(NT, False), maps)
    xmid1 = np.stack([np.concatenate([res[b * 4 + r]['y'] for r in range(4)], 0) for b in range(B)])
    out = ffn(xmid1, 1, True)
    return out.astype(np.float32)
```
